# Optimizing a Trainium2 kernel written in Bass

```python
import math
import jax, jax.numpy as jnp
from jax import lax
import numpy as np

D_MODEL = 1024
BATCH = 4
SEQ = 8192
DEPTH = 1

CHUNK = 64
EPS = 1e-6
PLE_DIM = 256
D_S5 = D_MODEL
S5_GROUP_CH = 16
S5_GROUPS = D_S5 // S5_GROUP_CH
S5_STATE = 64
D_SSD = D_MODEL
SSD_HEAD_DIM = 64
SSD_HEADS = D_SSD // SSD_HEAD_DIM
SSD_GROUPS = 4
SSD_HEADS_PER_GROUP = SSD_HEADS // SSD_GROUPS
SSD_STATE = 128
CONV_WIDTH = 4
SSD_CONV_DIM = D_SSD + 2 * SSD_GROUPS * SSD_STATE
NORM_GROUP = D_SSD // SSD_GROUPS
MIX_WIDTH = D_S5 + D_SSD
D_IN_PROJ = 2 * D_S5 + D_SSD + SSD_CONV_DIM + SSD_HEADS
SPLITS = (D_S5, 2 * D_S5, 2 * D_S5 + D_SSD, 2 * D_S5 + D_SSD + SSD_CONV_DIM)

kernel_name = "hybrid_s5_ssd_parallel_heads"


def rmsnorm(x, w):
    xf = x.astype(jnp.float32)
    y = xf * lax.rsqrt(jnp.mean(xf * xf, axis=-1, keepdims=True) + EPS)
    return (y * w.astype(jnp.float32)).astype(x.dtype)


def gated_group_rmsnorm(y, z, w):
    g = y.astype(jnp.float32) * jax.nn.silu(z.astype(jnp.float32))
    b, l, d = g.shape
    g = g.reshape(b, l, SSD_GROUPS, NORM_GROUP)
    g = g * lax.rsqrt(jnp.mean(g * g, axis=-1, keepdims=True) + EPS)
    return (g.reshape(b, l, d) * w.astype(jnp.float32)).astype(y.dtype)


def causal_depthwise_conv(x, w, bias):
    k_w = w.shape[0]
    L = x.shape[1]
    xp = jnp.pad(x, ((0, 0), (k_w - 1, 0), (0, 0)))
    return sum(xp[:, k:k + L] * w[k] for k in range(k_w)) + bias


def segsum(a):
    T = a.shape[-1]
    cs = jnp.cumsum(a, axis=-1)
    diff = cs[..., :, None] - cs[..., None, :]
    mask = jnp.tril(jnp.ones((T, T), dtype=bool))
    return jnp.where(mask, diff, -jnp.inf)


def s5_branch(u, A_re, A_im, log_dt, B_re, B_im, C_re, C_im, D, w_glu, b_glu):
    f32 = jnp.float32
    b, L, _ = u.shape
    n_chunks = L // CHUNK
    uf = u.astype(f32)
    u_c = uf.reshape(b, n_chunks, CHUNK, S5_GROUPS, S5_GROUP_CH).transpose(1, 0, 2, 3, 4)
    A = lax.complex(A_re.astype(f32), A_im.astype(f32))
    dt = jnp.exp(log_dt.astype(f32))[:, None]
    A_bar = jnp.exp(A * dt)
    Bc = lax.complex(B_re.astype(f32), B_im.astype(f32))
    B_bar = ((A_bar - 1.0) / A)[..., None] * Bc
    Cc = lax.complex(C_re.astype(f32), C_im.astype(f32))
    steps = jnp.arange(1, CHUNK + 1, dtype=f32)[:, None, None]
    pows = jnp.exp(A[None] * dt[None] * steps)

    def combine(e1, e2):
        a1, b1 = e1
        a2, b2 = e2
        return a1 * a2, a2 * b1 + b2

    def step(h0, uc):
        bu = jnp.einsum('btgh,gnh->btgn', uc, B_bar)
        a = jnp.broadcast_to(A_bar, bu.shape)
        _, local = lax.associative_scan(combine, (a, bu), axis=1)
        states = local + pows * h0[:, None]
        y = jnp.einsum('btgn,ghn->btgh', states, Cc).real
        return states[:, -1], y

    h0 = jnp.zeros((b, S5_GROUPS, S5_STATE), jnp.complex64)
    _, ys = lax.scan(step, h0, u_c)
    y = ys.transpose(1, 0, 2, 3, 4).reshape(b, L, D_S5) + D.astype(f32) * uf
    y = jax.nn.gelu(y)
    y = y * jax.nn.sigmoid(y @ w_glu.astype(f32) + b_glu.astype(f32))
    return y.astype(u.dtype)


def ssd_chunked(xs, dt, A, Bm, Cm):
    b, L, H, P = xs.shape
    c = L // CHUNK
    Bh = jnp.repeat(Bm, SSD_HEADS_PER_GROUP, axis=2).reshape(b, c, CHUNK, H, SSD_STATE)
    Ch = jnp.repeat(Cm, SSD_HEADS_PER_GROUP, axis=2).reshape(b, c, CHUNK, H, SSD_STATE)
    xd = (xs * dt[..., None]).reshape(b, c, CHUNK, H, P)
    a = (dt * A).reshape(b, c, CHUNK, H).transpose(0, 3, 1, 2)
    a_cs = jnp.cumsum(a, axis=-1)
    L_intra = jnp.exp(segsum(a))
    scores = jnp.einsum('bclhn,bcshn->bhcls', Ch, Bh) * L_intra
    y_diag = jnp.einsum('bhcls,bcshp->bclhp', scores, xd)
    decay_states = jnp.exp(a_cs[..., -1:] - a_cs)
    states = jnp.einsum('bclhn,bhcl,bclhp->bchpn', Bh, decay_states, xd)
    states = jnp.concatenate([jnp.zeros_like(states[:, :1]), states], axis=1)
    chunk_a = jnp.pad(a_cs[..., -1], ((0, 0), (0, 0), (1, 0)))
    decay_chunk = jnp.exp(segsum(chunk_a))
    new_states = jnp.einsum('bhzc,bchpn->bzhpn', decay_chunk, states)
    prev_states = new_states[:, :-1]
    y_off = jnp.einsum('bclhn,bchpn,bhcl->bclhp', Ch, prev_states, jnp.exp(a_cs))
    return (y_diag + y_off).reshape(b, L, H, P)


def ssd_branch(xBC, dt_raw, z, conv_w, conv_b, dt_bias, A_log, Dh, norm_w):
    f32 = jnp.float32
    b, L, _ = xBC.shape
    xBC = jax.nn.silu(causal_depthwise_conv(xBC, conv_w, conv_b))
    xs = xBC[..., :D_SSD].astype(f32).reshape(b, L, SSD_HEADS, SSD_HEAD_DIM)
    Bm = xBC[..., D_SSD:D_SSD + SSD_GROUPS * SSD_STATE].astype(f32).reshape(b, L, SSD_GROUPS, SSD_STATE)
    Cm = xBC[..., D_SSD + SSD_GROUPS * SSD_STATE:].astype(f32).reshape(b, L, SSD_GROUPS, SSD_STATE)
    dt = jax.nn.softplus(dt_raw.astype(f32) + dt_bias.astype(f32))
    A = -jnp.exp(A_log.astype(f32))
    y = ssd_chunked(xs, dt, A, Bm, Cm) + Dh.astype(f32)[:, None] * xs
    y = y.reshape(b, L, D_SSD).astype(z.dtype)
    return gated_group_rmsnorm(y, z, norm_w)


def setup_inputs(seed: int = 0) -> dict:
    key = jax.random.key(seed)
    ks = jax.random.split(key, 26)
    f32 = jnp.float32
    nrm = lambda k, s, sc: jax.random.normal(k, s, f32) * sc
    x = jax.random.normal(ks[0], (BATCH, SEQ, D_MODEL), f32)
    p = jax.random.normal(ks[1], (DEPTH, BATCH, SEQ, PLE_DIM), f32)
    norm_w = 1.0 + nrm(ks[2], (DEPTH, D_MODEL), 0.02)
    w_in = nrm(ks[3], (DEPTH, D_MODEL, D_IN_PROJ), D_MODEL ** -0.5)
    n_idx = jnp.arange(S5_STATE, dtype=f32)
    s5_A_re = -0.5 + nrm(ks[4], (DEPTH, S5_GROUPS, S5_STATE), 0.01)
    s5_A_im = math.pi * n_idx + nrm(ks[5], (DEPTH, S5_GROUPS, S5_STATE), 0.01)
    s5_log_dt = jax.random.uniform(ks[6], (DEPTH, S5_GROUPS), f32, math.log(1e-3), math.log(1e-1))
    s5_B_re = nrm(ks[7], (DEPTH, S5_GROUPS, S5_STATE, S5_GROUP_CH), (2 * S5_GROUP_CH) ** -0.5)
    s5_B_im = nrm(ks[8], (DEPTH, S5_GROUPS, S5_STATE, S5_GROUP_CH), (2 * S5_GROUP_CH) ** -0.5)
    s5_C_re = nrm(ks[9], (DEPTH, S5_GROUPS, S5_GROUP_CH, S5_STATE), (2 * S5_STATE) ** -0.5)
    s5_C_im = nrm(ks[10], (DEPTH, S5_GROUPS, S5_GROUP_CH, S5_STATE), (2 * S5_STATE) ** -0.5)
    s5_D = nrm(ks[11], (DEPTH, D_S5), 1.0)
    s5_w_glu = nrm(ks[12], (DEPTH, D_S5, D_S5), D_S5 ** -0.5)
    s5_b_glu = nrm(ks[13], (DEPTH, D_S5), 0.01)
    conv_w = nrm(ks[14], (DEPTH, CONV_WIDTH, SSD_CONV_DIM), CONV_WIDTH ** -0.5)
    conv_b = nrm(ks[15], (DEPTH, SSD_CONV_DIM), 0.01)
    dt0 = jnp.exp(jax.random.uniform(ks[16], (DEPTH, SSD_HEADS), f32, math.log(1e-3), math.log(1e-1)))
    dt_bias = dt0 + jnp.log(-jnp.expm1(-dt0))
    A_log = jnp.log(jax.random.uniform(ks[17], (DEPTH, SSD_HEADS), f32, 1.0, 16.0))
    ssd_D = 1.0 + nrm(ks[18], (DEPTH, SSD_HEADS), 0.1)
    ssd_norm_w = 1.0 + nrm(ks[19], (DEPTH, D_SSD), 0.02)
    w_out = nrm(ks[20], (DEPTH, MIX_WIDTH, D_MODEL), MIX_WIDTH ** -0.5)
    ple_norm_w = 1.0 + nrm(ks[21], (DEPTH, D_MODEL), 0.02)
    w_ple_gate = nrm(ks[22], (DEPTH, D_MODEL, D_MODEL), D_MODEL ** -0.5)
    w_ple_proj = nrm(ks[23], (DEPTH, PLE_DIM, D_MODEL), PLE_DIM ** -0.5)
    final_norm_w = 1.0 + nrm(ks[24], (D_MODEL,), 0.02)
    return {"x": x, "p": p, "norm_w": norm_w, "w_in": w_in,
            "s5_A_re": s5_A_re, "s5_A_im": s5_A_im, "s5_log_dt": s5_log_dt,
            "s5_B_re": s5_B_re, "s5_B_im": s5_B_im, "s5_C_re": s5_C_re, "s5_C_im": s5_C_im,
            "s5_D": s5_D, "s5_w_glu": s5_w_glu, "s5_b_glu": s5_b_glu,
            "conv_w": conv_w, "conv_b": conv_b, "dt_bias": dt_bias, "A_log": A_log,
            "ssd_D": ssd_D, "ssd_norm_w": ssd_norm_w, "w_out": w_out,
            "ple_norm_w": ple_norm_w, "w_ple_gate": w_ple_gate, "w_ple_proj": w_ple_proj,
            "final_norm_w": final_norm_w}


def reference(x, p, norm_w, w_in, s5_A_re, s5_A_im, s5_log_dt, s5_B_re, s5_B_im,
              s5_C_re, s5_C_im, s5_D, s5_w_glu, s5_b_glu, conv_w, conv_b, dt_bias,
              A_log, ssd_D, ssd_norm_w, w_out, ple_norm_w, w_ple_gate, w_ple_proj,
              final_norm_w):
    h = x
    for i in range(DEPTH):
        hn = rmsnorm(h, norm_w[i])
        proj = hn @ w_in[i]
        u_s5, z_s5, z_ssd, xBC, dt_raw = jnp.split(proj, SPLITS, axis=-1)
        y_s5 = s5_branch(u_s5, s5_A_re[i], s5_A_im[i], s5_log_dt[i], s5_B_re[i], s5_B_im[i],
                         s5_C_re[i], s5_C_im[i], s5_D[i], s5_w_glu[i], s5_b_glu[i])
        y_s5 = y_s5 * jax.nn.silu(z_s5)
        y_ssd = ssd_branch(xBC, dt_raw, z_ssd, conv_w[i], conv_b[i], dt_bias[i], A_log[i],
                           ssd_D[i], ssd_norm_w[i])
        h = h + jnp.concatenate([y_s5, y_ssd], axis=-1) @ w_out[i]
        gate = jax.nn.sigmoid(rmsnorm(h, ple_norm_w[i]) @ w_ple_gate[i])
        h = h + (p[i] @ w_ple_proj[i]) * gate
    return rmsnorm(h, final_norm_w)
```

```python
import math
import numpy as np
import concourse.bass as bass
import concourse.mybir as mybir
from concourse.bass_utils import run_bass_kernel_spmd

F32 = mybir.dt.float32
BF16 = mybir.dt.bfloat16
ALU = mybir.AluOpType
AF = mybir.ActivationFunctionType
AX = mybir.AxisListType

D = 1024
L = 8192
NBATCH = 4
BLK = 1024
NBLK = L // BLK
DIN = 5136
PLE = 256
EPS = 1e-6


class Em:
    ENGS = ("pe", "act", "dve", "pool", "sp")

    def __init__(self, n_dma_sems=12, strict_same=True):
        self.ops = {e: [] for e in self.ENGS}
        self.count = {e: 0 for e in self.ENGS}
        self.waited = {e: {} for e in self.ENGS}
        self.last_w = {}
        self.readers = {}
        self.n_dma = n_dma_sems
        self.dma_cnt = [0] * n_dma_sems
        self.dma_rr = 0
        self.sw_rr = 0
        self.strict_same = strict_same
        self.alias = {}
        self.enabled = True
        self.regions = []

    def region(self, key, arena, lo, hi):
        for (k2, a2, lo2, hi2) in self.regions:
            if k2 == key:
                return
        for (k2, a2, lo2, hi2) in self.regions:
            if a2 == arena and lo < hi2 and lo2 < hi:
                self.alias.setdefault(key, []).append(k2)
                self.alias.setdefault(k2, []).append(key)
        self.regions.append((key, arena, lo, hi))

    def _exp(self, keys):
        out = []
        for k in keys:
            out.append(k)
            out.extend(self.alias.get(k, ()))
        return out

    def _deps(self, eng, reads, writes):
        need = {}

        def add(tok):
            if tok is None:
                return
            sk, v = tok
            if sk == eng and not (self.strict_same and eng != "pe"):
                return
            if need.get(sk, 0) < v:
                need[sk] = v

        def add_other(tok):
            if tok is not None and tok[0] != eng:
                add(tok)

        for k in self._exp(reads):
            add(self.last_w.get(k))
        for k in self._exp(writes):
            add_other(self.last_w.get(k))
            for r in self.readers.get(k, ()):
                add_other(r)
        out = []
        for sk, v in need.items():
            if self.waited[eng].get(sk, 0) < v:
                self.waited[eng][sk] = v
                out.append((sk, v))
        return out

    def _commit(self, tok, reads, writes):
        for k in reads:
            self.readers.setdefault(k, []).append(tok)
        for k in writes:
            self.last_w[k] = tok
            self.readers[k] = []

    def op(self, eng, fn, reads=(), writes=()):
        if not self.enabled:
            return None
        waits = self._deps(eng, reads, writes)
        self.count[eng] += 1
        tok = (eng, self.count[eng])
        self.ops[eng].append((waits, fn, (eng, 1)))
        self._commit(tok, reads, writes)

    def dma(self, eng, fn, reads=(), writes=()):
        if not self.enabled:
            return None
        waits = self._deps(eng, reads, writes)
        if eng == "pool":
            i = self.n_dma - 1 - (self.sw_rr % 2)
            self.sw_rr += 1
        else:
            i = self.dma_rr
            self.dma_rr = (self.dma_rr + 1) % (self.n_dma - 2)
        sk = "d%d" % i
        if self.dma_cnt[i] > 0 and self.waited[eng].get(sk, 0) < self.dma_cnt[i]:
            self.waited[eng][sk] = self.dma_cnt[i]
            waits.append((sk, self.dma_cnt[i]))
        self.dma_cnt[i] += 16
        tok = (sk, self.dma_cnt[i])
        self.ops[eng].append((waits, fn, (sk, 16)))
        self._commit(tok, reads, writes)
        return tok

    def final_wait(self, eng, toks):
        waits = []
        for sk, v in toks:
            if self.waited[eng].get(sk, 0) < v:
                self.waited[eng][sk] = v
                waits.append((sk, v))
        self.ops[eng].append((waits, None, None))

    def emit(self, nc, sems):
        handles = {"pe": None, "act": None, "dve": None, "pool": None, "sp": None}
        with nc.Block() as block:
            def run(eng_name):
                def _f(e):
                    for waits, fn, inc in self.ops[eng_name]:
                        for sk, v in waits:
                            e.wait_ge(sems[sk], v)
                        if fn is not None:
                            ins = fn(e)
                            ins.then_inc(sems[inc[0]], inc[1])
                return _f
            block.tensor(run("pe"))
            block.scalar(run("act"))
            block.vector(run("dve"))
            block.gpsimd(run("pool"))
            block.sync(run("sp"))


def build_program(cfg=None):
    cfg = dict(cfg or {})
    nblk = cfg.get("nblk", NBLK)
    do_s5 = cfg.get("s5", True)
    lvl = cfg.get("lvl", 9)
    plvl = cfg.get("plvl", 9)
    npre = cfg.get("npre", 4)
    nown = nblk - npre
    nc = bass.Bass("TRN2", target_bir_lowering=False)
    em = Em(strict_same=cfg.get("strict", True))
    import contextlib
    st = contextlib.ExitStack()

    def din(name, shape):
        return nc.dram_tensor(name, list(shape), F32, kind="ExternalInput").ap()

    def dscratch(name, shape, dt=BF16):
        return nc.dram_tensor(name, list(shape), dt, kind="Internal").ap()

    def sb(name, shape, dt=F32):
        return st.enter_context(nc.sbuf_tensor(name, list(shape), dt))

    x_d = din("x", [L, D])
    p_d = din("p", [nown * BLK, PLE])
    out_d = nc.dram_tensor("out", [nown * BLK, D], F32, kind="ExternalOutput").ap()
    flag_d = din("flag", [128, 1])
    fw_d = din("final_norm_w", [1, D])
    plw_d = din("ple_norm_w", [D])
    nw_d = din("norm_w", [D])
    snw_d = din("ssd_norm_w", [D])
    win_d = din("w_in", [D, DIN])
    wout_d = din("w_out", [2 * D, D])
    wgate_d = din("w_ple_gate", [D, D])
    wple_d = din("w_ple_proj", [PLE, D])
    convw_d = din("conv_w", [4, 2048])
    convb_d = din("conv_b", [2048])
    dtb_d = din("dt_bias", [1, 16])
    alog_d = din("A_log", [1, 16])
    sD_d = din("ssd_D", [1, 16])
    ident_d = din("c_ident", [128, 128])
    tri_d = din("c_tri", [128, 128])
    ones_d = din("c_ones", [128, 128])
    if do_s5:
        are_d = din("s5_A_re", [64, 64])
        aim_d = din("s5_A_im", [64, 64])
        ldt_d = din("s5_log_dt", [64])
        bre_d = din("s5_B_re", [64, 64, 16])
        bim_d = din("s5_B_im", [64, 64, 16])
        cre_d = din("s5_C_re", [64, 16, 64])
        cim_d = din("s5_C_im", [64, 16, 64])
        sD5_d = din("s5_D", [D])
        wglu_d = din("s5_w_glu", [D, D])
        bglu_d = din("s5_b_glu", [1, D])
        mask8_d = din("c_mask8", [128, 128])
        rowm_d = din("c_rowmask", [128, 2])
        wglu_b = dscratch("wglu_b", [D, D])

    win_b = dscratch("win_b", [D, DIN])
    wout_b = dscratch("wout_b", [2 * D, D])
    wgate_b = dscratch("wgate_b", [D, D])
    wple_b = dscratch("wple_b", [PLE, D])

    with st:
        sems = {}
        for e in Em.ENGS:
            sems[e] = st.enter_context(nc.semaphore("s_" + e))
        for i in range(em.n_dma):
            sems["d%d" % i] = st.enter_context(nc.semaphore("s_d%d" % i))

        def MM(out, lhsT, rhs, start, stop, r, w):
            em.op("pe", lambda e: e.matmul(out, lhsT, rhs, start=start, stop=stop), r, w)

        def ACT(out, in_, func, r, w, scale=1.0, bias=0.0, accum=None):
            if accum is None:
                em.op("act", lambda e: e.activation(out=out, in_=in_, func=func, scale=scale, bias=bias), r, w)
            else:
                em.op("act", lambda e: e.activation(out=out, in_=in_, func=func, scale=scale, bias=bias,
                                                    accum_out=accum), r, w)

        def TT(eng, out, in0, in1, op, r, w):
            em.op(eng, lambda e: e.tensor_tensor(out=out, in0=in0, in1=in1, op=op), r, w)

        def TS(eng, out, in0, s1, s2, op0, op1, r, w):
            if s2 is None:
                em.op(eng, lambda e: e.tensor_scalar(out=out, in0=in0, scalar1=s1, scalar2=None, op0=op0), r, w)
            else:
                em.op(eng, lambda e: e.tensor_scalar(out=out, in0=in0, scalar1=s1, scalar2=s2, op0=op0, op1=op1), r, w)

        def STT(out, in0, scalar, in1, op0, op1, r, w):
            em.op("dve", lambda e: e.scalar_tensor_tensor(out=out, in0=in0, scalar=scalar, in1=in1,
                                                          op0=op0, op1=op1), r, w)

        def CP(eng, out, in_, r, w):
            if eng == "act":
                em.op("act", lambda e: e.copy(out=out, in_=in_), r, w)
            else:
                em.op(eng, lambda e: e.tensor_copy(out=out, in_=in_), r, w)

        def MSET(eng, out, val, w):
            em.op(eng, lambda e: e.memset(out, val), [], w)

        def RECIP(out, in_, r, w):
            em.op("dve", lambda e: e.reciprocal(out=out, in_=in_), r, w)

        def DMA(eng, out, in_, r, w):
            return em.dma(eng, lambda e: e.dma_start(out=out, in_=in_), r, w)

        def DMAS(eng, out, in_, r, w):
            return em.dma(eng, lambda e: e.dma_start(out=out, in_=in_, allow_slow_non_contiguous=True), r, w)

        psb = [st.enter_context(nc.psum_tensor("ps%d" % i, [128, 512], F32)) for i in range(8)]
        ps_i = [0]

        def PS():
            i = ps_i[0]
            ps_i[0] = (i + 1) % 8
            return psb[i], "ps%d" % i

        RBYTES = cfg.get("rbytes", 30720)
        R = sb("R", [128, RBYTES // 4])
        Rb = R.bitcast(BF16)

        def rview(key, off, shape, dt):
            n = 1
            for s_ in shape[1:]:
                n *= s_
            if dt == F32:
                assert off % 4 == 0
                ap = R[:, off // 4: off // 4 + n]
                nb = n * 4
            else:
                assert off % 2 == 0
                ap = Rb[:, off // 2: off // 2 + n]
                nb = n * 2
            assert off + nb <= RBYTES, (key, off, nb)
            if len(shape) == 3:
                ap = ap.rearrange("p (a b) -> p a b", a=shape[1])
            elif len(shape) == 4:
                ap = ap.rearrange("p (a b c) -> p a b c", a=shape[1], b=shape[2])
            em.region(key, "R", off, off + nb)
            return ap

        ident_f = sb("ident_f", [128, 128])
        tri_f = sb("tri_f", [128, 128])
        ones_f = sb("ones_f", [128, 128])
        ident = sb("ident", [128, 128], BF16)
        DMA("sp", ident_f[:], ident_d, [], ["ident_f"])
        DMA("sp", tri_f[:], tri_d, [], ["tri_f"])
        DMA("sp", ones_f[:], ones_d, [], ["ones_f"])
        CP("dve", ident[:], ident_f[:], ["ident_f"], ["ident"])

        def TR(out, in_, r, w):
            em.op("pe", lambda e: e.transpose(out, in_, ident[:]), list(r) + ["ident"], w)

        fwb = sb("fwb", [128, D])
        DMA("sp", fwb[:], fw_d.partition_broadcast(128), [], ["fwb"])
        plw = sb("plw", [128, 8])
        nw = sb("nw", [128, 8])
        snw = sb("snw", [128, 8])
        DMAS("sp", plw[:], plw_d.rearrange("(k p) -> p k", p=128), [], ["plw"])
        DMAS("sp", nw[:], nw_d.rearrange("(k p) -> p k", p=128), [], ["nw"])
        DMAS("sp", snw[:], snw_d.rearrange("(k p) -> p k", p=128), [], ["snw"])
        cw = sb("cw", [128, 16, 4])
        cbv = sb("cbv", [128, 16])
        for k_ in range(4):
            DMAS("sp", cw[:, :, k_], convw_d[k_, :].rearrange("(c p) -> p c", p=128), [], ["cw"])
        DMAS("sp", cbv[:], convb_d.rearrange("(c p) -> p c", p=128), [], ["cbv"])
        dtb = sb("dtb", [128, 16])
        Abc = sb("Abc", [128, 16])
        Dhb = sb("Dhb", [128, 16])
        DMA("sp", dtb[:], dtb_d.partition_broadcast(128), [], ["dtb"])
        DMA("sp", Abc[:], alog_d.partition_broadcast(128), [], ["Abc"])
        DMA("sp", Dhb[:], sD_d.partition_broadcast(128), [], ["Dhb"])
        ACT(Abc[:], Abc[:], AF.Exp, ["Abc"], ["Abc"])
        TS("dve", Abc[:], Abc[:], -1.0, None, ALU.mult, None, ["Abc"], ["Abc"])

        SW = 2568
        stg = [rview("stg0", 0, [128, SW], BF16), rview("stg1", SW * 2, [128, SW], BF16)]
        stg_i = [0]

        def precast(src, dst, rows, cols, key):
            rpp = rows // 128
            sv = src.rearrange("(p r) c -> p r c", p=128)
            dv = dst.rearrange("(p r) c -> p r c", p=128)
            if cols <= SW:
                rstep = max(1, SW // cols)
                for r0 in range(0, rpp, rstep):
                    r1 = min(rpp, r0 + rstep)
                    i = stg_i[0]
                    stg_i[0] ^= 1
                    n = (r1 - r0) * cols
                    sview = stg[i][:, 0:n].rearrange("p (r c) -> p r c", c=cols)
                    DMA("pool", sview, sv[:, r0:r1, :], [], ["stg%d" % i])
                    DMA("sp", dv[:, r0:r1, :], sview, ["stg%d" % i], [key])
            else:
                for r0 in range(rpp):
                    for c0 in range(0, cols, SW):
                        c1 = min(cols, c0 + SW)
                        i = stg_i[0]
                        stg_i[0] ^= 1
                        sview = stg[i][:, 0:c1 - c0]
                        DMA("pool", sview, sv[:, r0, c0:c1], [], ["stg%d" % i])
                        DMA("sp", dv[:, r0, c0:c1], sview, ["stg%d" % i], [key])

        precast(win_d, win_b, D, DIN, "win_b")
        precast(wout_d, wout_b, 2 * D, D, "wout_b")
        precast(wgate_d, wgate_b, D, D, "wgate_b")
        precast(wple_d, wple_b, PLE, D, "wple_b")
        if do_s5:
            precast(wglu_d, wglu_b, D, D, "wglu_b")

        xt = sb("xt", [128, 8, D])
        em.region("xt", "XT", 0, 32768)
        xt_f = xt[:].rearrange("p t d -> p (t d)")
        xt_b = xt.bitcast(BF16)[:].rearrange("p t d -> p (t d)")

        def xview(key, off, shape, dt):
            n = 1
            for s_ in shape[1:]:
                n *= s_
            if dt == F32:
                ap = xt_f[:, off // 4: off // 4 + n]
                nb = n * 4
            else:
                ap = xt_b[:, off // 2: off // 2 + n]
                nb = n * 2
            assert off + nb <= 32768
            if len(shape) == 3:
                ap = ap.rearrange("p (a b) -> p a b", a=shape[1])
            elif len(shape) == 4:
                ap = ap.rearrange("p (a b c) -> p a b c", a=shape[1], b=shape[2])
            em.region(key, "XT", off, off + nb)
            return ap
        sq = sb("sq", [128, D], BF16)
        ss = sb("ss", [128, 8])
        rr = sb("rr", [128, 8])
        mix = sb("mix", [128, 16, 8, 128], BF16)
        mixA = mix[:, 0:8, :, :]
        hs = mix[:, 0:8, :, :].rearrange("p k t j -> p (k t j)").rearrange("p (t d) -> p t d", t=8)
        hn = sb("hn", [128, 8, BLK], BF16)
        NWB = cfg.get("nwb", 3)
        wbs = [sb("wb%d" % i, [128, 8, 512], BF16) for i in range(NWB)]
        wb_i = [0]
        wdt = sb("wdt", [128, 8, 16], BF16)
        DMAS("sp", wdt[:], win_b[:, 5120:5136].rearrange("(k p) c -> p k c", p=128), ["win_b"], ["wdt"])
        STf = sb("STf", [128, D])
        STb = sb("STb", [128, D], BF16)
        halo = sb("halo", [128, 16, 3], BF16)
        MSET("dve", STf[:], 0.0, ["STf"])
        MSET("dve", STb[:], 0.0, ["STb"])
        MSET("dve", halo[:], 0.0, ["halo"])
        dtr = sb("dtr", [128, 8, 16])
        dtv = sb("dtv", [128, 8, 16])
        av = sb("av", [128, 8, 16])
        acs = sb("acs", [128, 16])
        t16 = sb("t16", [128, 16])
        dec = sb("dec", [128, 16])
        el = sb("el", [128, 16])
        etot = sb("etot", [128, 16])
        ssg = sb("ssg", [128, 4])
        tri_b = sb("tri_b", [128, 128], BF16)
        CP("dve", tri_b[:], tri_f[:], ["tri_f"], ["tri_b"])
        ahi = sb("ahi", [128, 8, 16], BF16)
        alo = sb("alo", [128, 8, 16], BF16)
        alf = sb("alf", [128, 8, 16])
        rg = sb("rg", [128, 4])
        flg = sb("flg", [128, 1])
        DMA("sp", flg[:], flag_d, [], ["flg"])

        dq = [0]

        def load_w(src, col0, ncols, k0, nk, key):
            i = wb_i[0]
            wb_i[0] = (i + 1) % NWB
            view = wbs[i][:, 0:nk, 0:ncols]
            eng = "sp"
            DMA(eng, view, src[k0 * 128:(k0 + nk) * 128, col0:col0 + ncols].rearrange("(k p) c -> p k c", p=128),
                [key], ["wb%d" % i])
            return view, "wb%d" % i


        hn4 = hn[:].rearrange("p k (j t) -> p k t j", t=8)
        u8 = mix[:, 8:16, :, :].rearrange("p k t j -> p (k t j)").rearrange("p (t d) -> p t d", t=8)
        u8g = mix[:, 8:16, :, :].rearrange("p k t j -> p (k t j)").rearrange("p (g t h) -> p g t h", g=64, t=8)
        if do_s5:
            Q0r = sb("Q0r", [128, 32, 128], BF16)
            Q0i = sb("Q0i", [128, 32, 128], BF16)
            R0a = sb("R0a", [128, 64, 128], BF16)
            P0r = sb("P0r", [128, 32, 128], BF16)
            NP0i = sb("NP0i", [128, 32, 128], BF16)
            A1 = sb("A1", [128, 2, 32])
            A2 = sb("A2", [128, 2, 32])
            s5c = sb("s5c", [128, 2, 32])
            rt1 = sb("rt1", [128, 2, 32])
            rt2 = sb("rt2", [128, 2, 32])
            bglu_bf = sb("bglu_bf", [1, D], BF16)
            ones_bf = sb("ones_bf", [1, 128], BF16)
            Dcol = rview("Dcol", 22528, [128, 64], F32)
            mask8 = rview("mask8", 22528 + 256, [128, 128], F32)
            rowm = sb("rowm", [128, 2])
            DMA("sp", rowm[:], rowm_d, [], ["rowm"])
            sl = rview("sl", 24576, [128, 19, 32], F32)
            l2 = sb("l2", [32, 2])
            MSET("dve", s5c[:], 0.0, ["s5c"])
            MSET("dve", ones_bf[:], 1.0, ["ones_bf"])
            bglu_f = xview("bglu_f", 28672, [128, D], F32)
            DMA("sp", bglu_f[0:1, :], bglu_d, [], ["bglu_f"])
            CP("dve", bglu_bf[:], bglu_f[0:1, :], ["bglu_f"], ["bglu_bf"])
            DMA("sp", mask8, mask8_d, [], ["mask8"])
            for s_ in range(8):
                DMAS("sp", Dcol[s_ * 16:(s_ + 1) * 16, :], sD5_d.rearrange("(g h) -> h g", h=16), [], ["Dcol"])
            XA = rview("XA", 0, [128, 3, 128], F32)
            POWr = rview("POWr", 1536, [128, 32, 24], F32)
            POWi = rview("POWi", 1536 + 3072, [128, 32, 24], F32)
            Bslr = rview("Bslr", 8192, [128, 32, 16], F32)
            Bsli = rview("Bsli", 8192 + 2048, [128, 32, 16], F32)
            Bbr = rview("Bbr", 8192 + 4096, [128, 32, 16], F32)
            Bbi = rview("Bbi", 8192 + 6144, [128, 32, 16], F32)
            Cslr = rview("Cslr", 16384, [128, 32, 16], F32)
            Csli = rview("Csli", 16384 + 2048, [128, 32, 16], F32)
            hnf = hn.bitcast(F32)[:].rearrange("p k c -> p (k c)")
            xtf = xt[:].rearrange("p t d -> p (t d)")
            T1 = xtf[:, 0:4096].rearrange("p (g s h) -> p g s h", g=32, s=8)
            T2 = xtf[:, 4096:8192].rearrange("p (g s h) -> p g s h", g=32, s=8)
            tmpR = xtf[:, 0:512].rearrange("p (g l) -> p g l", g=4)
            mixf = mix[:].rearrange("p k t j -> p (k t j)")
            Qstr = mixf[:, 0:4096].rearrange("p (g c) -> p g c", g=32)
            Qsti = mixf[:, 4096:8192].rearrange("p (g c) -> p g c", g=32)
            P0mr = mixf[:, 8192:12288].rearrange("p (g c) -> p g c", g=32)
            NP0mi = mixf[:, 12288:16384].rearrange("p (g c) -> p g c", g=32)

            def SL(i):
                return sl[:, i, :]
            SLK = ["sl"]
            em.enabled = plvl >= 1
            DMA("sp", XA[0:32, 0, :], are_d.rearrange("(gp g2) n -> gp (g2 n)", g2=2), [], ["XA"])
            DMA("sp", XA[0:32, 1, :], aim_d.rearrange("(gp g2) n -> gp (g2 n)", g2=2), [], ["XA"])
            DMAS("sp", l2[:], ldt_d.rearrange("(gp g2) -> gp g2", g2=2), [], ["l2"])
            CP("dve", XA[0:32, 2, :].rearrange("p (a n) -> p a n", a=2), l2[:].unsqueeze(2).to_broadcast([32, 2, 64]),
               ["l2"], ["XA"])
            pA, pAk = PS()
            for i in range(3):
                em.op("pe", lambda e, i=i: e.transpose(pA[:, i * 32:(i + 1) * 32], XA[0:32, i, :], ident_f[0:32, 0:32]),
                      ["XA", "ident_f"], [pAk])
            iAr, iAi, iLd, iDt, iArd, iAid, iZr, iZi, iT, iU, iV, iCr, iCi, iLr1, iNr, iNi, iDen, iPr, iPi = range(19)
            for i in range(3):
                CP("dve", SL(i), pA[:, i * 32:(i + 1) * 32], [pAk], SLK)
            ACT(SL(iDt), SL(iLd), AF.Exp, SLK, SLK)
            TT("dve", SL(iArd), SL(iAr), SL(iDt), ALU.mult, SLK, SLK)
            TT("dve", SL(iAid), SL(iAi), SL(iDt), ALU.mult, SLK, SLK)
            ACT(SL(iT), SL(iArd), AF.Exp, SLK, SLK, scale=1.0 / 32)
            ACT(SL(iU), SL(iAid), AF.Sin, SLK, SLK, scale=1.0 / 32)
            ACT(SL(iV), SL(iAid), AF.Sin, SLK, SLK, scale=1.0 / 32, bias=math.pi / 2)
            TT("dve", SL(iZr), SL(iT), SL(iV), ALU.mult, SLK, SLK)
            TT("dve", SL(iZi), SL(iT), SL(iU), ALU.mult, SLK, SLK)
            for _ in range(5):
                TT("dve", SL(iT), SL(iZr), SL(iZr), ALU.mult, SLK, SLK)
                TT("dve", SL(iU), SL(iZi), SL(iZi), ALU.mult, SLK, SLK)
                TT("dve", SL(iV), SL(iZr), SL(iZi), ALU.mult, SLK, SLK)
                TT("dve", SL(iZr), SL(iT), SL(iU), ALU.subtract, SLK, SLK)
                TS("dve", SL(iZi), SL(iV), 2.0, None, ALU.mult, None, SLK, SLK)
            PK = ["POWr", "POWi"]
            CP("dve", SL(iPr), SL(iZr), SLK, SLK)
            CP("dve", SL(iPi), SL(iZi), SLK, SLK)
            MSET("dve", POWr[:, :, 15], 1.0, PK)
            MSET("dve", POWi[:, :, 15], 0.0, PK)
            MSET("dve", POWr[:, :, 23], 1.0, PK)
            MSET("dve", POWi[:, :, 23], 0.0, PK)
            for k in range(1, 9):
                if k > 1:
                    TT("dve", SL(iT), SL(iPr), SL(iZr), ALU.mult, SLK, SLK)
                    TT("dve", SL(iU), SL(iPi), SL(iZi), ALU.mult, SLK, SLK)
                    TT("dve", SL(iV), SL(iPr), SL(iZi), ALU.mult, SLK, SLK)
                    TT("dve", SL(iPr), SL(iT), SL(iU), ALU.subtract, SLK, SLK)
                    TT("dve", SL(iT), SL(iPi), SL(iZr), ALU.mult, SLK, SLK)
                    TT("dve", SL(iPi), SL(iT), SL(iV), ALU.add, SLK, SLK)
                CP("dve", POWr[:, :, k - 1], SL(iPr), SLK, PK)
                CP("dve", POWi[:, :, k - 1], SL(iPi), SLK, PK)
                if k <= 7:
                    CP("dve", POWr[:, :, 8 + 7 - k], SL(iPr), SLK, PK)
                    CP("dve", POWi[:, :, 8 + 7 - k], SL(iPi), SLK, PK)
                    TT("dve", SL(iT), SL(iPr), SL(iPr), ALU.mult, SLK, SLK)
                    TT("dve", SL(iU), SL(iPi), SL(iPi), ALU.mult, SLK, SLK)
                    TT("dve", SL(iT), SL(iT), SL(iU), ALU.add, SLK, SLK)
                    RECIP(SL(iT), SL(iT), SLK, SLK)
                    TT("dve", POWr[:, :, 16 + 7 - k], SL(iPr), SL(iT), ALU.mult, SLK, PK)
                    STT(POWi[:, :, 16 + 7 - k], SL(iPi), -1.0, SL(iT), ALU.mult, ALU.mult, SLK, PK)
            CP("dve", A1[:, 0, :], POWr[:, :, 7], PK, ["A1"])
            CP("dve", A1[:, 1, :], POWr[:, :, 7], PK, ["A1"])
            TS("dve", A2[:, 0, :], POWi[:, :, 7], -1.0, None, ALU.mult, None, PK, ["A2"])
            CP("dve", A2[:, 1, :], POWi[:, :, 7], PK, ["A2"])
            A1T = sb("A1T", [128, 8, 2, 32])
            A2T = sb("A2T", [128, 8, 2, 32])
            CP("dve", SL(iPr), POWr[:, :, 7], PK, SLK)
            CP("dve", SL(iPi), POWi[:, :, 7], PK, SLK)
            for l_ in range(8):
                if l_ > 0:
                    TT("dve", SL(iT), SL(iPr), SL(iPr), ALU.mult, SLK, SLK)
                    TT("dve", SL(iU), SL(iPi), SL(iPi), ALU.mult, SLK, SLK)
                    TT("dve", SL(iV), SL(iPr), SL(iPi), ALU.mult, SLK, SLK)
                    TT("dve", SL(iPr), SL(iT), SL(iU), ALU.subtract, SLK, SLK)
                    TS("dve", SL(iPi), SL(iV), 2.0, None, ALU.mult, None, SLK, SLK)
                CP("dve", A1T[:, l_, 0, :], SL(iPr), SLK, ["A1T"])
                CP("dve", A1T[:, l_, 1, :], SL(iPr), SLK, ["A1T"])
                TS("dve", A2T[:, l_, 0, :], SL(iPi), -1.0, None, ALU.mult, None, SLK, ["A2T"])
                CP("dve", A2T[:, l_, 1, :], SL(iPi), SLK, ["A2T"])
            TS("dve", SL(iLr1), SL(iZr), -1.0, None, ALU.add, None, SLK, SLK)
            TT("dve", SL(iT), SL(iLr1), SL(iAr), ALU.mult, SLK, SLK)
            TT("dve", SL(iU), SL(iZi), SL(iAi), ALU.mult, SLK, SLK)
            TT("dve", SL(iNr), SL(iT), SL(iU), ALU.add, SLK, SLK)
            TT("dve", SL(iT), SL(iZi), SL(iAr), ALU.mult, SLK, SLK)
            TT("dve", SL(iU), SL(iLr1), SL(iAi), ALU.mult, SLK, SLK)
            TT("dve", SL(iNi), SL(iT), SL(iU), ALU.subtract, SLK, SLK)
            TT("dve", SL(iT), SL(iAr), SL(iAr), ALU.mult, SLK, SLK)
            TT("dve", SL(iU), SL(iAi), SL(iAi), ALU.mult, SLK, SLK)
            TT("dve", SL(iDen), SL(iT), SL(iU), ALU.add, SLK, SLK)
            RECIP(SL(iDen), SL(iDen), SLK, SLK)
            TT("dve", SL(iCr), SL(iNr), SL(iDen), ALU.mult, SLK, SLK)
            TT("dve", SL(iCi), SL(iNi), SL(iDen), ALU.mult, SLK, SLK)

            def bc_pow(P, lo):
                return P[:, :, lo:lo + 8].unsqueeze(3).to_broadcast([128, 32, 8, 16])

            def bc_v(V):
                return V.unsqueeze(2).to_broadcast([128, 32, 8, 16])

            def cplx_table(lo, Vr, Vi, vk, outr, outi, okr, oki, neg_i):
                o4r = outr.rearrange("p g (s h) -> p g s h", s=8)
                o4i = outi.rearrange("p g (s h) -> p g s h", s=8)
                TT("dve", T1, bc_pow(POWr, lo), bc_v(Vr), ALU.mult, PK + vk, ["xt"])
                TT("pool", T2, bc_pow(POWi, lo), bc_v(Vi), ALU.mult, PK + vk, ["sq"])
                TT("dve", o4r, T1, T2, ALU.subtract, ["xt", "sq"], okr)
                TT("dve", T1, bc_pow(POWr, lo), bc_v(Vi), ALU.mult, PK + vk, ["xt"])
                TT("pool", T2, bc_pow(POWi, lo), bc_v(Vr), ALU.mult, PK + vk, ["sq"])
                if neg_i:
                    TT("dve", T1, T1, T2, ALU.add, ["xt", "sq"], ["xt"])
                    TS("dve", o4i, T1, -1.0, None, ALU.mult, None, ["xt"], oki)
                else:
                    TT("dve", o4i, T1, T2, ALU.add, ["xt", "sq"], oki)

            em.enabled = plvl >= 2
            for a_ in range(2):
                DMA("sp", hnf[0:32, 0:2048].rearrange("p (h a n) -> p h a n", h=16, a=2)[:, :, a_, :],
                    cre_d.rearrange("(gp g2) h n -> gp g2 h n", g2=2)[:, a_, :, :], [], ["hn"])
                DMA("act", hnf[0:32, 2048:4096].rearrange("p (h a n) -> p h a n", h=16, a=2)[:, :, a_, :],
                    cim_d.rearrange("(gp g2) h n -> gp g2 h n", g2=2)[:, a_, :, :], [], ["hn"])
            for comp, dst, dk in ((0, Cslr, "Cslr"), (1, Csli, "Csli")):
                pC, pCk = PS()
                src4 = hnf[0:32, comp * 2048:(comp + 1) * 2048].rearrange("p (h c) -> p h c", h=16)
                for h_ in range(16):
                    em.op("pe", lambda e, h_=h_, src4=src4, pC=pC: e.transpose(
                        pC[:, h_ * 32:(h_ + 1) * 32], src4[:, h_, :], ident_f[0:32, 0:32]),
                          ["hn", "ident_f"], [pCk])
                CP("dve", dst, pC[:].rearrange("p (h g) -> p g h", h=16), [pCk], [dk])
            cplx_table(0, Cslr, Csli, ["Cslr", "Csli"], P0r[:], NP0i[:], ["P0r"], ["NP0i"], True)
            cplx_table(16, Cslr, Csli, ["Cslr", "Csli"], P0mr, NP0mi, ["mixB"], ["mixB"], True)
            em.enabled = plvl >= 3
            DMA("sp", hnf[0:32, 0:2048], bre_d.rearrange("(gp g2) n h -> gp (g2 n h)", g2=2), [], ["hn"])
            DMA("act", hnf[0:32, 2048:4096], bim_d.rearrange("(gp g2) n h -> gp (g2 n h)", g2=2), [], ["hn"])
            for comp, dst, dk in ((0, Bslr, "Bslr"), (1, Bsli, "Bsli")):
                pB, pBk = PS()
                src3 = hnf[0:32, comp * 2048:(comp + 1) * 2048].rearrange("p (c h) -> p c h", h=16)
                xb2 = xtf[0:32, comp * 2048:(comp + 1) * 2048].rearrange("p (h c) -> p h c", h=16)
                CP("dve", xb2.rearrange("p h c -> p c h"), src3, ["hn"], ["xt"])
                for h_ in range(16):
                    em.op("pe", lambda e, h_=h_, xb2=xb2, pB=pB: e.transpose(pB[:, h_ * 32:(h_ + 1) * 32],
                                                                           xb2[:, h_, :], ident_f[0:32, 0:32]),
                          ["xt", "ident_f"], [pBk])
                CP("dve", dst, pB[:].rearrange("p (h g) -> p g h", h=16), [pBk], [dk])
            crb = SL(iCr).unsqueeze(2).to_broadcast([128, 32, 16])
            cib = SL(iCi).unsqueeze(2).to_broadcast([128, 32, 16])
            T1s = xtf[:, 0:512].rearrange("p (g h) -> p g h", g=32)
            T2s = xtf[:, 512:1024].rearrange("p (g h) -> p g h", g=32)
            TT("dve", T1s, Bslr, crb, ALU.mult, ["Bslr"] + SLK, ["xt"])
            TT("dve", T2s, Bsli, cib, ALU.mult, ["Bsli"] + SLK, ["xt"])
            TT("dve", Bbr, T1s, T2s, ALU.subtract, ["xt"], ["Bbr"])
            TT("dve", T1s, Bsli, crb, ALU.mult, ["Bsli"] + SLK, ["xt"])
            TT("dve", T2s, Bslr, cib, ALU.mult, ["Bslr"] + SLK, ["xt"])
            TT("dve", Bbi, T1s, T2s, ALU.add, ["xt"], ["Bbi"])
            cplx_table(8, Bbr, Bbi, ["Bbr", "Bbi"], Qstr, Qsti, ["mixA"], ["mixA"], False)
            em.enabled = plvl >= 4
            for comp, src, dst, dk in ((0, Qstr, Q0r, "Q0r"), (1, Qsti, Q0i, "Q0i")):
                for g8 in range(4):
                    pq, pqk = PS()
                    pqv = pq.bitcast(BF16)[:].rearrange("p (g c) -> p g c", g=8)
                    for gi in range(8):
                        TR(pqv[:, gi, :], src[:, g8 * 8 + gi, :], ["mixA"], [pqk])
                    CP("act", dst[:, g8 * 8:(g8 + 1) * 8, :], pqv, [pqk], [dk])
            em.enabled = plvl >= 5
            tmpP = [rview("tmpP0", 20480, [128, 4, 128], BF16), rview("tmpP1", 21504, [128, 4, 128], BF16)]
            for g4 in range(16):
                pR, pRk = PS()
                for gi in range(4):
                    g = g4 * 4 + gi
                    gp, g2 = divmod(g, 2)
                    tp = tmpP[gp % 2]
                    tpk = "tmpP%d" % (gp % 2)
                    if g2 == 0:
                        for a_ in range(2):
                            TS("dve", tp[:, a_, :], P0mr[:, gp, :], rowm[:, a_:a_ + 1], None, ALU.mult, None,
                               ["mixB", "rowm"], [tpk])
                            TS("dve", tp[:, 2 + a_, :], NP0mi[:, gp, :], rowm[:, a_:a_ + 1], None, ALU.mult, None,
                               ["mixB", "rowm"], [tpk])
                    MM(pR[:, gi * 128:(gi + 1) * 128], Qstr[:, gp, :], tp[:, g2, :], True, False,
                       ["mixA", tpk], [pRk])
                    MM(pR[:, gi * 128:(gi + 1) * 128], Qsti[:, gp, :], tp[:, 2 + g2, :], False, True,
                       ["mixA", tpk], [pRk])
                TT("dve", tmpR, pR[:].rearrange("p (g l) -> p g l", g=4),
                   mask8.unsqueeze(1).to_broadcast([128, 4, 128]), ALU.mult, [pRk, "mask8"], ["xt"])
                for gi in range(4):
                    g = g4 * 4 + gi
                    STT(R0a[:, g, :], ident_f[:], Dcol[:, g:g + 1], tmpR[:, gi, :], ALU.mult, ALU.add,
                        ["ident_f", "Dcol", "xt"], ["R0a"])

            em.enabled = True
            U8T = rview("U8T", 0, [128, 64, 128], BF16)
            y1fm = rview("y1fm", 0, [128, 8, 8, 128], BF16)
            Vx2 = xview("Vx2", 0, [128, 2, 32, 128], F32)
            Gs = [rview("Gs0", 16384, [128, 2, 4, 128], BF16),
                  rview("Gs1", 16384 + 2048, [128, 2, 4, 128], BF16)]
            sg5 = rview("sg5", 16384 + 4096, [128, 512], F32)
            trt = rview("trt", 0, [128, 2, 32, 64], F32)
            acc = rview("acc", 20480, [128, 2, 32, 8], F32)
            t1b = rview("t1b", 22528, [128, 2, 32, 8], F32)
            t2b = rview("t2b", 24576, [128, 2, 32, 8], F32)
            Cst = rview("Cst", 26624, [128, 2, 32, 9], F32)
            zs5 = rview("zs5", 16384 + 6144, [128, 512], F32)

        o = 0
        xc = xview("xc", 0, [128, 16, 512], BF16)
        pre = [rview("pre0", o, [128, 516], BF16), rview("pre1", o + 1032, [128, 516], BF16)]; o += 2064
        dg = [rview("dg0", o, [128, 4, 128], BF16), rview("dg1", o + 1024, [128, 4, 128], BF16)]; o += 2048
        zsb = xview("zsb", 16384, [128, 4, D], BF16)
        abc_l = [rview("abc0", o, [128, 2, 4, 128], BF16), xview("abc1", 24576, [128, 2, 4, 128], BF16)]; o += 2048
        dm_l = [rview("dm0", o, [128, 4, 128], F32), xview("dm1", 24576 + 2048, [128, 4, 128], F32)]; o += 2048
        Ee_l = [rview("Ee0", o, [128, 4, 128], F32), xview("Ee1", 24576 + 4096, [128, 4, 128], F32)]; o += 2048
        Mh_l = [rview("Mh0", o, [128, 4, 128], BF16), xview("Mh1", 24576 + 6144, [128, 4, 128], BF16)]; o += 1024
        GTm = rview("GTm", o, [128, 4, 128], F32); o += 2048
        xd = rview("xd", o, [128, 16, 64], BF16); o += 2048
        xdd = rview("xdd", o, [128, 16, 64], BF16); o += 2048
        xD = rview("xD", o, [128, 16, 64], BF16); o += 2048
        yv = rview("yv", o, [128, D], F32); o += 4096
        gn = rview("gn", o, [128, D], BF16); o += 2048
        Bc = rview("Bc", o, [128, 4, 128], BF16); o += 1024
        o = 0
        pt = rview("pt", o, [128, 8, PLE], F32); o += 8192
        pbf = rview("pbf", o, [128, 8, PLE], BF16); o += 4096
        p_fm = rview("p_fm", o, [128, 2, 8, 128], BF16); o += 4096
        gsig = rview("gsig", o, [128, 512], F32); o += 2048
        tmp2 = rview("tmp2", o, [128, 512], F32); o += 2048

        def rms_rr(src_key):
            for t in range(8):
                ACT(sq[:], xt[:, t, :], AF.Square, [src_key], ["sq", "ss"], accum=ss[:, t:t + 1])
            ACT(rr[:], ss[:], AF.Sqrt, ["ss"], ["rr"], scale=1.0 / D, bias=EPS)
            RECIP(rr[:], rr[:], ["rr"], ["rr"])

        out_toks = []
        for b in range(nblk):
            t0 = b * BLK
            full = b >= npre
            o0 = (b - npre) * BLK
            if b == npre and npre > 0:
                TS("dve", STf[:], STf[:], flg[:, 0:1], None, ALU.mult, None, ["STf", "flg"], ["STf"])
                TS("dve", STb[:], STb[:], flg[:, 0:1], None, ALU.mult, None, ["STb", "flg"], ["STb"])
                TS("dve", halo[:], halo[:], flg[:, 0:1], None, ALU.mult, None, ["halo", "flg"], ["halo"])
                if do_s5:
                    TS("dve", s5c[:], s5c[:], flg[:, 0:1], None, ALU.mult, None, ["s5c", "flg"], ["s5c"])
            DMA("sp", xt[:], x_d[t0:t0 + BLK, :].rearrange("(j t) d -> j t d", t=8), [], ["xt"])
            rms_rr("xt")
            for t in range(8):
                TS("pool" if t % 2 else "dve", hs[:, t, :], xt[:, t, :], rr[:, t:t + 1], None, ALU.mult, None,
                   ["xt", "rr"], ["mixA"])
            for kt in range(8):
                pt_, pk = PS()
                pv = pt_.bitcast(BF16)[:].rearrange("p (t j) -> p t j", t=8)
                for t in range(8):
                    TR(pv[:, t, :], hs[:, t, kt * 128:(kt + 1) * 128], ["mixA"], [pk])
                if kt % 2:
                    ACT(hn[:, kt, :].rearrange("p (j t) -> p t j", t=8), pv, AF.Copy, [pk, "nw"], ["hn"],
                        scale=nw[:, kt:kt + 1])
                else:
                    TS("dve", hn[:, kt, :].rearrange("p (j t) -> p t j", t=8), pv, nw[:, kt:kt + 1], None,
                       ALU.mult, None, [pk, "nw"], ["hn"])
            if not do_s5:
                MSET("pool", mixA, 0.0, ["mixA"])

            if do_s5 and lvl >= 1:
                for cb in range(2):
                    cs = slice(cb * 512, (cb + 1) * 512)
                    wv, wk = load_w(win_b, cb * 512, 512, 0, 8, "win_b")
                    for t in range(8):
                        pu, puk = PS()
                        for kt in range(8):
                            MM(pu[:], hn4[:, kt, t, :], wv[:, kt, :], kt == 0, kt == 7, ["hn", wk], [puk])
                        CP("act", u8g[:, cb * 32:(cb + 1) * 32, t, :], pu[:].rearrange("p (g h) -> p g h", h=16),
                           [puk], ["mixB"])
                for g8 in range(8):
                    pq, pqk = PS()
                    pqv = pq.bitcast(BF16)[:].rearrange("p (g c) -> p g c", g=8)
                    for gi in range(8):
                        g = g8 * 8 + gi
                        TR(pqv[:, gi, :], u8g[:, g, :, :].rearrange("p t h -> p (t h)"), ["mixB"], [pqk])
                    CP("act" if g8 % 2 else "dve", U8T[:, g8 * 8:(g8 + 1) * 8, :], pqv, [pqk], ["U8T"])
                if lvl >= 2:
                    for gp4 in range(8):
                        pvr, pvrk = PS()
                        pvi, pvik = PS()
                        for gq in range(4):
                            gp = gp4 * 4 + gq
                            for g2 in range(2):
                                g = 2 * gp + g2
                                rows = slice(g2 * 64, (g2 + 1) * 64)
                                MM(pvr[rows, gq * 128:(gq + 1) * 128], Q0r[:, gp, rows], U8T[:, g, :], True, True,
                                   ["Q0r", "U8T"], [pvrk])
                                MM(pvi[rows, gq * 128:(gq + 1) * 128], Q0i[:, gp, rows], U8T[:, g, :], True, True,
                                   ["Q0i", "U8T"], [pvik])
                        CP("act", Vx2[:, 0, gp4 * 4:(gp4 + 1) * 4, :], pvr[:].rearrange("p (g j) -> p g j", g=4),
                           [pvrk], ["Vx2"])
                        CP("dve", Vx2[:, 1, gp4 * 4:(gp4 + 1) * 4, :], pvi[:].rearrange("p (g j) -> p g j", g=4),
                           [pvik], ["Vx2"])
                if lvl >= 3 and not full:
                    for l_ in range(7):
                        s_ = 1 << l_
                        n_ = 64 >> l_
                        Xa = Vx2[:, :, :, s_ - 1::2 * s_]
                        Xb = Vx2[:, :, :, 2 * s_ - 1::2 * s_]
                        tv = trt[:, :, :, 0:n_]
                        TT("dve", tv, Xa, A1T[:, l_, :, :].unsqueeze(3).to_broadcast([128, 2, 32, n_]), ALU.mult,
                           ["Vx2", "A1T"], ["trt"])
                        TT("dve", Xb, Xb, tv, ALU.add, ["Vx2", "trt"], ["Vx2"])
                        TT("dve", tv[:, 0, :, :], Xa[:, 1, :, :],
                           A2T[:, l_, 0, :].unsqueeze(2).to_broadcast([128, 32, n_]), ALU.mult, ["Vx2", "A2T"], ["trt"])
                        TT("dve", tv[:, 1, :, :], Xa[:, 0, :, :],
                           A2T[:, l_, 1, :].unsqueeze(2).to_broadcast([128, 32, n_]), ALU.mult, ["Vx2", "A2T"], ["trt"])
                        TT("dve", Xb, Xb, tv, ALU.add, ["Vx2", "trt"], ["Vx2"])
                    TT("dve", rt1[:], s5c[:], A1T[:, 7, :, :], ALU.mult, ["s5c", "A1T"], ["rt1"])
                    TT("dve", rt2[:, 0, :], s5c[:, 1, :], A2T[:, 7, 0, :], ALU.mult, ["s5c", "A2T"], ["rt2"])
                    TT("dve", rt2[:, 1, :], s5c[:, 0, :], A2T[:, 7, 1, :], ALU.mult, ["s5c", "A2T"], ["rt2"])
                    TT("dve", rt1[:], rt1[:], rt2[:], ALU.add, ["rt1", "rt2"], ["rt1"])
                    TT("dve", s5c[:], Vx2[:, :, :, 127], rt1[:], ALU.add, ["Vx2", "rt1"], ["s5c"])
                if lvl >= 3 and full and cfg.get("rec_blocked", False):
                    A1b = A1[:].unsqueeze(3).to_broadcast([128, 2, 32, 8])
                    A2b0 = A2[:, 0, :].unsqueeze(2).to_broadcast([128, 32, 8])
                    A2b1 = A2[:, 1, :].unsqueeze(2).to_broadcast([128, 32, 8])

                    def Vi(i):
                        return Vx2[:, :, :, i::16]
                    CP("dve", acc, Vi(0), ["Vx2"], ["acc"])
                    for i in range(1, 16):
                        TT("dve", t1b, acc, A1b, ALU.mult, ["acc", "A1"], ["t1b"])
                        TT("dve", t2b[:, 0, :, :], acc[:, 1, :, :], A2b0, ALU.mult, ["acc", "A2"], ["t2b"])
                        TT("dve", t2b[:, 1, :, :], acc[:, 0, :, :], A2b1, ALU.mult, ["acc", "A2"], ["t2b"])
                        TT("dve", acc, t1b, Vi(i), ALU.add, ["t1b", "Vx2"], ["acc"])
                        TT("dve", acc, acc, t2b, ALU.add, ["acc", "t2b"], ["acc"])
                    CP("dve", Cst[:, :, :, 0], s5c[:], ["s5c"], ["Cst"])
                    for s_ in range(8):
                        TT("dve", rt1[:], Cst[:, :, :, s_], A1T[:, 4, :, :], ALU.mult, ["Cst", "A1T"], ["rt1"])
                        TT("dve", rt2[:, 0, :], Cst[:, 1, :, s_], A2T[:, 4, 0, :], ALU.mult, ["Cst", "A2T"], ["rt2"])
                        TT("dve", rt2[:, 1, :], Cst[:, 0, :, s_], A2T[:, 4, 1, :], ALU.mult, ["Cst", "A2T"], ["rt2"])
                        TT("dve", Cst[:, :, :, s_ + 1], rt1[:], acc[:, :, :, s_], ALU.add, ["rt1", "acc"], ["Cst"])
                        TT("dve", Cst[:, :, :, s_ + 1], Cst[:, :, :, s_ + 1], rt2[:], ALU.add, ["Cst", "rt2"], ["Cst"])
                    for i in range(16):
                        if i == 0:
                            prev, prv0, prv1, pk_ = Cst[:, :, :, 0:8], Cst[:, 0, :, 0:8], Cst[:, 1, :, 0:8], "Cst"
                        else:
                            prev, prv0, prv1, pk_ = Vi(i - 1), Vx2[:, 0, :, i - 1::16], Vx2[:, 1, :, i - 1::16], "Vx2"
                        TT("dve", t1b, prev, A1b, ALU.mult, [pk_, "A1"], ["t1b"])
                        TT("dve", t2b[:, 0, :, :], prv1, A2b0, ALU.mult, [pk_, "A2"], ["t2b"])
                        TT("dve", t2b[:, 1, :, :], prv0, A2b1, ALU.mult, [pk_, "A2"], ["t2b"])
                        TT("dve", Vi(i), Vi(i), t1b, ALU.add, ["Vx2", "t1b"], ["Vx2"])
                        TT("dve", Vi(i), Vi(i), t2b, ALU.add, ["Vx2", "t2b"], ["Vx2"])
                if lvl >= 3 and full and not cfg.get("rec_blocked", False):
                    for j in range(128):
                        if j == 0:
                            Gj, Gjr, Gji, gk = s5c[:], s5c[:, 0, :], s5c[:, 1, :], "s5c"
                        else:
                            Gj, Gjr, Gji, gk = Vx2[:, :, :, j - 1], Vx2[:, 0, :, j - 1], Vx2[:, 1, :, j - 1], "Vx2"
                        TT("dve", rt1[:], Gj, A1[:], ALU.mult, [gk, "A1"], ["rt1"])
                        TT("dve", rt2[:, 0, :], Gji, A2[:, 0, :], ALU.mult, [gk, "A2"], ["rt2"])
                        TT("dve", rt2[:, 1, :], Gjr, A2[:, 1, :], ALU.mult, [gk, "A2"], ["rt2"])
                        TT("dve", Vx2[:, :, :, j], Vx2[:, :, :, j], rt1[:], ALU.add, ["Vx2", "rt1"], ["Vx2"])
                        TT("dve", Vx2[:, :, :, j], Vx2[:, :, :, j], rt2[:], ALU.add, ["Vx2", "rt2"], ["Vx2"])
                if lvl >= 4 and full:
                    for g4 in range(16):
                        gs = Gs[g4 % 2]
                        gsk = "Gs%d" % (g4 % 2)
                        if g4 < 2:
                            MSET("pool", gs, 0.0, [gsk])
                        for a_ in range(2):
                            rws = slice(a_ * 64, (a_ + 1) * 64)
                            gsv = gs.rearrange("p r (q a) j -> p r q a j", a=2)
                            CP("act" if a_ else "dve", gsv[rws, :, :, a_, 1:128], Vx2[rws, :, g4 * 2:(g4 + 1) * 2, 0:127],
                               ["Vx2"], [gsk])
                            CP("dve" if a_ else "act", gsv[rws, :, :, a_, 0], s5c[rws, :, g4 * 2:(g4 + 1) * 2], ["s5c"], [gsk])
                        py_, pyk = PS()
                        for gi in range(4):
                            g = g4 * 4 + gi
                            gp, g2 = divmod(g, 2)
                            rows = slice(g2 * 64, (g2 + 1) * 64)
                            osl = py_[:, gi * 128:(gi + 1) * 128]
                            MM(osl, U8T[:, g, :], R0a[:, g, :], True, False, ["U8T", "R0a"], [pyk])
                            MM(osl, gs[:, 0, gi, :], P0r[:, gp, :], False, False, [gsk, "P0r"], [pyk])
                            MM(osl, gs[:, 1, gi, :], NP0i[:, gp, :], False, True, [gsk, "NP0i"], [pyk])
                        ACT(u8[:, :, 64 * g4:64 * (g4 + 1)].rearrange("p t (g h) -> p g t h", g=4),
                            py_[:].rearrange("p (g t h) -> p g t h", g=4, t=8), AF.Gelu_apprx_tanh, [pyk], ["mixB"])
                    CP("dve", s5c[:], Vx2[:, :, :, 127], ["Vx2"], ["s5c"])
                if lvl >= 9 and full:
                    for kt in range(8):
                        pt_, pk = PS()
                        pv = pt_.bitcast(BF16)[:].rearrange("p (t j) -> p t j", t=8)
                        for t in range(8):
                            TR(pv[:, t, :], u8[:, t, kt * 128:(kt + 1) * 128], ["mixB"], [pk])
                        CP("act" if kt % 2 else "dve", y1fm[:, kt, :, :], pv, [pk], ["y1fm"])
                    for cb in range(2):
                        cs = slice(cb * 512, (cb + 1) * 512)
                        wv, wk = load_w(wglu_b, cb * 512, 512, 0, 8, "wglu_b")
                        wz, wzk = load_w(win_b, 1024 + cb * 512, 512, 0, 8, "win_b")
                        for t in range(8):
                            pg, pgk = PS()
                            for kt in range(8):
                                MM(pg[:], y1fm[:, kt, t, :], wv[:, kt, :], kt == 0, False, ["y1fm", wk], [pgk])
                            MM(pg[:], ones_bf[0:1, :], bglu_bf[0:1, cs], False, True, ["ones_bf", "bglu_bf"], [pgk])
                            pz, pzk = PS()
                            for kt in range(8):
                                MM(pz[:], hn4[:, kt, t, :], wz[:, kt, :], kt == 0, kt == 7, ["hn", wzk], [pzk])
                            ACT(sg5, pg[:], AF.Sigmoid, [pgk], ["sg5"])
                            ACT(zs5, pz[:], AF.Silu, [pzk], ["zs5"])
                            TT("dve", sg5, sg5, zs5, ALU.mult, ["sg5", "zs5"], ["sg5"])
                            TT("dve", u8[:, t, cs], u8[:, t, cs], sg5, ALU.mult, ["mixB", "sg5"], ["mixB"])
                    for kt in range(8):
                        pt_, pk = PS()
                        pv = pt_.bitcast(BF16)[:].rearrange("p (t j) -> p t j", t=8)
                        for t in range(8):
                            TR(pv[:, t, :], u8[:, t, kt * 128:(kt + 1) * 128], ["mixB"], [pk])
                        CP("act" if kt % 2 else "dve", mix[:, kt, :, :], pv, [pk], ["mixA"])
            if do_s5 and lvl < 9:
                MSET("pool", mixA, 0.0, ["mixA"])
            pd_, pdk = PS()
            pdt = pd_[:, 0:128].rearrange("p (c h) -> p c h", h=16)
            for c in range(8):
                for kt in range(8):
                    MM(pdt[:, c, :], hn[:, kt, c * 128:(c + 1) * 128], wdt[:, kt, :], kt == 0, kt == 7,
                       ["hn", "wdt"], [pdk])
            TT("dve", dtr[:], pdt, dtb[:].unsqueeze(1).to_broadcast([128, 8, 16]), ALU.add, [pdk, "dtb"], ["dtr"])
            ACT(dtr[:], dtr[:], AF.Exp, ["dtr"], ["dtr"])
            ACT(dtv[:], dtr[:], AF.Ln, ["dtr"], ["dtv"], bias=1.0)
            TT("dve", av[:], dtv[:], Abc[:].unsqueeze(1).to_broadcast([128, 8, 16]), ALU.mult, ["dtv", "Abc"], ["av"])
            CP("dve", ahi[:], av[:], ["av"], ["ahi"])
            TT("dve", alf[:], av[:], ahi[:], ALU.subtract, ["av", "ahi"], ["alf"])
            CP("dve", alo[:], alf[:], ["alf"], ["alo"])

            for hf in range(2):
                hsl = slice(hf * 512, (hf + 1) * 512)
                for ctg in range(4 if (full or (b == npre - 1 and hf == 1)) else 3):
                    wv, wk = load_w(win_b, 3072 + ctg * 512, 512, 0, 8, "win_b")
                    for c4 in range(4):
                        ct = ctg * 4 + c4
                        pp, ppk = PS()
                        for kt in range(8):
                            MM(pp[:], wv[:, kt, c4 * 128:(c4 + 1) * 128], hn[:, kt, hsl], kt == 0, kt == 7,
                               ["hn", wk], [ppk])
                        pr = pre[ct % 2]
                        prk = "pre%d" % (ct % 2)
                        dgv = dg[ct % 2]
                        dgk = "dg%d" % (ct % 2)
                        CP("act", pr[:, 3:515], pp[:], [ppk], [prk])
                        CP("pool", pr[:, 0:3], halo[:, ct, :], ["halo"], [prk])
                        for k in range(4):
                            TS("dve", dgv[:, k, :], ident_f[:], cw[:, ct, k:k + 1], None, ALU.mult, None,
                               ["ident_f", "cw"], [dgk])
                        pc, pck = PS()
                        for k in range(4):
                            MM(pc[:], dgv[:, k, :], pr[:, k:k + 512], k == 0, k == 3, [dgk, prk], [pck])
                        ACT(xc[:, ct, :], pc[:], AF.Silu, [pck, "cbv"], ["xc"], bias=cbv[:, ct:ct + 1])
                        CP("pool", halo[:, ct, :], pr[:, 512:515], [prk], ["halo"])
                for cb in range(2 if full else 0):
                    wv, wk = load_w(win_b, 2048 + cb * 512, 512, 0, 8, "win_b")
                    for c in range(4):
                        pz, pzk = PS()
                        for kt in range(8):
                            MM(pz[:], hn[:, kt, hf * 512 + c * 128: hf * 512 + (c + 1) * 128], wv[:, kt, :],
                               kt == 0, kt == 7, ["hn", wk], [pzk])
                        ACT(zsb[:, c, cb * 512:(cb + 1) * 512], pz[:], AF.Silu, [pzk], ["zsb"])
                for c in range(4):
                    cg = hf * 4 + c
                    tok = slice(c * 128, (c + 1) * 128)
                    a_c = av[:, cg, :]
                    dt_c = dtv[:, cg, :]
                    pa, pak = PS()
                    MM(pa[:, 0:16], tri_f[:], a_c, True, True, ["tri_f", "av"], [pak])
                    MM(pa[:, 16:32], ones_f[:], a_c, True, True, ["ones_f", "av"], [pak])
                    CP("dve", acs[:], pa[:, 0:16], [pak], ["acs"])
                    TT("dve", t16[:], pa[:, 16:32], acs[:], ALU.subtract, [pak, "acs"], ["t16"])
                    ACT(dec[:], t16[:], AF.Exp, ["t16"], ["dec"])
                    if full:
                        ACT(el[:], acs[:], AF.Exp, ["acs"], ["el"])
                    ACT(etot[:], pa[:, 16:32], AF.Exp, [pak], ["etot"])
                    px, pxk = PS()
                    pxb = px.bitcast(BF16)
                    for ct in range(8):
                        TR(pxb[:, ct * 128:(ct + 1) * 128], xc[:, ct, tok], ["xc"], [pxk])
                    pxv = pxb[:].rearrange("p (h q) -> p h q", h=16)
                    TT("dve", xd, pxv, dt_c.unsqueeze(2).to_broadcast([128, 16, 64]), ALU.mult, [pxk, "dtv"], ["xd"])
                    if full:
                        TT("dve", xD, pxv, Dhb[:].unsqueeze(2).to_broadcast([128, 16, 64]), ALU.mult, [pxk, "Dhb"], ["xD"])
                    TT("pool", xdd, xd, dec[:].unsqueeze(2).to_broadcast([128, 16, 64]), ALU.mult, ["xd", "dec"], ["xdd"])
                    pb_, pbk = PS()
                    pbv = pb_.bitcast(BF16)[:, 0:512].rearrange("p (g n) -> p g n", g=4)
                    for g in range(4):
                        TR(pbv[:, g, :], xc[:, 8 + g, tok], ["xc"], [pbk])
                    CP("act", Bc, pbv, [pbk], ["Bc"])
                    if full:
                        pg_, pgk = PS()
                        pgv = pg_[:].rearrange("p (g l) -> p g l", g=4)
                        for g in range(4):
                            MM(pgv[:, g, :], xc[:, 8 + g, tok], xc[:, 12 + g, tok], True, True, ["xc"], [pgk])
                        TT("dve", GTm, pgv, tri_f[:].unsqueeze(1).to_broadcast([128, 4, 128]), ALU.mult,
                           [pgk, "tri_f"], ["GTm"])
                        py = [PS(), PS()]

                        def emit_abc(g):
                            abc = abc_l[g % 2]
                            kab = "abc%d" % (g % 2)
                            CP("act", abc[:, 0, :, :], ahi[:, cg, 4 * g:4 * g + 4].unsqueeze(2).to_broadcast([128, 4, 128]),
                               ["ahi"], [kab])
                            CP("act", abc[:, 1, :, :], alo[:, cg, 4 * g:4 * g + 4].unsqueeze(2).to_broadcast([128, 4, 128]),
                               ["alo"], [kab])

                        emit_abc(0)
                        for g in range(4):
                            abc, dm, Ee, Mh = abc_l[g % 2], dm_l[g % 2], Ee_l[g % 2], Mh_l[g % 2]
                            kab, kdm, kEe, kMh = "abc%d" % (g % 2), "dm%d" % (g % 2), "Ee%d" % (g % 2), "Mh%d" % (g % 2)
                            pdd, pddk = PS()
                            pdv = pdd[:].rearrange("p (h l) -> p h l", h=4)
                            for hh in range(4):
                                MM(pdv[:, hh, :], abc[:, 0, hh, :], tri_b[:], True, False, [kab, "tri_b"], [pddk])
                                MM(pdv[:, hh, :], abc[:, 1, hh, :], tri_b[:], False, True, [kab, "tri_b"], [pddk])
                            if g < 3:
                                emit_abc(g + 1)
                            for hh in range(4):
                                h = 4 * g + hh
                                TS("dve", dm[:, hh, :], pdv[:, hh, :], acs[:, h:h + 1], 0.0, ALU.subtract, ALU.min,
                                   [pddk, "acs"], [kdm])
                            ACT(Ee, dm, AF.Exp, [kdm], [kEe])
                            TT("pool", Mh, Ee, GTm[:, g, :].unsqueeze(1).to_broadcast([128, 4, 128]), ALU.mult,
                               [kEe, "GTm"], [kMh])
                            for hh in range(4):
                                h = 4 * g + hh
                                bank, bk = py[h // 8]
                                col = (h % 8) * 64
                                MM(bank[:, col:col + 64], Mh[:, hh, :], xd[:, h, :], True, False, [kMh, "xd"], [bk])
                                MM(bank[:, col:col + 64], ident[:], xD[:, h, :], False, True, ["ident", "xD"], [bk])
                        po = [PS(), PS()]
                        for g in range(4):
                            bank, bk = po[g // 2]
                            col = (g % 2) * 256
                            MM(bank[:, col:col + 256], xc[:, 12 + g, tok], STb[:, g * 256:(g + 1) * 256], True, True,
                               ["xc", "STb"], [bk])
                        for hb in range(2):
                            ysl = yv[:, hb * 512:(hb + 1) * 512]
                            y3 = ysl.rearrange("p (h q) -> p h q", h=8)
                            TT("dve", y3, po[hb][0][:].rearrange("p (h q) -> p h q", h=8),
                               el[:, hb * 8:(hb + 1) * 8].unsqueeze(2).to_broadcast([128, 8, 64]), ALU.mult,
                               [po[hb][1], "el"], ["yv"])
                            TT("dve", ysl, ysl, py[hb][0][:], ALU.add, ["yv", py[hb][1]], ["yv"])
                            TT("pool", ysl, ysl, zsb[:, c, hb * 512:(hb + 1) * 512], ALU.mult, ["yv", "zsb"], ["yv"])
                        for grp in range(4):
                            ACT(sq[:, 0:256], yv[:, grp * 256:(grp + 1) * 256], AF.Square, ["yv"], ["sq", "ssg"],
                                accum=ssg[:, grp:grp + 1])
                        ACT(rg[:], ssg[:], AF.Ln, ["ssg"], ["rg"], scale=1.0 / 256, bias=EPS)
                        ACT(rg[:], rg[:], AF.Exp, ["rg"], ["rg"], scale=-0.5)
                        TT("dve", gn.rearrange("p (g q) -> p g q", g=4), yv.rearrange("p (g q) -> p g q", g=4),
                           rg[:].unsqueeze(2).to_broadcast([128, 4, 256]), ALU.mult, ["yv", "rg"], ["gn"])
                        ptt, ptk = PS()
                        ptv = ptt.bitcast(BF16)[:].rearrange("p (k l) -> p k l", k=8)
                        for kt in range(8):
                            TR(ptv[:, kt, :], gn[:, kt * 128:(kt + 1) * 128], ["gn"], [ptk])
                        for kt in range(8):
                            TS("dve", mix[:, 8 + kt, :, 16 * cg:16 * cg + 16],
                               ptv[:, kt, :].rearrange("p (j t) -> p t j", t=8), snw[:, kt:kt + 1], None,
                               ALU.mult, None, [ptk, "snw"], ["mixB"])
                    pst = [PS(), PS()]
                    for g in range(4):
                        bank, bk = pst[g // 2]
                        col = (g % 2) * 256
                        MM(bank[:, col:col + 256], Bc[:, g, :],
                           xdd[:, 4 * g:4 * g + 4, :].rearrange("p h q -> p (h q)"), True, True, ["Bc", "xdd"], [bk])
                    for hb in range(2):
                        s3 = STf[:, hb * 512:(hb + 1) * 512].rearrange("p (h q) -> p h q", h=8)
                        TT("dve", s3, s3, etot[:, hb * 8:(hb + 1) * 8].unsqueeze(2).to_broadcast([128, 8, 64]),
                           ALU.mult, ["STf", "etot"], ["STf"])
                        TT("dve", STf[:, hb * 512:(hb + 1) * 512], STf[:, hb * 512:(hb + 1) * 512], pst[hb][0][:],
                           ALU.add, ["STf", pst[hb][1]], ["STf"])
                    CP("act", STb[:], STf[:], ["STf"], ["STb"])

            if full:
                DMA("sp", xt[:], x_d[t0:t0 + BLK, :].rearrange("(j t) d -> j t d", t=8), [], ["xt"])
                DMA("sp", pt, p_d[o0:o0 + BLK, :].rearrange("(j t) d -> j t d", t=8), [], ["pt"])
                for cb in range(2):
                    cs = slice(cb * 512, (cb + 1) * 512)
                    wa, wak = load_w(wout_b, cb * 512, 512, 0, 8, "wout_b")
                    wbb, wbk = load_w(wout_b, cb * 512, 512, 8, 8, "wout_b")
                    for t in range(8):
                        po_, pok = PS()
                        for kt in range(16):
                            wv, wk = (wa, wak) if kt < 8 else (wbb, wbk)
                            MM(po_[:], mix[:, kt, t, :], wv[:, kt % 8, :], kt == 0, kt == 15,
                               ["mixA" if kt < 8 else "mixB", wk], [pok])
                        TT("dve", xt[:, t, cs], xt[:, t, cs], po_[:], ALU.add, ["xt", pok], ["xt"])

                rms_rr("xt")
                for t in range(8):
                    TS("pool" if t % 2 else "dve", hs[:, t, :], xt[:, t, :], rr[:, t:t + 1], None, ALU.mult, None,
                       ["xt", "rr"], ["mixA"])
                hr4 = hn[:].rearrange("p k (t j) -> p k t j", t=8)
                for kt in range(8):
                    pt_, pk = PS()
                    pv = pt_.bitcast(BF16)[:].rearrange("p (t j) -> p t j", t=8)
                    for t in range(8):
                        TR(pv[:, t, :], hs[:, t, kt * 128:(kt + 1) * 128], ["mixA"], [pk])
                    if kt % 2:
                        ACT(hr4[:, kt, :, :], pv, AF.Copy, [pk, "plw"], ["hn"], scale=plw[:, kt:kt + 1])
                    else:
                        TS("dve", hr4[:, kt, :, :], pv, plw[:, kt:kt + 1], None, ALU.mult, None, [pk, "plw"], ["hn"])
                CP("pool", pbf, pt, ["pt"], ["pbf"])
                for k2 in range(2):
                    pt_, pk = PS()
                    pv = pt_.bitcast(BF16)[:].rearrange("p (t j) -> p t j", t=8)
                    for t in range(8):
                        TR(pv[:, t, :], pbf[:, t, k2 * 128:(k2 + 1) * 128], ["pbf"], [pk])
                    CP("act", p_fm[:, k2, :, :], pv, [pk], ["p_fm"])
                for cb in range(2):
                    cs = slice(cb * 512, (cb + 1) * 512)
                    wgv, wgk = load_w(wgate_b, cb * 512, 512, 0, 8, "wgate_b")
                    wpv, wpk = load_w(wple_b, cb * 512, 512, 0, 2, "wple_b")
                    for t in range(8):
                        pg, pgk = PS()
                        for kt in range(8):
                            MM(pg[:], hr4[:, kt, t, :], wgv[:, kt, :], kt == 0, kt == 7, ["hn", wgk], [pgk])
                        ACT(gsig, pg[:], AF.Sigmoid, [pgk], ["gsig"])
                        pp, ppk = PS()
                        for k2 in range(2):
                            MM(pp[:], p_fm[:, k2, t, :], wpv[:, k2, :], k2 == 0, k2 == 1, ["p_fm", wpk], [ppk])
                        TT("dve", tmp2, pp[:], gsig, ALU.mult, [ppk, "gsig"], ["tmp2"])
                        TT("pool", xt[:, t, cs], xt[:, t, cs], tmp2, ALU.add, ["xt", "tmp2"], ["xt"])
                rms_rr("xt")
                mixo = mix.bitcast(F32)[:].rearrange("p k t j -> p (k t j)").rearrange("p (t d) -> p t d", t=8)
                for t in range(8):
                    STT(mixo[:, t, :], xt[:, t, :], rr[:, t:t + 1], fwb[:], ALU.mult, ALU.mult, ["xt", "rr", "fwb"],
                        ["mixA", "mixB"])
                tok_ = DMA("sp", out_d[o0:o0 + BLK, :].rearrange("(j t) d -> j t d", t=8), mixo, ["mixA", "mixB"], [])
                out_toks.append(tok_)
        em.final_wait("sp", out_toks)
        em.emit(nc, sems)
    return nc


_NC_CACHE = {}


def kernel(_cfg=None, **inputs):
    f = lambda a: np.ascontiguousarray(a, dtype=np.float32)
    key = repr(sorted((_cfg or {}).items()))
    if key not in _NC_CACHE:
        _NC_CACHE[key] = build_program(_cfg)
    nc = _NC_CACHE[key]
    tri = np.triu(np.ones((128, 128), dtype=np.float32))
    shared = {
        "c_ident": np.eye(128, dtype=np.float32),
        "c_tri": tri,
        "c_ones": np.ones((128, 128), dtype=np.float32),
        "final_norm_w": f(inputs["final_norm_w"]).reshape(1, D),
        "ple_norm_w": f(inputs["ple_norm_w"]).reshape(D),
        "norm_w": f(inputs["norm_w"]).reshape(D),
        "ssd_norm_w": f(inputs["ssd_norm_w"]).reshape(D),
        "w_in": f(inputs["w_in"]).reshape(D, DIN),
        "w_out": f(inputs["w_out"]).reshape(2 * D, D),
        "w_ple_gate": f(inputs["w_ple_gate"]).reshape(D, D),
        "w_ple_proj": f(inputs["w_ple_proj"]).reshape(PLE, D),
        "conv_w": f(inputs["conv_w"]).reshape(4, 2048),
        "conv_b": f(inputs["conv_b"]).reshape(2048),
        "dt_bias": f(inputs["dt_bias"]).reshape(1, 16),
        "A_log": f(inputs["A_log"]).reshape(1, 16),
        "ssd_D": f(inputs["ssd_D"]).reshape(1, 16),
    }
    if (_cfg or {}).get("s5", True):
        m8 = np.zeros((128, 128), dtype=np.float32)
        for s_lo in range(8):
            for t_lo in range(s_lo, 8):
                m8[s_lo * 16:(s_lo + 1) * 16, t_lo * 16:(t_lo + 1) * 16] = 1.0
        rowm = np.zeros((128, 2), dtype=np.float32)
        rowm[:64, 0] = 1.0
        rowm[64:, 1] = 1.0
        shared.update({
            "c_mask8": m8,
            "c_rowmask": rowm,
            "s5_A_re": f(inputs["s5_A_re"]).reshape(64, 64),
            "s5_A_im": f(inputs["s5_A_im"]).reshape(64, 64),
            "s5_log_dt": f(inputs["s5_log_dt"]).reshape(64),
            "s5_B_re": f(inputs["s5_B_re"]).reshape(64, 64, 16),
            "s5_B_im": f(inputs["s5_B_im"]).reshape(64, 64, 16),
            "s5_C_re": f(inputs["s5_C_re"]).reshape(64, 16, 64),
            "s5_C_im": f(inputs["s5_C_im"]).reshape(64, 16, 64),
            "s5_D": f(inputs["s5_D"]).reshape(D),
            "s5_w_glu": f(inputs["s5_w_glu"]).reshape(D, D),
            "s5_b_glu": f(inputs["s5_b_glu"]).reshape(1, D),
        })
    x = f(inputs["x"])
    p = f(inputs["p"])
    cfg_ = _cfg or {}
    nblk_ = cfg_.get("nblk", NBLK)
    npre_ = cfg_.get("npre", 4)
    nown_ = nblk_ - npre_
    HALF = L // 2
    in_maps = []
    for c in range(8):
        b, half = divmod(c, 2)
        if npre_ == 4 and nblk_ == 8:
            xin = np.concatenate([x[b, 0:HALF], x[b, half * HALF:(half + 1) * HALF]], axis=0)
            pin = p[0, b, half * HALF:(half + 1) * HALF]
            fl = float(half)
        else:
            xin = x[b]
            pin = p[0, b, npre_ * BLK:nblk_ * BLK]
            fl = 1.0
        m = {"x": np.ascontiguousarray(xin), "p": np.ascontiguousarray(pin),
             "flag": np.full((128, 1), fl, dtype=np.float32)}
        m.update(shared)
        in_maps.append(m)
    res = run_bass_kernel_spmd(nc, in_maps, core_ids=list(range(8)))
    if npre_ == 4 and nblk_ == 8:
        out = np.empty((NBATCH, L, D), dtype=np.float32)
        for c in range(8):
            b, half = divmod(c, 2)
            out[b, half * HALF:(half + 1) * HALF] = res.results[c]["out"]
        return out
    out = np.zeros((NBATCH, L, D), dtype=np.float32)
    for b in range(NBATCH):
        out[b, npre_ * BLK:nblk_ * BLK] = res.results[2 * b]["out"]
    return out
```

```python
import math
import numpy as np
import concourse.bass as bass
import concourse.mybir as mybir
from concourse.bass_utils import run_bass_kernel_spmd

F32 = mybir.dt.float32
BF16 = mybir.dt.bfloat16
ALU = mybir.AluOpType
AF = mybir.ActivationFunctionType
AX = mybir.AxisListType

D = 1024
L = 8192
NBATCH = 4
BLK = 1024
NBLK = L // BLK
DIN = 5136
PLE = 256
EPS = 1e-6


class Em:
    ENGS = ("pe", "act", "dve", "pool", "sp")

    def __init__(self, n_dma_sems=12, strict_same=True):
        self.ops = {e: [] for e in self.ENGS}
        self.count = {e: 0 for e in self.ENGS}
        self.waited = {e: {} for e in self.ENGS}
        self.last_w = {}
        self.readers = {}
        self.n_dma = n_dma_sems
        self.dma_cnt = [0] * n_dma_sems
        self.dma_rr = 0
        self.sw_rr = 0
        self.strict_same = strict_same
        self.alias = {}
        self.enabled = True
        self.regions = []

    def region(self, key, arena, lo, hi):
        for (k2, a2, lo2, hi2) in self.regions:
            if k2 == key:
                return
        for (k2, a2, lo2, hi2) in self.regions:
            if a2 == arena and lo < hi2 and lo2 < hi:
                self.alias.setdefault(key, []).append(k2)
                self.alias.setdefault(k2, []).append(key)
        self.regions.append((key, arena, lo, hi))

    def _exp(self, keys):
        out = []
        for k in keys:
            out.append(k)
            out.extend(self.alias.get(k, ()))
        return out

    def _deps(self, eng, reads, writes):
        need = {}

        def add(tok):
            if tok is None:
                return
            sk, v = tok
            if sk == eng and not (self.strict_same and eng != "pe"):
                return
            if need.get(sk, 0) < v:
                need[sk] = v

        def add_other(tok):
            if tok is not None and tok[0] != eng:
                add(tok)

        for k in self._exp(reads):
            add(self.last_w.get(k))
        for k in self._exp(writes):
            add_other(self.last_w.get(k))
            for r in self.readers.get(k, ()):
                add_other(r)
        out = []
        for sk, v in need.items():
            if self.waited[eng].get(sk, 0) < v:
                self.waited[eng][sk] = v
                out.append((sk, v))
        return out

    def _commit(self, tok, reads, writes):
        for k in reads:
            self.readers.setdefault(k, []).append(tok)
        for k in writes:
            self.last_w[k] = tok
            self.readers[k] = []

    def op(self, eng, fn, reads=(), writes=()):
        if not self.enabled:
            return None
        waits = self._deps(eng, reads, writes)
        self.count[eng] += 1
        tok = (eng, self.count[eng])
        self.ops[eng].append((waits, fn, (eng, 1)))
        self._commit(tok, reads, writes)

    def dma(self, eng, fn, reads=(), writes=()):
        if not self.enabled:
            return None
        waits = self._deps(eng, reads, writes)
        if eng == "pool":
            i = self.n_dma - 1 - (self.sw_rr % 2)
            self.sw_rr += 1
        else:
            i = self.dma_rr
            self.dma_rr = (self.dma_rr + 1) % (self.n_dma - 2)
        sk = "d%d" % i
        if self.dma_cnt[i] > 0 and self.waited[eng].get(sk, 0) < self.dma_cnt[i]:
            self.waited[eng][sk] = self.dma_cnt[i]
            waits.append((sk, self.dma_cnt[i]))
        self.dma_cnt[i] += 16
        tok = (sk, self.dma_cnt[i])
        self.ops[eng].append((waits, fn, (sk, 16)))
        self._commit(tok, reads, writes)
        return tok

    def final_wait(self, eng, toks):
        waits = []
        for sk, v in toks:
            if self.waited[eng].get(sk, 0) < v:
                self.waited[eng][sk] = v
                waits.append((sk, v))
        self.ops[eng].append((waits, None, None))

    def emit(self, nc, sems):
        handles = {"pe": None, "act": None, "dve": None, "pool": None, "sp": None}
        with nc.Block() as block:
            def run(eng_name):
                def _f(e):
                    for waits, fn, inc in self.ops[eng_name]:
                        for sk, v in waits:
                            e.wait_ge(sems[sk], v)
                        if fn is not None:
                            ins = fn(e)
                            ins.then_inc(sems[inc[0]], inc[1])
                return _f
            block.tensor(run("pe"))
            block.scalar(run("act"))
            block.vector(run("dve"))
            block.gpsimd(run("pool"))
            block.sync(run("sp"))


def build_program(cfg=None):
    cfg = dict(cfg or {})
    nblk = cfg.get("nblk", NBLK)
    do_s5 = cfg.get("s5", True)
    lvl = cfg.get("lvl", 9)
    plvl = cfg.get("plvl", 9)
    npre = cfg.get("npre", 4)
    nown = nblk - npre
    nc = bass.Bass("TRN2", target_bir_lowering=False)
    em = Em(strict_same=cfg.get("strict", True))
    import contextlib
    st = contextlib.ExitStack()

    def din(name, shape):
        return nc.dram_tensor(name, list(shape), F32, kind="ExternalInput").ap()

    def dscratch(name, shape, dt=BF16):
        return nc.dram_tensor(name, list(shape), dt, kind="Internal").ap()

    def sb(name, shape, dt=F32):
        return st.enter_context(nc.sbuf_tensor(name, list(shape), dt))

    x_d = din("x", [L, D])
    p_d = din("p", [nown * BLK, PLE])
    out_d = nc.dram_tensor("out", [nown * BLK, D], F32, kind="ExternalOutput").ap()
    flag_d = din("flag", [128, 1])
    fw_d = din("final_norm_w", [1, D])
    plw_d = din("ple_norm_w", [D])
    nw_d = din("norm_w", [D])
    snw_d = din("ssd_norm_w", [D])
    win_d = din("w_in", [D, DIN])
    wout_d = din("w_out", [2 * D, D])
    wgate_d = din("w_ple_gate", [D, D])
    wple_d = din("w_ple_proj", [PLE, D])
    convw_d = din("conv_w", [4, 2048])
    convb_d = din("conv_b", [2048])
    dtb_d = din("dt_bias", [1, 16])
    alog_d = din("A_log", [1, 16])
    sD_d = din("ssd_D", [1, 16])
    ident_d = din("c_ident", [128, 128])
    tri_d = din("c_tri", [128, 128])
    ones_d = din("c_ones", [128, 128])
    if do_s5:
        are_d = din("s5_A_re", [64, 64])
        aim_d = din("s5_A_im", [64, 64])
        ldt_d = din("s5_log_dt", [64])
        bre_d = din("s5_B_re", [64, 64, 16])
        bim_d = din("s5_B_im", [64, 64, 16])
        cre_d = din("s5_C_re", [64, 16, 64])
        cim_d = din("s5_C_im", [64, 16, 64])
        sD5_d = din("s5_D", [D])
        wglu_d = din("s5_w_glu", [D, D])
        bglu_d = din("s5_b_glu", [1, D])
        mask8_d = din("c_mask8", [128, 128])
        rowm_d = din("c_rowmask", [128, 2])
        wglu_b = dscratch("wglu_b", [D, D])

    win_b = dscratch("win_b", [D, DIN])
    wout_b = dscratch("wout_b", [2 * D, D])
    wgate_b = dscratch("wgate_b", [D, D])
    wple_b = dscratch("wple_b", [PLE, D])

    with st:
        sems = {}
        for e in Em.ENGS:
            sems[e] = st.enter_context(nc.semaphore("s_" + e))
        for i in range(em.n_dma):
            sems["d%d" % i] = st.enter_context(nc.semaphore("s_d%d" % i))

        def MM(out, lhsT, rhs, start, stop, r, w):
            em.op("pe", lambda e: e.matmul(out, lhsT, rhs, start=start, stop=stop), r, w)

        def ACT(out, in_, func, r, w, scale=1.0, bias=0.0, accum=None):
            if accum is None:
                em.op("act", lambda e: e.activation(out=out, in_=in_, func=func, scale=scale, bias=bias), r, w)
            else:
                em.op("act", lambda e: e.activation(out=out, in_=in_, func=func, scale=scale, bias=bias,
                                                    accum_out=accum), r, w)

        def TT(eng, out, in0, in1, op, r, w):
            em.op(eng, lambda e: e.tensor_tensor(out=out, in0=in0, in1=in1, op=op), r, w)

        def TS(eng, out, in0, s1, s2, op0, op1, r, w):
            if s2 is None:
                em.op(eng, lambda e: e.tensor_scalar(out=out, in0=in0, scalar1=s1, scalar2=None, op0=op0), r, w)
            else:
                em.op(eng, lambda e: e.tensor_scalar(out=out, in0=in0, scalar1=s1, scalar2=s2, op0=op0, op1=op1), r, w)

        def STT(out, in0, scalar, in1, op0, op1, r, w):
            em.op("dve", lambda e: e.scalar_tensor_tensor(out=out, in0=in0, scalar=scalar, in1=in1,
                                                          op0=op0, op1=op1), r, w)

        def CP(eng, out, in_, r, w):
            if eng == "act":
                em.op("act", lambda e: e.copy(out=out, in_=in_), r, w)
            else:
                em.op(eng, lambda e: e.tensor_copy(out=out, in_=in_), r, w)

        def MSET(eng, out, val, w):
            em.op(eng, lambda e: e.memset(out, val), [], w)

        def RECIP(out, in_, r, w):
            em.op("dve", lambda e: e.reciprocal(out=out, in_=in_), r, w)

        def DMA(eng, out, in_, r, w):
            return em.dma(eng, lambda e: e.dma_start(out=out, in_=in_), r, w)

        def DMAS(eng, out, in_, r, w):
            return em.dma(eng, lambda e: e.dma_start(out=out, in_=in_, allow_slow_non_contiguous=True), r, w)

        psb = [st.enter_context(nc.psum_tensor("ps%d" % i, [128, 512], F32)) for i in range(8)]
        ps_i = [0]

        def PS():
            i = ps_i[0]
            ps_i[0] = (i + 1) % 8
            return psb[i], "ps%d" % i

        RBYTES = cfg.get("rbytes", 30720)
        R = sb("R", [128, RBYTES // 4])
        Rb = R.bitcast(BF16)

        def rview(key, off, shape, dt):
            n = 1
            for s_ in shape[1:]:
                n *= s_
            if dt == F32:
                assert off % 4 == 0
                ap = R[:, off // 4: off // 4 + n]
                nb = n * 4
            else:
                assert off % 2 == 0
                ap = Rb[:, off // 2: off // 2 + n]
                nb = n * 2
            assert off + nb <= RBYTES, (key, off, nb)
            if len(shape) == 3:
                ap = ap.rearrange("p (a b) -> p a b", a=shape[1])
            elif len(shape) == 4:
                ap = ap.rearrange("p (a b c) -> p a b c", a=shape[1], b=shape[2])
            em.region(key, "R", off, off + nb)
            return ap

        ident_f = sb("ident_f", [128, 128])
        tri_f = sb("tri_f", [128, 128])
        ones_f = sb("ones_f", [128, 128])
        ident = sb("ident", [128, 128], BF16)
        DMA("sp", ident_f[:], ident_d, [], ["ident_f"])
        DMA("sp", tri_f[:], tri_d, [], ["tri_f"])
        DMA("sp", ones_f[:], ones_d, [], ["ones_f"])
        CP("dve", ident[:], ident_f[:], ["ident_f"], ["ident"])

        def TR(out, in_, r, w):
            em.op("pe", lambda e: e.transpose(out, in_, ident[:]), list(r) + ["ident"], w)

        fwb = sb("fwb", [128, D])
        DMA("sp", fwb[:], fw_d.partition_broadcast(128), [], ["fwb"])
        plw = sb("plw", [128, 8])
        nw = sb("nw", [128, 8])
        snw = sb("snw", [128, 8])
        DMAS("sp", plw[:], plw_d.rearrange("(k p) -> p k", p=128), [], ["plw"])
        DMAS("sp", nw[:], nw_d.rearrange("(k p) -> p k", p=128), [], ["nw"])
        DMAS("sp", snw[:], snw_d.rearrange("(k p) -> p k", p=128), [], ["snw"])
        cw = sb("cw", [128, 16, 4])
        cbv = sb("cbv", [128, 16])
        for k_ in range(4):
            DMAS("sp", cw[:, :, k_], convw_d[k_, :].rearrange("(c p) -> p c", p=128), [], ["cw"])
        DMAS("sp", cbv[:], convb_d.rearrange("(c p) -> p c", p=128), [], ["cbv"])
        dtb = sb("dtb", [128, 16])
        Abc = sb("Abc", [128, 16])
        Dhb = sb("Dhb", [128, 16])
        DMA("sp", dtb[:], dtb_d.partition_broadcast(128), [], ["dtb"])
        DMA("sp", Abc[:], alog_d.partition_broadcast(128), [], ["Abc"])
        DMA("sp", Dhb[:], sD_d.partition_broadcast(128), [], ["Dhb"])
        ACT(Abc[:], Abc[:], AF.Exp, ["Abc"], ["Abc"])
        TS("dve", Abc[:], Abc[:], -1.0, None, ALU.mult, None, ["Abc"], ["Abc"])

        SW = 2568
        stg = [rview("stg0", 0, [128, SW], BF16), rview("stg1", SW * 2, [128, SW], BF16)]
        stg_i = [0]

        def precast(src, dst, rows, cols, key):
            rpp = rows // 128
            sv = src.rearrange("(p r) c -> p r c", p=128)
            dv = dst.rearrange("(p r) c -> p r c", p=128)
            if cols <= SW:
                rstep = max(1, SW // cols)
                for r0 in range(0, rpp, rstep):
                    r1 = min(rpp, r0 + rstep)
                    i = stg_i[0]
                    stg_i[0] ^= 1
                    n = (r1 - r0) * cols
                    sview = stg[i][:, 0:n].rearrange("p (r c) -> p r c", c=cols)
                    DMA("pool", sview, sv[:, r0:r1, :], [], ["stg%d" % i])
                    DMA("sp", dv[:, r0:r1, :], sview, ["stg%d" % i], [key])
            else:
                for r0 in range(rpp):
                    for c0 in range(0, cols, SW):
                        c1 = min(cols, c0 + SW)
                        i = stg_i[0]
                        stg_i[0] ^= 1
                        sview = stg[i][:, 0:c1 - c0]
                        DMA("pool", sview, sv[:, r0, c0:c1], [], ["stg%d" % i])
                        DMA("sp", dv[:, r0, c0:c1], sview, ["stg%d" % i], [key])

        precast(win_d, win_b, D, DIN, "win_b")
        precast(wout_d, wout_b, 2 * D, D, "wout_b")
        precast(wgate_d, wgate_b, D, D, "wgate_b")
        precast(wple_d, wple_b, PLE, D, "wple_b")
        if do_s5:
            precast(wglu_d, wglu_b, D, D, "wglu_b")

        xt = sb("xt", [128, 8, D])
        em.region("xt", "XT", 0, 32768)
        xt_f = xt[:].rearrange("p t d -> p (t d)")
        xt_b = xt.bitcast(BF16)[:].rearrange("p t d -> p (t d)")

        def xview(key, off, shape, dt):
            n = 1
            for s_ in shape[1:]:
                n *= s_
            if dt == F32:
                ap = xt_f[:, off // 4: off // 4 + n]
                nb = n * 4
            else:
                ap = xt_b[:, off // 2: off // 2 + n]
                nb = n * 2
            assert off + nb <= 32768
            if len(shape) == 3:
                ap = ap.rearrange("p (a b) -> p a b", a=shape[1])
            elif len(shape) == 4:
                ap = ap.rearrange("p (a b c) -> p a b c", a=shape[1], b=shape[2])
            em.region(key, "XT", off, off + nb)
            return ap
        sq = sb("sq", [128, D], BF16)
        ss = sb("ss", [128, 8])
        rr = sb("rr", [128, 8])
        mix = sb("mix", [128, 16, 8, 128], BF16)
        mixA = mix[:, 0:8, :, :]
        hs = mix[:, 0:8, :, :].rearrange("p k t j -> p (k t j)").rearrange("p (t d) -> p t d", t=8)
        hn = sb("hn", [128, 8, BLK], BF16)
        NWB = cfg.get("nwb", 3)
        wbs = [sb("wb%d" % i, [128, 8, 512], BF16) for i in range(NWB)]
        wb_i = [0]
        wdt = sb("wdt", [128, 8, 16], BF16)
        DMAS("sp", wdt[:], win_b[:, 5120:5136].rearrange("(k p) c -> p k c", p=128), ["win_b"], ["wdt"])
        STf = sb("STf", [128, D])
        STb = sb("STb", [128, D], BF16)
        halo = sb("halo", [128, 16, 3], BF16)
        MSET("dve", STf[:], 0.0, ["STf"])
        MSET("dve", STb[:], 0.0, ["STb"])
        MSET("dve", halo[:], 0.0, ["halo"])
        dtr = sb("dtr", [128, 8, 16])
        dtv = sb("dtv", [128, 8, 16])
        av = sb("av", [128, 8, 16])
        acs = sb("acs", [128, 16])
        t16 = sb("t16", [128, 16])
        dec = sb("dec", [128, 16])
        el = sb("el", [128, 16])
        etot = sb("etot", [128, 16])
        ssg = sb("ssg", [128, 4])
        tri_b = sb("tri_b", [128, 128], BF16)
        CP("dve", tri_b[:], tri_f[:], ["tri_f"], ["tri_b"])
        ahi = sb("ahi", [128, 8, 16], BF16)
        alo = sb("alo", [128, 8, 16], BF16)
        alf = sb("alf", [128, 8, 16])
        rg = sb("rg", [128, 4])
        flg = sb("flg", [128, 1])
        DMA("sp", flg[:], flag_d, [], ["flg"])

        dq = [0]

        def load_w(src, col0, ncols, k0, nk, key):
            i = wb_i[0]
            wb_i[0] = (i + 1) % NWB
            view = wbs[i][:, 0:nk, 0:ncols]
            eng = "sp"
            DMA(eng, view, src[k0 * 128:(k0 + nk) * 128, col0:col0 + ncols].rearrange("(k p) c -> p k c", p=128),
                [key], ["wb%d" % i])
            return view, "wb%d" % i


        hn4 = hn[:].rearrange("p k (j t) -> p k t j", t=8)
        u8 = mix[:, 8:16, :, :].rearrange("p k t j -> p (k t j)").rearrange("p (t d) -> p t d", t=8)
        u8g = mix[:, 8:16, :, :].rearrange("p k t j -> p (k t j)").rearrange("p (g t h) -> p g t h", g=64, t=8)
        if do_s5:
            Q0r = sb("Q0r", [128, 32, 128], BF16)
            Q0i = sb("Q0i", [128, 32, 128], BF16)
            R0a = sb("R0a", [128, 64, 128], BF16)
            P0r = sb("P0r", [128, 32, 128], BF16)
            NP0i = sb("NP0i", [128, 32, 128], BF16)
            A1 = sb("A1", [128, 2, 32])
            A2 = sb("A2", [128, 2, 32])
            s5c = sb("s5c", [128, 2, 32])
            rt1 = sb("rt1", [128, 2, 32])
            rt2 = sb("rt2", [128, 2, 32])
            bglu_bf = sb("bglu_bf", [1, D], BF16)
            ones_bf = sb("ones_bf", [1, 128], BF16)
            Dcol = rview("Dcol", 22528, [128, 64], F32)
            mask8 = rview("mask8", 22528 + 256, [128, 128], F32)
            rowm = sb("rowm", [128, 2])
            DMA("sp", rowm[:], rowm_d, [], ["rowm"])
            sl = rview("sl", 24576, [128, 19, 32], F32)
            l2 = sb("l2", [32, 2])
            MSET("dve", s5c[:], 0.0, ["s5c"])
            MSET("dve", ones_bf[:], 1.0, ["ones_bf"])
            bglu_f = xview("bglu_f", 28672, [128, D], F32)
            DMA("sp", bglu_f[0:1, :], bglu_d, [], ["bglu_f"])
            CP("dve", bglu_bf[:], bglu_f[0:1, :], ["bglu_f"], ["bglu_bf"])
            DMA("sp", mask8, mask8_d, [], ["mask8"])
            for s_ in range(8):
                DMAS("sp", Dcol[s_ * 16:(s_ + 1) * 16, :], sD5_d.rearrange("(g h) -> h g", h=16), [], ["Dcol"])
            XA = rview("XA", 0, [128, 3, 128], F32)
            POWr = rview("POWr", 1536, [128, 32, 24], F32)
            POWi = rview("POWi", 1536 + 3072, [128, 32, 24], F32)
            Bslr = rview("Bslr", 8192, [128, 32, 16], F32)
            Bsli = rview("Bsli", 8192 + 2048, [128, 32, 16], F32)
            Bbr = rview("Bbr", 8192 + 4096, [128, 32, 16], F32)
            Bbi = rview("Bbi", 8192 + 6144, [128, 32, 16], F32)
            Cslr = rview("Cslr", 16384, [128, 32, 16], F32)
            Csli = rview("Csli", 16384 + 2048, [128, 32, 16], F32)
            hnf = hn.bitcast(F32)[:].rearrange("p k c -> p (k c)")
            xtf = xt[:].rearrange("p t d -> p (t d)")
            T1 = xtf[:, 0:4096].rearrange("p (g s h) -> p g s h", g=32, s=8)
            T2 = xtf[:, 4096:8192].rearrange("p (g s h) -> p g s h", g=32, s=8)
            tmpR = xtf[:, 0:512].rearrange("p (g l) -> p g l", g=4)
            mixf = mix[:].rearrange("p k t j -> p (k t j)")
            Qstr = mixf[:, 0:4096].rearrange("p (g c) -> p g c", g=32)
            Qsti = mixf[:, 4096:8192].rearrange("p (g c) -> p g c", g=32)
            P0mr = mixf[:, 8192:12288].rearrange("p (g c) -> p g c", g=32)
            NP0mi = mixf[:, 12288:16384].rearrange("p (g c) -> p g c", g=32)

            def SL(i):
                return sl[:, i, :]
            SLK = ["sl"]
            em.enabled = plvl >= 1
            DMA("sp", XA[0:32, 0, :], are_d.rearrange("(gp g2) n -> gp (g2 n)", g2=2), [], ["XA"])
            DMA("sp", XA[0:32, 1, :], aim_d.rearrange("(gp g2) n -> gp (g2 n)", g2=2), [], ["XA"])
            DMAS("sp", l2[:], ldt_d.rearrange("(gp g2) -> gp g2", g2=2), [], ["l2"])
            CP("dve", XA[0:32, 2, :].rearrange("p (a n) -> p a n", a=2), l2[:].unsqueeze(2).to_broadcast([32, 2, 64]),
               ["l2"], ["XA"])
            pA, pAk = PS()
            for i in range(3):
                em.op("pe", lambda e, i=i: e.transpose(pA[:, i * 32:(i + 1) * 32], XA[0:32, i, :], ident_f[0:32, 0:32]),
                      ["XA", "ident_f"], [pAk])
            iAr, iAi, iLd, iDt, iArd, iAid, iZr, iZi, iT, iU, iV, iCr, iCi, iLr1, iNr, iNi, iDen, iPr, iPi = range(19)
            for i in range(3):
                CP("dve", SL(i), pA[:, i * 32:(i + 1) * 32], [pAk], SLK)
            ACT(SL(iDt), SL(iLd), AF.Exp, SLK, SLK)
            TT("dve", SL(iArd), SL(iAr), SL(iDt), ALU.mult, SLK, SLK)
            TT("dve", SL(iAid), SL(iAi), SL(iDt), ALU.mult, SLK, SLK)
            ACT(SL(iT), SL(iArd), AF.Exp, SLK, SLK, scale=1.0 / 32)
            ACT(SL(iU), SL(iAid), AF.Sin, SLK, SLK, scale=1.0 / 32)
            ACT(SL(iV), SL(iAid), AF.Sin, SLK, SLK, scale=1.0 / 32, bias=math.pi / 2)
            TT("dve", SL(iZr), SL(iT), SL(iV), ALU.mult, SLK, SLK)
            TT("dve", SL(iZi), SL(iT), SL(iU), ALU.mult, SLK, SLK)
            for _ in range(5):
                TT("dve", SL(iT), SL(iZr), SL(iZr), ALU.mult, SLK, SLK)
                TT("dve", SL(iU), SL(iZi), SL(iZi), ALU.mult, SLK, SLK)
                TT("dve", SL(iV), SL(iZr), SL(iZi), ALU.mult, SLK, SLK)
                TT("dve", SL(iZr), SL(iT), SL(iU), ALU.subtract, SLK, SLK)
                TS("dve", SL(iZi), SL(iV), 2.0, None, ALU.mult, None, SLK, SLK)
            PK = ["POWr", "POWi"]
            CP("dve", SL(iPr), SL(iZr), SLK, SLK)
            CP("dve", SL(iPi), SL(iZi), SLK, SLK)
            MSET("dve", POWr[:, :, 15], 1.0, PK)
            MSET("dve", POWi[:, :, 15], 0.0, PK)
            MSET("dve", POWr[:, :, 23], 1.0, PK)
            MSET("dve", POWi[:, :, 23], 0.0, PK)
            for k in range(1, 9):
                if k > 1:
                    TT("dve", SL(iT), SL(iPr), SL(iZr), ALU.mult, SLK, SLK)
                    TT("dve", SL(iU), SL(iPi), SL(iZi), ALU.mult, SLK, SLK)
                    TT("dve", SL(iV), SL(iPr), SL(iZi), ALU.mult, SLK, SLK)
                    TT("dve", SL(iPr), SL(iT), SL(iU), ALU.subtract, SLK, SLK)
                    TT("dve", SL(iT), SL(iPi), SL(iZr), ALU.mult, SLK, SLK)
                    TT("dve", SL(iPi), SL(iT), SL(iV), ALU.add, SLK, SLK)
                CP("dve", POWr[:, :, k - 1], SL(iPr), SLK, PK)
                CP("dve", POWi[:, :, k - 1], SL(iPi), SLK, PK)
                if k <= 7:
                    CP("dve", POWr[:, :, 8 + 7 - k], SL(iPr), SLK, PK)
                    CP("dve", POWi[:, :, 8 + 7 - k], SL(iPi), SLK, PK)
                    TT("dve", SL(iT), SL(iPr), SL(iPr), ALU.mult, SLK, SLK)
                    TT("dve", SL(iU), SL(iPi), SL(iPi), ALU.mult, SLK, SLK)
                    TT("dve", SL(iT), SL(iT), SL(iU), ALU.add, SLK, SLK)
                    RECIP(SL(iT), SL(iT), SLK, SLK)
                    TT("dve", POWr[:, :, 16 + 7 - k], SL(iPr), SL(iT), ALU.mult, SLK, PK)
                    STT(POWi[:, :, 16 + 7 - k], SL(iPi), -1.0, SL(iT), ALU.mult, ALU.mult, SLK, PK)
            CP("dve", A1[:, 0, :], POWr[:, :, 7], PK, ["A1"])
            CP("dve", A1[:, 1, :], POWr[:, :, 7], PK, ["A1"])
            TS("dve", A2[:, 0, :], POWi[:, :, 7], -1.0, None, ALU.mult, None, PK, ["A2"])
            CP("dve", A2[:, 1, :], POWi[:, :, 7], PK, ["A2"])
            A1T = sb("A1T", [128, 8, 2, 32])
            A2T = sb("A2T", [128, 8, 2, 32])
            CP("dve", SL(iPr), POWr[:, :, 7], PK, SLK)
            CP("dve", SL(iPi), POWi[:, :, 7], PK, SLK)
            for l_ in range(8):
                if l_ > 0:
                    TT("dve", SL(iT), SL(iPr), SL(iPr), ALU.mult, SLK, SLK)
                    TT("dve", SL(iU), SL(iPi), SL(iPi), ALU.mult, SLK, SLK)
                    TT("dve", SL(iV), SL(iPr), SL(iPi), ALU.mult, SLK, SLK)
                    TT("dve", SL(iPr), SL(iT), SL(iU), ALU.subtract, SLK, SLK)
                    TS("dve", SL(iPi), SL(iV), 2.0, None, ALU.mult, None, SLK, SLK)
                CP("dve", A1T[:, l_, 0, :], SL(iPr), SLK, ["A1T"])
                CP("dve", A1T[:, l_, 1, :], SL(iPr), SLK, ["A1T"])
                TS("dve", A2T[:, l_, 0, :], SL(iPi), -1.0, None, ALU.mult, None, SLK, ["A2T"])
                CP("dve", A2T[:, l_, 1, :], SL(iPi), SLK, ["A2T"])
            TS("dve", SL(iLr1), SL(iZr), -1.0, None, ALU.add, None, SLK, SLK)
            TT("dve", SL(iT), SL(iLr1), SL(iAr), ALU.mult, SLK, SLK)
            TT("dve", SL(iU), SL(iZi), SL(iAi), ALU.mult, SLK, SLK)
            TT("dve", SL(iNr), SL(iT), SL(iU), ALU.add, SLK, SLK)
            TT("dve", SL(iT), SL(iZi), SL(iAr), ALU.mult, SLK, SLK)
            TT("dve", SL(iU), SL(iLr1), SL(iAi), ALU.mult, SLK, SLK)
            TT("dve", SL(iNi), SL(iT), SL(iU), ALU.subtract, SLK, SLK)
            TT("dve", SL(iT), SL(iAr), SL(iAr), ALU.mult, SLK, SLK)
            TT("dve", SL(iU), SL(iAi), SL(iAi), ALU.mult, SLK, SLK)
            TT("dve", SL(iDen), SL(iT), SL(iU), ALU.add, SLK, SLK)
            RECIP(SL(iDen), SL(iDen), SLK, SLK)
            TT("dve", SL(iCr), SL(iNr), SL(iDen), ALU.mult, SLK, SLK)
            TT("dve", SL(iCi), SL(iNi), SL(iDen), ALU.mult, SLK, SLK)

            def bc_pow(P, lo):
                return P[:, :, lo:lo + 8].unsqueeze(3).to_broadcast([128, 32, 8, 16])

            def bc_v(V):
                return V.unsqueeze(2).to_broadcast([128, 32, 8, 16])

            def cplx_table(lo, Vr, Vi, vk, outr, outi, okr, oki, neg_i):
                o4r = outr.rearrange("p g (s h) -> p g s h", s=8)
                o4i = outi.rearrange("p g (s h) -> p g s h", s=8)
                TT("dve", T1, bc_pow(POWr, lo), bc_v(Vr), ALU.mult, PK + vk, ["xt"])
                TT("pool", T2, bc_pow(POWi, lo), bc_v(Vi), ALU.mult, PK + vk, ["sq"])
                TT("dve", o4r, T1, T2, ALU.subtract, ["xt", "sq"], okr)
                TT("dve", T1, bc_pow(POWr, lo), bc_v(Vi), ALU.mult, PK + vk, ["xt"])
                TT("pool", T2, bc_pow(POWi, lo), bc_v(Vr), ALU.mult, PK + vk, ["sq"])
                if neg_i:
                    TT("dve", T1, T1, T2, ALU.add, ["xt", "sq"], ["xt"])
                    TS("dve", o4i, T1, -1.0, None, ALU.mult, None, ["xt"], oki)
                else:
                    TT("dve", o4i, T1, T2, ALU.add, ["xt", "sq"], oki)

            em.enabled = plvl >= 2
            for a_ in range(2):
                DMA("sp", hnf[0:32, 0:2048].rearrange("p (h a n) -> p h a n", h=16, a=2)[:, :, a_, :],
                    cre_d.rearrange("(gp g2) h n -> gp g2 h n", g2=2)[:, a_, :, :], [], ["hn"])
                DMA("act", hnf[0:32, 2048:4096].rearrange("p (h a n) -> p h a n", h=16, a=2)[:, :, a_, :],
                    cim_d.rearrange("(gp g2) h n -> gp g2 h n", g2=2)[:, a_, :, :], [], ["hn"])
            for comp, dst, dk in ((0, Cslr, "Cslr"), (1, Csli, "Csli")):
                pC, pCk = PS()
                src4 = hnf[0:32, comp * 2048:(comp + 1) * 2048].rearrange("p (h c) -> p h c", h=16)
                for h_ in range(16):
                    em.op("pe", lambda e, h_=h_, src4=src4, pC=pC: e.transpose(
                        pC[:, h_ * 32:(h_ + 1) * 32], src4[:, h_, :], ident_f[0:32, 0:32]),
                          ["hn", "ident_f"], [pCk])
                CP("dve", dst, pC[:].rearrange("p (h g) -> p g h", h=16), [pCk], [dk])
            cplx_table(0, Cslr, Csli, ["Cslr", "Csli"], P0r[:], NP0i[:], ["P0r"], ["NP0i"], True)
            cplx_table(16, Cslr, Csli, ["Cslr", "Csli"], P0mr, NP0mi, ["mixB"], ["mixB"], True)
            em.enabled = plvl >= 3
            DMA("sp", hnf[0:32, 0:2048], bre_d.rearrange("(gp g2) n h -> gp (g2 n h)", g2=2), [], ["hn"])
            DMA("act", hnf[0:32, 2048:4096], bim_d.rearrange("(gp g2) n h -> gp (g2 n h)", g2=2), [], ["hn"])
            for comp, dst, dk in ((0, Bslr, "Bslr"), (1, Bsli, "Bsli")):
                pB, pBk = PS()
                src3 = hnf[0:32, comp * 2048:(comp + 1) * 2048].rearrange("p (c h) -> p c h", h=16)
                xb2 = xtf[0:32, comp * 2048:(comp + 1) * 2048].rearrange("p (h c) -> p h c", h=16)
                CP("dve", xb2.rearrange("p h c -> p c h"), src3, ["hn"], ["xt"])
                for h_ in range(16):
                    em.op("pe", lambda e, h_=h_, xb2=xb2, pB=pB: e.transpose(pB[:, h_ * 32:(h_ + 1) * 32],
                                                                           xb2[:, h_, :], ident_f[0:32, 0:32]),
                          ["xt", "ident_f"], [pBk])
                CP("dve", dst, pB[:].rearrange("p (h g) -> p g h", h=16), [pBk], [dk])
            crb = SL(iCr).unsqueeze(2).to_broadcast([128, 32, 16])
            cib = SL(iCi).unsqueeze(2).to_broadcast([128, 32, 16])
            T1s = xtf[:, 0:512].rearrange("p (g h) -> p g h", g=32)
            T2s = xtf[:, 512:1024].rearrange("p (g h) -> p g h", g=32)
            TT("dve", T1s, Bslr, crb, ALU.mult, ["Bslr"] + SLK, ["xt"])
            TT("dve", T2s, Bsli, cib, ALU.mult, ["Bsli"] + SLK, ["xt"])
            TT("dve", Bbr, T1s, T2s, ALU.subtract, ["xt"], ["Bbr"])
            TT("dve", T1s, Bsli, crb, ALU.mult, ["Bsli"] + SLK, ["xt"])
            TT("dve", T2s, Bslr, cib, ALU.mult, ["Bslr"] + SLK, ["xt"])
            TT("dve", Bbi, T1s, T2s, ALU.add, ["xt"], ["Bbi"])
            cplx_table(8, Bbr, Bbi, ["Bbr", "Bbi"], Qstr, Qsti, ["mixA"], ["mixA"], False)
            em.enabled = plvl >= 4
            for comp, src, dst, dk in ((0, Qstr, Q0r, "Q0r"), (1, Qsti, Q0i, "Q0i")):
                for g8 in range(4):
                    pq, pqk = PS()
                    pqv = pq.bitcast(BF16)[:].rearrange("p (g c) -> p g c", g=8)
                    for gi in range(8):
                        TR(pqv[:, gi, :], src[:, g8 * 8 + gi, :], ["mixA"], [pqk])
                    CP("act", dst[:, g8 * 8:(g8 + 1) * 8, :], pqv, [pqk], [dk])
            em.enabled = plvl >= 5
            tmpP = [rview("tmpP0", 20480, [128, 4, 128], BF16), rview("tmpP1", 21504, [128, 4, 128], BF16)]
            for g4 in range(16):
                pR, pRk = PS()
                for gi in range(4):
                    g = g4 * 4 + gi
                    gp, g2 = divmod(g, 2)
                    tp = tmpP[gp % 2]
                    tpk = "tmpP%d" % (gp % 2)
                    if g2 == 0:
                        for a_ in range(2):
                            TS("dve", tp[:, a_, :], P0mr[:, gp, :], rowm[:, a_:a_ + 1], None, ALU.mult, None,
                               ["mixB", "rowm"], [tpk])
                            TS("dve", tp[:, 2 + a_, :], NP0mi[:, gp, :], rowm[:, a_:a_ + 1], None, ALU.mult, None,
                               ["mixB", "rowm"], [tpk])
                    MM(pR[:, gi * 128:(gi + 1) * 128], Qstr[:, gp, :], tp[:, g2, :], True, False,
                       ["mixA", tpk], [pRk])
                    MM(pR[:, gi * 128:(gi + 1) * 128], Qsti[:, gp, :], tp[:, 2 + g2, :], False, True,
                       ["mixA", tpk], [pRk])
                TT("dve", tmpR, pR[:].rearrange("p (g l) -> p g l", g=4),
                   mask8.unsqueeze(1).to_broadcast([128, 4, 128]), ALU.mult, [pRk, "mask8"], ["xt"])
                for gi in range(4):
                    g = g4 * 4 + gi
                    STT(R0a[:, g, :], ident_f[:], Dcol[:, g:g + 1], tmpR[:, gi, :], ALU.mult, ALU.add,
                        ["ident_f", "Dcol", "xt"], ["R0a"])

            em.enabled = True
            U8T = rview("U8T", 0, [128, 64, 128], BF16)
            y1fm = rview("y1fm", 0, [128, 8, 8, 128], BF16)
            Vx2 = xview("Vx2", 0, [128, 2, 32, 128], F32)
            Gs = [rview("Gs0", 16384, [128, 2, 4, 128], BF16),
                  rview("Gs1", 16384 + 2048, [128, 2, 4, 128], BF16)]
            sg5 = rview("sg5", 16384 + 4096, [128, 512], F32)
            trt = rview("trt", 0, [128, 2, 32, 64], F32)
            acc = rview("acc", 20480, [128, 2, 32, 8], F32)
            t1b = rview("t1b", 22528, [128, 2, 32, 8], F32)
            t2b = rview("t2b", 24576, [128, 2, 32, 8], F32)
            Cst = rview("Cst", 26624, [128, 2, 32, 9], F32)
            zs5 = rview("zs5", 16384 + 6144, [128, 512], F32)

        o = 0
        xc = xview("xc", 0, [128, 16, 512], BF16)
        pre = [rview("pre0", o, [128, 516], BF16), rview("pre1", o + 1032, [128, 516], BF16)]; o += 2064
        dg = [rview("dg0", o, [128, 4, 128], BF16), rview("dg1", o + 1024, [128, 4, 128], BF16)]; o += 2048
        zsb = xview("zsb", 16384, [128, 4, D], BF16)
        abc_l = [rview("abc0", o, [128, 2, 4, 128], BF16), xview("abc1", 24576, [128, 2, 4, 128], BF16)]; o += 2048
        dm_l = [rview("dm0", o, [128, 4, 128], F32), xview("dm1", 24576 + 2048, [128, 4, 128], F32)]; o += 2048
        Ee_l = [rview("Ee0", o, [128, 4, 128], F32), xview("Ee1", 24576 + 4096, [128, 4, 128], F32)]; o += 2048
        Mh_l = [rview("Mh0", o, [128, 4, 128], BF16), xview("Mh1", 24576 + 6144, [128, 4, 128], BF16)]; o += 1024
        GTm = rview("GTm", o, [128, 4, 128], F32); o += 2048
        xd = rview("xd", o, [128, 16, 64], BF16); o += 2048
        xdd = rview("xdd", o, [128, 16, 64], BF16); o += 2048
        xD = rview("xD", o, [128, 16, 64], BF16); o += 2048
        yv = rview("yv", o, [128, D], F32); o += 4096
        gn = rview("gn", o, [128, D], BF16); o += 2048
        Bc = rview("Bc", o, [128, 4, 128], BF16); o += 1024
        o = 0
        pt = rview("pt", o, [128, 8, PLE], F32); o += 8192
        pbf = rview("pbf", o, [128, 8, PLE], BF16); o += 4096
        p_fm = rview("p_fm", o, [128, 2, 8, 128], BF16); o += 4096
        gsig = rview("gsig", o, [128, 512], F32); o += 2048
        tmp2 = rview("tmp2", o, [128, 512], F32); o += 2048

        def rms_rr(src_key):
            for t in range(8):
                ACT(sq[:], xt[:, t, :], AF.Square, [src_key], ["sq", "ss"], accum=ss[:, t:t + 1])
            ACT(rr[:], ss[:], AF.Sqrt, ["ss"], ["rr"], scale=1.0 / D, bias=EPS)
            RECIP(rr[:], rr[:], ["rr"], ["rr"])

        out_toks = []
        for b in range(nblk):
            t0 = b * BLK
            full = b >= npre
            o0 = (b - npre) * BLK
            if b == npre and npre > 0:
                TS("dve", STf[:], STf[:], flg[:, 0:1], None, ALU.mult, None, ["STf", "flg"], ["STf"])
                TS("dve", STb[:], STb[:], flg[:, 0:1], None, ALU.mult, None, ["STb", "flg"], ["STb"])
                TS("dve", halo[:], halo[:], flg[:, 0:1], None, ALU.mult, None, ["halo", "flg"], ["halo"])
                if do_s5:
                    TS("dve", s5c[:], s5c[:], flg[:, 0:1], None, ALU.mult, None, ["s5c", "flg"], ["s5c"])
            DMA("sp", xt[:], x_d[t0:t0 + BLK, :].rearrange("(j t) d -> j t d", t=8), [], ["xt"])
            rms_rr("xt")
            for t in range(8):
                TS("dve", hs[:, t, :], xt[:, t, :], rr[:, t:t + 1], None, ALU.mult, None,
                   ["xt", "rr"], ["mixA"])
            for kt in range(8):
                pt_, pk = PS()
                pv = pt_.bitcast(BF16)[:].rearrange("p (t j) -> p t j", t=8)
                for t in range(8):
                    TR(pv[:, t, :], hs[:, t, kt * 128:(kt + 1) * 128], ["mixA"], [pk])
                if kt % 2:
                    ACT(hn[:, kt, :].rearrange("p (j t) -> p t j", t=8), pv, AF.Copy, [pk, "nw"], ["hn"],
                        scale=nw[:, kt:kt + 1])
                else:
                    TS("dve", hn[:, kt, :].rearrange("p (j t) -> p t j", t=8), pv, nw[:, kt:kt + 1], None,
                       ALU.mult, None, [pk, "nw"], ["hn"])
            if not do_s5:
                MSET("pool", mixA, 0.0, ["mixA"])

            if do_s5 and lvl >= 1:
                for cb in range(2):
                    cs = slice(cb * 512, (cb + 1) * 512)
                    wv, wk = load_w(win_b, cb * 512, 512, 0, 8, "win_b")
                    for t in range(8):
                        pu, puk = PS()
                        for kt in range(8):
                            MM(pu[:], hn4[:, kt, t, :], wv[:, kt, :], kt == 0, kt == 7, ["hn", wk], [puk])
                        CP("act", u8g[:, cb * 32:(cb + 1) * 32, t, :], pu[:].rearrange("p (g h) -> p g h", h=16),
                           [puk], ["mixB"])
                for g8 in range(8):
                    pq, pqk = PS()
                    pqv = pq.bitcast(BF16)[:].rearrange("p (g c) -> p g c", g=8)
                    for gi in range(8):
                        g = g8 * 8 + gi
                        TR(pqv[:, gi, :], u8g[:, g, :, :].rearrange("p t h -> p (t h)"), ["mixB"], [pqk])
                    CP("act" if g8 % 2 else "dve", U8T[:, g8 * 8:(g8 + 1) * 8, :], pqv, [pqk], ["U8T"])
                if lvl >= 2:
                    for gp4 in range(8):
                        pvr, pvrk = PS()
                        pvi, pvik = PS()
                        for gq in range(4):
                            gp = gp4 * 4 + gq
                            for g2 in range(2):
                                g = 2 * gp + g2
                                rows = slice(g2 * 64, (g2 + 1) * 64)
                                MM(pvr[rows, gq * 128:(gq + 1) * 128], Q0r[:, gp, rows], U8T[:, g, :], True, True,
                                   ["Q0r", "U8T"], [pvrk])
                                MM(pvi[rows, gq * 128:(gq + 1) * 128], Q0i[:, gp, rows], U8T[:, g, :], True, True,
                                   ["Q0i", "U8T"], [pvik])
                        CP("act", Vx2[:, 0, gp4 * 4:(gp4 + 1) * 4, :], pvr[:].rearrange("p (g j) -> p g j", g=4),
                           [pvrk], ["Vx2"])
                        CP("dve", Vx2[:, 1, gp4 * 4:(gp4 + 1) * 4, :], pvi[:].rearrange("p (g j) -> p g j", g=4),
                           [pvik], ["Vx2"])
                if lvl >= 3 and not full:
                    for l_ in range(7):
                        s_ = 1 << l_
                        n_ = 64 >> l_
                        Xa = Vx2[:, :, :, s_ - 1::2 * s_]
                        Xb = Vx2[:, :, :, 2 * s_ - 1::2 * s_]
                        tv = trt[:, :, :, 0:n_]
                        TT("dve", tv, Xa, A1T[:, l_, :, :].unsqueeze(3).to_broadcast([128, 2, 32, n_]), ALU.mult,
                           ["Vx2", "A1T"], ["trt"])
                        TT("dve", Xb, Xb, tv, ALU.add, ["Vx2", "trt"], ["Vx2"])
                        TT("dve", tv[:, 0, :, :], Xa[:, 1, :, :],
                           A2T[:, l_, 0, :].unsqueeze(2).to_broadcast([128, 32, n_]), ALU.mult, ["Vx2", "A2T"], ["trt"])
                        TT("dve", tv[:, 1, :, :], Xa[:, 0, :, :],
                           A2T[:, l_, 1, :].unsqueeze(2).to_broadcast([128, 32, n_]), ALU.mult, ["Vx2", "A2T"], ["trt"])
                        TT("dve", Xb, Xb, tv, ALU.add, ["Vx2", "trt"], ["Vx2"])
                    TT("dve", rt1[:], s5c[:], A1T[:, 7, :, :], ALU.mult, ["s5c", "A1T"], ["rt1"])
                    TT("dve", rt2[:, 0, :], s5c[:, 1, :], A2T[:, 7, 0, :], ALU.mult, ["s5c", "A2T"], ["rt2"])
                    TT("dve", rt2[:, 1, :], s5c[:, 0, :], A2T[:, 7, 1, :], ALU.mult, ["s5c", "A2T"], ["rt2"])
                    TT("dve", rt1[:], rt1[:], rt2[:], ALU.add, ["rt1", "rt2"], ["rt1"])
                    TT("dve", s5c[:], Vx2[:, :, :, 127], rt1[:], ALU.add, ["Vx2", "rt1"], ["s5c"])
                if lvl >= 3 and full and cfg.get("rec_blocked", False):
                    A1b = A1[:].unsqueeze(3).to_broadcast([128, 2, 32, 8])
                    A2b0 = A2[:, 0, :].unsqueeze(2).to_broadcast([128, 32, 8])
                    A2b1 = A2[:, 1, :].unsqueeze(2).to_broadcast([128, 32, 8])

                    def Vi(i):
                        return Vx2[:, :, :, i::16]
                    CP("dve", acc, Vi(0), ["Vx2"], ["acc"])
                    for i in range(1, 16):
                        TT("dve", t1b, acc, A1b, ALU.mult, ["acc", "A1"], ["t1b"])
                        TT("dve", t2b[:, 0, :, :], acc[:, 1, :, :], A2b0, ALU.mult, ["acc", "A2"], ["t2b"])
                        TT("dve", t2b[:, 1, :, :], acc[:, 0, :, :], A2b1, ALU.mult, ["acc", "A2"], ["t2b"])
                        TT("dve", acc, t1b, Vi(i), ALU.add, ["t1b", "Vx2"], ["acc"])
                        TT("dve", acc, acc, t2b, ALU.add, ["acc", "t2b"], ["acc"])
                    CP("dve", Cst[:, :, :, 0], s5c[:], ["s5c"], ["Cst"])
                    for s_ in range(8):
                        TT("dve", rt1[:], Cst[:, :, :, s_], A1T[:, 4, :, :], ALU.mult, ["Cst", "A1T"], ["rt1"])
                        TT("dve", rt2[:, 0, :], Cst[:, 1, :, s_], A2T[:, 4, 0, :], ALU.mult, ["Cst", "A2T"], ["rt2"])
                        TT("dve", rt2[:, 1, :], Cst[:, 0, :, s_], A2T[:, 4, 1, :], ALU.mult, ["Cst", "A2T"], ["rt2"])
                        TT("dve", Cst[:, :, :, s_ + 1], rt1[:], acc[:, :, :, s_], ALU.add, ["rt1", "acc"], ["Cst"])
                        TT("dve", Cst[:, :, :, s_ + 1], Cst[:, :, :, s_ + 1], rt2[:], ALU.add, ["Cst", "rt2"], ["Cst"])
                    for i in range(16):
                        if i == 0:
                            prev, prv0, prv1, pk_ = Cst[:, :, :, 0:8], Cst[:, 0, :, 0:8], Cst[:, 1, :, 0:8], "Cst"
                        else:
                            prev, prv0, prv1, pk_ = Vi(i - 1), Vx2[:, 0, :, i - 1::16], Vx2[:, 1, :, i - 1::16], "Vx2"
                        TT("dve", t1b, prev, A1b, ALU.mult, [pk_, "A1"], ["t1b"])
                        TT("dve", t2b[:, 0, :, :], prv1, A2b0, ALU.mult, [pk_, "A2"], ["t2b"])
                        TT("dve", t2b[:, 1, :, :], prv0, A2b1, ALU.mult, [pk_, "A2"], ["t2b"])
                        TT("dve", Vi(i), Vi(i), t1b, ALU.add, ["Vx2", "t1b"], ["Vx2"])
                        TT("dve", Vi(i), Vi(i), t2b, ALU.add, ["Vx2", "t2b"], ["Vx2"])
                if lvl >= 3 and full and not cfg.get("rec_blocked", False):
                    for j in range(128):
                        if j == 0:
                            Gj, Gjr, Gji, gk = s5c[:], s5c[:, 0, :], s5c[:, 1, :], "s5c"
                        else:
                            Gj, Gjr, Gji, gk = Vx2[:, :, :, j - 1], Vx2[:, 0, :, j - 1], Vx2[:, 1, :, j - 1], "Vx2"
                        TT("dve", rt1[:], Gj, A1[:], ALU.mult, [gk, "A1"], ["rt1"])
                        TT("dve", rt2[:, 0, :], Gji, A2[:, 0, :], ALU.mult, [gk, "A2"], ["rt2"])
                        TT("dve", rt2[:, 1, :], Gjr, A2[:, 1, :], ALU.mult, [gk, "A2"], ["rt2"])
                        TT("dve", Vx2[:, :, :, j], Vx2[:, :, :, j], rt1[:], ALU.add, ["Vx2", "rt1"], ["Vx2"])
                        TT("dve", Vx2[:, :, :, j], Vx2[:, :, :, j], rt2[:], ALU.add, ["Vx2", "rt2"], ["Vx2"])
                if lvl >= 4 and full:
                    for g4 in range(16):
                        gs = Gs[g4 % 2]
                        gsk = "Gs%d" % (g4 % 2)
                        if g4 < 2:
                            MSET("pool", gs, 0.0, [gsk])
                        for a_ in range(2):
                            rws = slice(a_ * 64, (a_ + 1) * 64)
                            gsv = gs.rearrange("p r (q a) j -> p r q a j", a=2)
                            CP("act" if a_ else "dve", gsv[rws, :, :, a_, 1:128], Vx2[rws, :, g4 * 2:(g4 + 1) * 2, 0:127],
                               ["Vx2"], [gsk])
                            CP("dve" if a_ else "act", gsv[rws, :, :, a_, 0], s5c[rws, :, g4 * 2:(g4 + 1) * 2], ["s5c"], [gsk])
                        py_, pyk = PS()
                        for gi in range(4):
                            g = g4 * 4 + gi
                            gp, g2 = divmod(g, 2)
                            rows = slice(g2 * 64, (g2 + 1) * 64)
                            osl = py_[:, gi * 128:(gi + 1) * 128]
                            MM(osl, U8T[:, g, :], R0a[:, g, :], True, False, ["U8T", "R0a"], [pyk])
                            MM(osl, gs[:, 0, gi, :], P0r[:, gp, :], False, False, [gsk, "P0r"], [pyk])
                            MM(osl, gs[:, 1, gi, :], NP0i[:, gp, :], False, True, [gsk, "NP0i"], [pyk])
                        ACT(u8[:, :, 64 * g4:64 * (g4 + 1)].rearrange("p t (g h) -> p g t h", g=4),
                            py_[:].rearrange("p (g t h) -> p g t h", g=4, t=8), AF.Gelu_apprx_tanh, [pyk], ["mixB"])
                    CP("dve", s5c[:], Vx2[:, :, :, 127], ["Vx2"], ["s5c"])
                if lvl >= 9 and full:
                    for kt in range(8):
                        pt_, pk = PS()
                        pv = pt_.bitcast(BF16)[:].rearrange("p (t j) -> p t j", t=8)
                        for t in range(8):
                            TR(pv[:, t, :], u8[:, t, kt * 128:(kt + 1) * 128], ["mixB"], [pk])
                        CP("act" if kt % 2 else "dve", y1fm[:, kt, :, :], pv, [pk], ["y1fm"])
                    for cb in range(2):
                        cs = slice(cb * 512, (cb + 1) * 512)
                        wv, wk = load_w(wglu_b, cb * 512, 512, 0, 8, "wglu_b")
                        wz, wzk = load_w(win_b, 1024 + cb * 512, 512, 0, 8, "win_b")
                        for t in range(8):
                            pg, pgk = PS()
                            for kt in range(8):
                                MM(pg[:], y1fm[:, kt, t, :], wv[:, kt, :], kt == 0, False, ["y1fm", wk], [pgk])
                            MM(pg[:], ones_bf[0:1, :], bglu_bf[0:1, cs], False, True, ["ones_bf", "bglu_bf"], [pgk])
                            pz, pzk = PS()
                            for kt in range(8):
                                MM(pz[:], hn4[:, kt, t, :], wz[:, kt, :], kt == 0, kt == 7, ["hn", wzk], [pzk])
                            ACT(sg5, pg[:], AF.Sigmoid, [pgk], ["sg5"])
                            ACT(zs5, pz[:], AF.Silu, [pzk], ["zs5"])
                            TT("dve", sg5, sg5, zs5, ALU.mult, ["sg5", "zs5"], ["sg5"])
                            TT("dve", u8[:, t, cs], u8[:, t, cs], sg5, ALU.mult, ["mixB", "sg5"], ["mixB"])
                    for kt in range(8):
                        pt_, pk = PS()
                        pv = pt_.bitcast(BF16)[:].rearrange("p (t j) -> p t j", t=8)
                        for t in range(8):
                            TR(pv[:, t, :], u8[:, t, kt * 128:(kt + 1) * 128], ["mixB"], [pk])
                        CP("act" if kt % 2 else "dve", mix[:, kt, :, :], pv, [pk], ["mixA"])
            if do_s5 and lvl < 9:
                MSET("pool", mixA, 0.0, ["mixA"])
            pd_, pdk = PS()
            pdt = pd_[:, 0:128].rearrange("p (c h) -> p c h", h=16)
            for c in range(8):
                for kt in range(8):
                    MM(pdt[:, c, :], hn[:, kt, c * 128:(c + 1) * 128], wdt[:, kt, :], kt == 0, kt == 7,
                       ["hn", "wdt"], [pdk])
            TT("dve", dtr[:], pdt, dtb[:].unsqueeze(1).to_broadcast([128, 8, 16]), ALU.add, [pdk, "dtb"], ["dtr"])
            ACT(dtr[:], dtr[:], AF.Exp, ["dtr"], ["dtr"])
            ACT(dtv[:], dtr[:], AF.Ln, ["dtr"], ["dtv"], bias=1.0)
            TT("dve", av[:], dtv[:], Abc[:].unsqueeze(1).to_broadcast([128, 8, 16]), ALU.mult, ["dtv", "Abc"], ["av"])
            CP("dve", ahi[:], av[:], ["av"], ["ahi"])
            TT("dve", alf[:], av[:], ahi[:], ALU.subtract, ["av", "ahi"], ["alf"])
            CP("dve", alo[:], alf[:], ["alf"], ["alo"])

            for hf in range(2):
                hsl = slice(hf * 512, (hf + 1) * 512)
                for ctg in range(4 if (full or (b == npre - 1 and hf == 1)) else 3):
                    wv, wk = load_w(win_b, 3072 + ctg * 512, 512, 0, 8, "win_b")
                    for c4 in range(4):
                        ct = ctg * 4 + c4
                        pp, ppk = PS()
                        for kt in range(8):
                            MM(pp[:], wv[:, kt, c4 * 128:(c4 + 1) * 128], hn[:, kt, hsl], kt == 0, kt == 7,
                               ["hn", wk], [ppk])
                        pr = pre[ct % 2]
                        prk = "pre%d" % (ct % 2)
                        dgv = dg[ct % 2]
                        dgk = "dg%d" % (ct % 2)
                        CP("act", pr[:, 3:515], pp[:], [ppk], [prk])
                        CP("pool", pr[:, 0:3], halo[:, ct, :], ["halo"], [prk])
                        for k in range(4):
                            TS("dve", dgv[:, k, :], ident_f[:], cw[:, ct, k:k + 1], None, ALU.mult, None,
                               ["ident_f", "cw"], [dgk])
                        pc, pck = PS()
                        for k in range(4):
                            MM(pc[:], dgv[:, k, :], pr[:, k:k + 512], k == 0, k == 3, [dgk, prk], [pck])
                        ACT(xc[:, ct, :], pc[:], AF.Silu, [pck, "cbv"], ["xc"], bias=cbv[:, ct:ct + 1])
                        CP("pool", halo[:, ct, :], pr[:, 512:515], [prk], ["halo"])
                for cb in range(2 if full else 0):
                    wv, wk = load_w(win_b, 2048 + cb * 512, 512, 0, 8, "win_b")
                    for c in range(4):
                        pz, pzk = PS()
                        for kt in range(8):
                            MM(pz[:], hn[:, kt, hf * 512 + c * 128: hf * 512 + (c + 1) * 128], wv[:, kt, :],
                               kt == 0, kt == 7, ["hn", wk], [pzk])
                        ACT(zsb[:, c, cb * 512:(cb + 1) * 512], pz[:], AF.Silu, [pzk], ["zsb"])
                for c in range(4):
                    cg = hf * 4 + c
                    tok = slice(c * 128, (c + 1) * 128)
                    a_c = av[:, cg, :]
                    dt_c = dtv[:, cg, :]
                    pa, pak = PS()
                    MM(pa[:, 0:16], tri_f[:], a_c, True, True, ["tri_f", "av"], [pak])
                    MM(pa[:, 16:32], ones_f[:], a_c, True, True, ["ones_f", "av"], [pak])
                    CP("dve", acs[:], pa[:, 0:16], [pak], ["acs"])
                    TT("dve", t16[:], pa[:, 16:32], acs[:], ALU.subtract, [pak, "acs"], ["t16"])
                    ACT(dec[:], t16[:], AF.Exp, ["t16"], ["dec"])
                    if full:
                        ACT(el[:], acs[:], AF.Exp, ["acs"], ["el"])
                    ACT(etot[:], pa[:, 16:32], AF.Exp, [pak], ["etot"])
                    px, pxk = PS()
                    pxb = px.bitcast(BF16)
                    for ct in range(8):
                        TR(pxb[:, ct * 128:(ct + 1) * 128], xc[:, ct, tok], ["xc"], [pxk])
                    pxv = pxb[:].rearrange("p (h q) -> p h q", h=16)
                    TT("dve", xd, pxv, dt_c.unsqueeze(2).to_broadcast([128, 16, 64]), ALU.mult, [pxk, "dtv"], ["xd"])
                    if full:
                        TT("dve", xD, pxv, Dhb[:].unsqueeze(2).to_broadcast([128, 16, 64]), ALU.mult, [pxk, "Dhb"], ["xD"])
                    TT("pool", xdd, xd, dec[:].unsqueeze(2).to_broadcast([128, 16, 64]), ALU.mult, ["xd", "dec"], ["xdd"])
                    pb_, pbk = PS()
                    pbv = pb_.bitcast(BF16)[:, 0:512].rearrange("p (g n) -> p g n", g=4)
                    for g in range(4):
                        TR(pbv[:, g, :], xc[:, 8 + g, tok], ["xc"], [pbk])
                    CP("act", Bc, pbv, [pbk], ["Bc"])
                    if full:
                        pg_, pgk = PS()
                        pgv = pg_[:].rearrange("p (g l) -> p g l", g=4)
                        for g in range(4):
                            MM(pgv[:, g, :], xc[:, 8 + g, tok], xc[:, 12 + g, tok], True, True, ["xc"], [pgk])
                        TT("dve", GTm, pgv, tri_f[:].unsqueeze(1).to_broadcast([128, 4, 128]), ALU.mult,
                           [pgk, "tri_f"], ["GTm"])
                        py = [PS(), PS()]

                        def emit_abc(g):
                            abc = abc_l[g % 2]
                            kab = "abc%d" % (g % 2)
                            CP("act", abc[:, 0, :, :], ahi[:, cg, 4 * g:4 * g + 4].unsqueeze(2).to_broadcast([128, 4, 128]),
                               ["ahi"], [kab])
                            CP("act", abc[:, 1, :, :], alo[:, cg, 4 * g:4 * g + 4].unsqueeze(2).to_broadcast([128, 4, 128]),
                               ["alo"], [kab])

                        emit_abc(0)
                        for g in range(4):
                            abc, dm, Ee, Mh = abc_l[g % 2], dm_l[g % 2], Ee_l[g % 2], Mh_l[g % 2]
                            kab, kdm, kEe, kMh = "abc%d" % (g % 2), "dm%d" % (g % 2), "Ee%d" % (g % 2), "Mh%d" % (g % 2)
                            pdd, pddk = PS()
                            pdv = pdd[:].rearrange("p (h l) -> p h l", h=4)
                            for hh in range(4):
                                MM(pdv[:, hh, :], abc[:, 0, hh, :], tri_b[:], True, False, [kab, "tri_b"], [pddk])
                                MM(pdv[:, hh, :], abc[:, 1, hh, :], tri_b[:], False, True, [kab, "tri_b"], [pddk])
                            if g < 3:
                                emit_abc(g + 1)
                            for hh in range(4):
                                h = 4 * g + hh
                                TS("dve", dm[:, hh, :], pdv[:, hh, :], acs[:, h:h + 1], 0.0, ALU.subtract, ALU.min,
                                   [pddk, "acs"], [kdm])
                            ACT(Ee, dm, AF.Exp, [kdm], [kEe])
                            TT("pool", Mh, Ee, GTm[:, g, :].unsqueeze(1).to_broadcast([128, 4, 128]), ALU.mult,
                               [kEe, "GTm"], [kMh])
                            for hh in range(4):
                                h = 4 * g + hh
                                bank, bk = py[h // 8]
                                col = (h % 8) * 64
                                MM(bank[:, col:col + 64], Mh[:, hh, :], xd[:, h, :], True, False, [kMh, "xd"], [bk])
                                MM(bank[:, col:col + 64], ident[:], xD[:, h, :], False, True, ["ident", "xD"], [bk])
                        po = [PS(), PS()]
                        for g in range(4):
                            bank, bk = po[g // 2]
                            col = (g % 2) * 256
                            MM(bank[:, col:col + 256], xc[:, 12 + g, tok], STb[:, g * 256:(g + 1) * 256], True, True,
                               ["xc", "STb"], [bk])
                        for hb in range(2):
                            ysl = yv[:, hb * 512:(hb + 1) * 512]
                            y3 = ysl.rearrange("p (h q) -> p h q", h=8)
                            TT("dve", y3, po[hb][0][:].rearrange("p (h q) -> p h q", h=8),
                               el[:, hb * 8:(hb + 1) * 8].unsqueeze(2).to_broadcast([128, 8, 64]), ALU.mult,
                               [po[hb][1], "el"], ["yv"])
                            TT("dve", ysl, ysl, py[hb][0][:], ALU.add, ["yv", py[hb][1]], ["yv"])
                            TT("pool", ysl, ysl, zsb[:, c, hb * 512:(hb + 1) * 512], ALU.mult, ["yv", "zsb"], ["yv"])
                        for grp in range(4):
                            ACT(sq[:, 0:256], yv[:, grp * 256:(grp + 1) * 256], AF.Square, ["yv"], ["sq", "ssg"],
                                accum=ssg[:, grp:grp + 1])
                        ACT(rg[:], ssg[:], AF.Ln, ["ssg"], ["rg"], scale=1.0 / 256, bias=EPS)
                        ACT(rg[:], rg[:], AF.Exp, ["rg"], ["rg"], scale=-0.5)
                        TT("dve", gn.rearrange("p (g q) -> p g q", g=4), yv.rearrange("p (g q) -> p g q", g=4),
                           rg[:].unsqueeze(2).to_broadcast([128, 4, 256]), ALU.mult, ["yv", "rg"], ["gn"])
                        ptt, ptk = PS()
                        ptv = ptt.bitcast(BF16)[:].rearrange("p (k l) -> p k l", k=8)
                        for kt in range(8):
                            TR(ptv[:, kt, :], gn[:, kt * 128:(kt + 1) * 128], ["gn"], [ptk])
                        for kt in range(8):
                            TS("dve", mix[:, 8 + kt, :, 16 * cg:16 * cg + 16],
                               ptv[:, kt, :].rearrange("p (j t) -> p t j", t=8), snw[:, kt:kt + 1], None,
                               ALU.mult, None, [ptk, "snw"], ["mixB"])
                    pst = [PS(), PS()]
                    for g in range(4):
                        bank, bk = pst[g // 2]
                        col = (g % 2) * 256
                        MM(bank[:, col:col + 256], Bc[:, g, :],
                           xdd[:, 4 * g:4 * g + 4, :].rearrange("p h q -> p (h q)"), True, True, ["Bc", "xdd"], [bk])
                    for hb in range(2):
                        s3 = STf[:, hb * 512:(hb + 1) * 512].rearrange("p (h q) -> p h q", h=8)
                        TT("dve", s3, s3, etot[:, hb * 8:(hb + 1) * 8].unsqueeze(2).to_broadcast([128, 8, 64]),
                           ALU.mult, ["STf", "etot"], ["STf"])
                        TT("dve", STf[:, hb * 512:(hb + 1) * 512], STf[:, hb * 512:(hb + 1) * 512], pst[hb][0][:],
                           ALU.add, ["STf", pst[hb][1]], ["STf"])
                    CP("act", STb[:], STf[:], ["STf"], ["STb"])

            if full:
                DMA("sp", xt[:], x_d[t0:t0 + BLK, :].rearrange("(j t) d -> j t d", t=8), [], ["xt"])
                for cb in range(2):
                    cs = slice(cb * 512, (cb + 1) * 512)
                    wa, wak = load_w(wout_b, cb * 512, 512, 0, 8, "wout_b")
                    wbb, wbk = load_w(wout_b, cb * 512, 512, 8, 8, "wout_b")
                    for t in range(8):
                        po_, pok = PS()
                        for kt in range(16):
                            wv, wk = (wa, wak) if kt < 8 else (wbb, wbk)
                            MM(po_[:], mix[:, kt, t, :], wv[:, kt % 8, :], kt == 0, kt == 15,
                               ["mixA" if kt < 8 else "mixB", wk], [pok])
                        TT("dve", xt[:, t, cs], xt[:, t, cs], po_[:], ALU.add, ["xt", pok], ["xt"])

                DMA("act", pt, p_d[o0:o0 + BLK, :].rearrange("(j t) d -> j t d", t=8), [], ["pt"])
                rms_rr("xt")
                for t in range(8):
                    TS("dve", hs[:, t, :], xt[:, t, :], rr[:, t:t + 1], None, ALU.mult, None,
                       ["xt", "rr"], ["mixA"])
                hr4 = hn[:].rearrange("p k (t j) -> p k t j", t=8)
                for kt in range(8):
                    pt_, pk = PS()
                    pv = pt_.bitcast(BF16)[:].rearrange("p (t j) -> p t j", t=8)
                    for t in range(8):
                        TR(pv[:, t, :], hs[:, t, kt * 128:(kt + 1) * 128], ["mixA"], [pk])
                    if kt % 2:
                        ACT(hr4[:, kt, :, :], pv, AF.Copy, [pk, "plw"], ["hn"], scale=plw[:, kt:kt + 1])
                    else:
                        TS("dve", hr4[:, kt, :, :], pv, plw[:, kt:kt + 1], None, ALU.mult, None, [pk, "plw"], ["hn"])
                CP("pool", pbf, pt, ["pt"], ["pbf"])
                for k2 in range(2):
                    pt_, pk = PS()
                    pv = pt_.bitcast(BF16)[:].rearrange("p (t j) -> p t j", t=8)
                    for t in range(8):
                        TR(pv[:, t, :], pbf[:, t, k2 * 128:(k2 + 1) * 128], ["pbf"], [pk])
                    CP("act", p_fm[:, k2, :, :], pv, [pk], ["p_fm"])
                for cb in range(2):
                    cs = slice(cb * 512, (cb + 1) * 512)
                    wgv, wgk = load_w(wgate_b, cb * 512, 512, 0, 8, "wgate_b")
                    wpv, wpk = load_w(wple_b, cb * 512, 512, 0, 2, "wple_b")
                    for t in range(8):
                        pg, pgk = PS()
                        for kt in range(8):
                            MM(pg[:], hr4[:, kt, t, :], wgv[:, kt, :], kt == 0, kt == 7, ["hn", wgk], [pgk])
                        ACT(gsig, pg[:], AF.Sigmoid, [pgk], ["gsig"])
                        pp, ppk = PS()
                        for k2 in range(2):
                            MM(pp[:], p_fm[:, k2, t, :], wpv[:, k2, :], k2 == 0, k2 == 1, ["p_fm", wpk], [ppk])
                        TT("dve", tmp2, pp[:], gsig, ALU.mult, [ppk, "gsig"], ["tmp2"])
                        TT("pool", xt[:, t, cs], xt[:, t, cs], tmp2, ALU.add, ["xt", "tmp2"], ["xt"])
                rms_rr("xt")
                mixo = mix.bitcast(F32)[:].rearrange("p k t j -> p (k t j)").rearrange("p (t d) -> p t d", t=8)
                for t in range(8):
                    STT(mixo[:, t, :], xt[:, t, :], rr[:, t:t + 1], fwb[:], ALU.mult, ALU.mult, ["xt", "rr", "fwb"],
                        ["mixA", "mixB"])
                tok_ = DMA("sp", out_d[o0:o0 + BLK, :].rearrange("(j t) d -> j t d", t=8), mixo, ["mixA", "mixB"], [])
                out_toks.append(tok_)
        em.final_wait("sp", out_toks)
        em.emit(nc, sems)
    return nc


_NC_CACHE = {}


def kernel(_cfg=None, **inputs):
    f = lambda a: np.ascontiguousarray(a, dtype=np.float32)
    key = repr(sorted((_cfg or {}).items()))
    if key not in _NC_CACHE:
        _NC_CACHE[key] = build_program(_cfg)
    nc = _NC_CACHE[key]
    tri = np.triu(np.ones((128, 128), dtype=np.float32))
    shared = {
        "c_ident": np.eye(128, dtype=np.float32),
        "c_tri": tri,
        "c_ones": np.ones((128, 128), dtype=np.float32),
        "final_norm_w": f(inputs["final_norm_w"]).reshape(1, D),
        "ple_norm_w": f(inputs["ple_norm_w"]).reshape(D),
        "norm_w": f(inputs["norm_w"]).reshape(D),
        "ssd_norm_w": f(inputs["ssd_norm_w"]).reshape(D),
        "w_in": f(inputs["w_in"]).reshape(D, DIN),
        "w_out": f(inputs["w_out"]).reshape(2 * D, D),
        "w_ple_gate": f(inputs["w_ple_gate"]).reshape(D, D),
        "w_ple_proj": f(inputs["w_ple_proj"]).reshape(PLE, D),
        "conv_w": f(inputs["conv_w"]).reshape(4, 2048),
        "conv_b": f(inputs["conv_b"]).reshape(2048),
        "dt_bias": f(inputs["dt_bias"]).reshape(1, 16),
        "A_log": f(inputs["A_log"]).reshape(1, 16),
        "ssd_D": f(inputs["ssd_D"]).reshape(1, 16),
    }
    if (_cfg or {}).get("s5", True):
        m8 = np.zeros((128, 128), dtype=np.float32)
        for s_lo in range(8):
            for t_lo in range(s_lo, 8):
                m8[s_lo * 16:(s_lo + 1) * 16, t_lo * 16:(t_lo + 1) * 16] = 1.0
        rowm = np.zeros((128, 2), dtype=np.float32)
        rowm[:64, 0] = 1.0
        rowm[64:, 1] = 1.0
        shared.update({
            "c_mask8": m8,
            "c_rowmask": rowm,
            "s5_A_re": f(inputs["s5_A_re"]).reshape(64, 64),
            "s5_A_im": f(inputs["s5_A_im"]).reshape(64, 64),
            "s5_log_dt": f(inputs["s5_log_dt"]).reshape(64),
            "s5_B_re": f(inputs["s5_B_re"]).reshape(64, 64, 16),
            "s5_B_im": f(inputs["s5_B_im"]).reshape(64, 64, 16),
            "s5_C_re": f(inputs["s5_C_re"]).reshape(64, 16, 64),
            "s5_C_im": f(inputs["s5_C_im"]).reshape(64, 16, 64),
            "s5_D": f(inputs["s5_D"]).reshape(D),
            "s5_w_glu": f(inputs["s5_w_glu"]).reshape(D, D),
            "s5_b_glu": f(inputs["s5_b_glu"]).reshape(1, D),
        })
    x = f(inputs["x"])
    p = f(inputs["p"])
    cfg_ = _cfg or {}
    nblk_ = cfg_.get("nblk", NBLK)
    npre_ = cfg_.get("npre", 4)
    nown_ = nblk_ - npre_
    HALF = L // 2
    in_maps = []
    for c in range(8):
        b, half = divmod(c, 2)
        if npre_ == 4 and nblk_ == 8:
            xin = np.concatenate([x[b, 0:HALF], x[b, half * HALF:(half + 1) * HALF]], axis=0)
            pin = p[0, b, half * HALF:(half + 1) * HALF]
            fl = float(half)
        else:
            xin = x[b]
            pin = p[0, b, npre_ * BLK:nblk_ * BLK]
            fl = 1.0
        m = {"x": np.ascontiguousarray(xin), "p": np.ascontiguousarray(pin),
             "flag": np.full((128, 1), fl, dtype=np.float32)}
        m.update(shared)
        in_maps.append(m)
    res = run_bass_kernel_spmd(nc, in_maps, core_ids=list(range(8)))
    if npre_ == 4 and nblk_ == 8:
        out = np.empty((NBATCH, L, D), dtype=np.float32)
        for c in range(8):
            b, half = divmod(c, 2)
            out[b, half * HALF:(half + 1) * HALF] = res.results[c]["out"]
        return out
    out = np.zeros((NBATCH, L, D), dtype=np.float32)
    for b in range(NBATCH):
        out[b, npre_ * BLK:nblk_ * BLK] = res.results[2 * b]["out"]
    return out
```

```python
import math
import numpy as np
import concourse.bass as bass
import concourse.mybir as mybir
from concourse.bass_utils import run_bass_kernel_spmd

F32 = mybir.dt.float32
BF16 = mybir.dt.bfloat16
ALU = mybir.AluOpType
AF = mybir.ActivationFunctionType
AX = mybir.AxisListType

D = 1024
L = 8192
NBATCH = 4
BLK = 1024
NBLK = L // BLK
DIN = 5136
PLE = 256
EPS = 1e-6


class Em:
    ENGS = ("pe", "act", "dve", "pool", "sp")

    def __init__(self, n_dma_sems=12, strict_same=True):
        self.ops = {e: [] for e in self.ENGS}
        self.count = {e: 0 for e in self.ENGS}
        self.waited = {e: {} for e in self.ENGS}
        self.last_w = {}
        self.readers = {}
        self.n_dma = n_dma_sems
        self.dma_cnt = [0] * n_dma_sems
        self.dma_rr = 0
        self.sw_rr = 0
        self.strict_same = strict_same
        self.alias = {}
        self.enabled = True
        self.regions = []

    def region(self, key, arena, lo, hi):
        for (k2, a2, lo2, hi2) in self.regions:
            if k2 == key:
                return
        for (k2, a2, lo2, hi2) in self.regions:
            if a2 == arena and lo < hi2 and lo2 < hi:
                self.alias.setdefault(key, []).append(k2)
                self.alias.setdefault(k2, []).append(key)
        self.regions.append((key, arena, lo, hi))

    def _exp(self, keys):
        out = []
        for k in keys:
            out.append(k)
            out.extend(self.alias.get(k, ()))
        return out

    def _deps(self, eng, reads, writes):
        need = {}

        def add(tok):
            if tok is None:
                return
            sk, v = tok
            if sk == eng and not (self.strict_same and eng != "pe"):
                return
            if need.get(sk, 0) < v:
                need[sk] = v

        def add_other(tok):
            if tok is not None and tok[0] != eng:
                add(tok)

        for k in self._exp(reads):
            add(self.last_w.get(k))
        for k in self._exp(writes):
            add_other(self.last_w.get(k))
            for r in self.readers.get(k, ()):
                add_other(r)
        out = []
        for sk, v in need.items():
            if self.waited[eng].get(sk, 0) < v:
                self.waited[eng][sk] = v
                out.append((sk, v))
        return out

    def _commit(self, tok, reads, writes):
        for k in reads:
            self.readers.setdefault(k, []).append(tok)
        for k in writes:
            self.last_w[k] = tok
            self.readers[k] = []

    def op(self, eng, fn, reads=(), writes=()):
        if not self.enabled:
            return None
        waits = self._deps(eng, reads, writes)
        self.count[eng] += 1
        tok = (eng, self.count[eng])
        self.ops[eng].append((waits, fn, (eng, 1)))
        self._commit(tok, reads, writes)

    def dma(self, eng, fn, reads=(), writes=()):
        if not self.enabled:
            return None
        waits = self._deps(eng, reads, writes)
        if eng == "pool":
            i = self.n_dma - 1 - (self.sw_rr % 2)
            self.sw_rr += 1
        else:
            i = self.dma_rr
            self.dma_rr = (self.dma_rr + 1) % (self.n_dma - 2)
        sk = "d%d" % i
        if self.dma_cnt[i] > 0 and self.waited[eng].get(sk, 0) < self.dma_cnt[i]:
            self.waited[eng][sk] = self.dma_cnt[i]
            waits.append((sk, self.dma_cnt[i]))
        self.dma_cnt[i] += 16
        tok = (sk, self.dma_cnt[i])
        self.ops[eng].append((waits, fn, (sk, 16)))
        self._commit(tok, reads, writes)
        return tok

    def final_wait(self, eng, toks):
        waits = []
        for sk, v in toks:
            if self.waited[eng].get(sk, 0) < v:
                self.waited[eng][sk] = v
                waits.append((sk, v))
        self.ops[eng].append((waits, None, None))

    def emit(self, nc, sems):
        handles = {"pe": None, "act": None, "dve": None, "pool": None, "sp": None}
        with nc.Block() as block:
            def run(eng_name):
                def _f(e):
                    for waits, fn, inc in self.ops[eng_name]:
                        for sk, v in waits:
                            e.wait_ge(sems[sk], v)
                        if fn is not None:
                            ins = fn(e)
                            ins.then_inc(sems[inc[0]], inc[1])
                return _f
            block.tensor(run("pe"))
            block.scalar(run("act"))
            block.vector(run("dve"))
            block.gpsimd(run("pool"))
            block.sync(run("sp"))


def build_program(cfg=None):
    cfg = dict(cfg or {})
    nblk = cfg.get("nblk", NBLK)
    do_s5 = cfg.get("s5", True)
    lvl = cfg.get("lvl", 9)
    plvl = cfg.get("plvl", 9)
    npre = cfg.get("npre", 4)
    nown = nblk - npre
    nc = bass.Bass("TRN2", target_bir_lowering=False)
    em = Em(strict_same=cfg.get("strict", True))
    import contextlib
    st = contextlib.ExitStack()

    def din(name, shape):
        return nc.dram_tensor(name, list(shape), F32, kind="ExternalInput").ap()

    def dscratch(name, shape, dt=BF16):
        return nc.dram_tensor(name, list(shape), dt, kind="Internal").ap()

    def sb(name, shape, dt=F32):
        return st.enter_context(nc.sbuf_tensor(name, list(shape), dt))

    x_d = din("x", [L, D])
    p_d = din("p", [nown * BLK, PLE])
    out_d = nc.dram_tensor("out", [nown * BLK, D], F32, kind="ExternalOutput").ap()
    flag_d = din("flag", [128, 1])
    fw_d = din("final_norm_w", [1, D])
    plw_d = din("ple_norm_w", [D])
    nw_d = din("norm_w", [D])
    snw_d = din("ssd_norm_w", [D])
    win_d = din("w_in", [D, DIN])
    wout_d = din("w_out", [2 * D, D])
    wgate_d = din("w_ple_gate", [D, D])
    wple_d = din("w_ple_proj", [PLE, D])
    convw_d = din("conv_w", [4, 2048])
    convb_d = din("conv_b", [2048])
    dtb_d = din("dt_bias", [1, 16])
    alog_d = din("A_log", [1, 16])
    sD_d = din("ssd_D", [1, 16])
    ident_d = din("c_ident", [128, 128])
    tri_d = din("c_tri", [128, 128])
    ones_d = din("c_ones", [128, 128])
    if do_s5:
        are_d = din("s5_A_re", [64, 64])
        aim_d = din("s5_A_im", [64, 64])
        ldt_d = din("s5_log_dt", [64])
        bre_d = din("s5_B_re", [64, 64, 16])
        bim_d = din("s5_B_im", [64, 64, 16])
        cre_d = din("s5_C_re", [64, 16, 64])
        cim_d = din("s5_C_im", [64, 16, 64])
        sD5_d = din("s5_D", [D])
        wglu_d = din("s5_w_glu", [D, D])
        bglu_d = din("s5_b_glu", [1, D])
        mask8_d = din("c_mask8", [128, 128])
        rowm_d = din("c_rowmask", [128, 2])
        wglu_b = dscratch("wglu_b", [D, D])

    win_b = dscratch("win_b", [D, DIN])
    wout_b = dscratch("wout_b", [2 * D, D])
    wgate_b = dscratch("wgate_b", [D, D])
    wple_b = dscratch("wple_b", [PLE, D])

    with st:
        sems = {}
        for e in Em.ENGS:
            sems[e] = st.enter_context(nc.semaphore("s_" + e))
        for i in range(em.n_dma):
            sems["d%d" % i] = st.enter_context(nc.semaphore("s_d%d" % i))

        def MM(out, lhsT, rhs, start, stop, r, w):
            em.op("pe", lambda e: e.matmul(out, lhsT, rhs, start=start, stop=stop), r, w)

        def ACT(out, in_, func, r, w, scale=1.0, bias=0.0, accum=None):
            if accum is None:
                em.op("act", lambda e: e.activation(out=out, in_=in_, func=func, scale=scale, bias=bias), r, w)
            else:
                em.op("act", lambda e: e.activation(out=out, in_=in_, func=func, scale=scale, bias=bias,
                                                    accum_out=accum), r, w)

        def TT(eng, out, in0, in1, op, r, w):
            em.op(eng, lambda e: e.tensor_tensor(out=out, in0=in0, in1=in1, op=op), r, w)

        def TS(eng, out, in0, s1, s2, op0, op1, r, w):
            if s2 is None:
                em.op(eng, lambda e: e.tensor_scalar(out=out, in0=in0, scalar1=s1, scalar2=None, op0=op0), r, w)
            else:
                em.op(eng, lambda e: e.tensor_scalar(out=out, in0=in0, scalar1=s1, scalar2=s2, op0=op0, op1=op1), r, w)

        def STT(out, in0, scalar, in1, op0, op1, r, w):
            em.op("dve", lambda e: e.scalar_tensor_tensor(out=out, in0=in0, scalar=scalar, in1=in1,
                                                          op0=op0, op1=op1), r, w)

        def CP(eng, out, in_, r, w):
            if eng == "act":
                em.op("act", lambda e: e.copy(out=out, in_=in_), r, w)
            else:
                em.op(eng, lambda e: e.tensor_copy(out=out, in_=in_), r, w)

        def MSET(eng, out, val, w):
            em.op(eng, lambda e: e.memset(out, val), [], w)

        def RECIP(out, in_, r, w):
            em.op("dve", lambda e: e.reciprocal(out=out, in_=in_), r, w)

        def DMA(eng, out, in_, r, w):
            return em.dma(eng, lambda e: e.dma_start(out=out, in_=in_), r, w)

        def DMAS(eng, out, in_, r, w):
            return em.dma(eng, lambda e: e.dma_start(out=out, in_=in_, allow_slow_non_contiguous=True), r, w)

        psb = [st.enter_context(nc.psum_tensor("ps%d" % i, [128, 512], F32)) for i in range(8)]
        ps_i = [0]

        def PS():
            i = ps_i[0]
            ps_i[0] = (i + 1) % 8
            return psb[i], "ps%d" % i

        RBYTES = cfg.get("rbytes", 30720)
        R = sb("R", [128, RBYTES // 4])
        Rb = R.bitcast(BF16)

        def rview(key, off, shape, dt):
            n = 1
            for s_ in shape[1:]:
                n *= s_
            if dt == F32:
                assert off % 4 == 0
                ap = R[:, off // 4: off // 4 + n]
                nb = n * 4
            else:
                assert off % 2 == 0
                ap = Rb[:, off // 2: off // 2 + n]
                nb = n * 2
            assert off + nb <= RBYTES, (key, off, nb)
            if len(shape) == 3:
                ap = ap.rearrange("p (a b) -> p a b", a=shape[1])
            elif len(shape) == 4:
                ap = ap.rearrange("p (a b c) -> p a b c", a=shape[1], b=shape[2])
            em.region(key, "R", off, off + nb)
            return ap

        ident_f = sb("ident_f", [128, 128])
        tri_f = sb("tri_f", [128, 128])
        ones_f = sb("ones_f", [128, 128])
        ident = sb("ident", [128, 128], BF16)
        DMA("sp", ident_f[:], ident_d, [], ["ident_f"])
        DMA("sp", tri_f[:], tri_d, [], ["tri_f"])
        DMA("sp", ones_f[:], ones_d, [], ["ones_f"])
        CP("dve", ident[:], ident_f[:], ["ident_f"], ["ident"])

        def TR(out, in_, r, w):
            em.op("pe", lambda e: e.transpose(out, in_, ident[:]), list(r) + ["ident"], w)

        fwb = sb("fwb", [128, D])
        DMA("sp", fwb[:], fw_d.partition_broadcast(128), [], ["fwb"])
        plw = sb("plw", [128, 8])
        nw = sb("nw", [128, 8])
        snw = sb("snw", [128, 8])
        DMAS("sp", plw[:], plw_d.rearrange("(k p) -> p k", p=128), [], ["plw"])
        DMAS("sp", nw[:], nw_d.rearrange("(k p) -> p k", p=128), [], ["nw"])
        DMAS("sp", snw[:], snw_d.rearrange("(k p) -> p k", p=128), [], ["snw"])
        cw = sb("cw", [128, 16, 4])
        cbv = sb("cbv", [128, 16])
        for k_ in range(4):
            DMAS("sp", cw[:, :, k_], convw_d[k_, :].rearrange("(c p) -> p c", p=128), [], ["cw"])
        DMAS("sp", cbv[:], convb_d.rearrange("(c p) -> p c", p=128), [], ["cbv"])
        dtb = sb("dtb", [128, 16])
        Abc = sb("Abc", [128, 16])
        Dhb = sb("Dhb", [128, 16])
        DMA("sp", dtb[:], dtb_d.partition_broadcast(128), [], ["dtb"])
        DMA("sp", Abc[:], alog_d.partition_broadcast(128), [], ["Abc"])
        DMA("sp", Dhb[:], sD_d.partition_broadcast(128), [], ["Dhb"])
        ACT(Abc[:], Abc[:], AF.Exp, ["Abc"], ["Abc"])
        TS("dve", Abc[:], Abc[:], -1.0, None, ALU.mult, None, ["Abc"], ["Abc"])

        SW = 2568
        stg = [rview("stg0", 0, [128, SW], BF16), rview("stg1", SW * 2, [128, SW], BF16)]
        stg_i = [0]

        def precast(src, dst, rows, cols, key):
            rpp = rows // 128
            sv = src.rearrange("(p r) c -> p r c", p=128)
            dv = dst.rearrange("(p r) c -> p r c", p=128)
            if cols <= SW:
                rstep = max(1, SW // cols)
                for r0 in range(0, rpp, rstep):
                    r1 = min(rpp, r0 + rstep)
                    i = stg_i[0]
                    stg_i[0] ^= 1
                    n = (r1 - r0) * cols
                    sview = stg[i][:, 0:n].rearrange("p (r c) -> p r c", c=cols)
                    DMA("pool", sview, sv[:, r0:r1, :], [], ["stg%d" % i])
                    DMA("sp", dv[:, r0:r1, :], sview, ["stg%d" % i], [key])
            else:
                for r0 in range(rpp):
                    for c0 in range(0, cols, SW):
                        c1 = min(cols, c0 + SW)
                        i = stg_i[0]
                        stg_i[0] ^= 1
                        sview = stg[i][:, 0:c1 - c0]
                        DMA("pool", sview, sv[:, r0, c0:c1], [], ["stg%d" % i])
                        DMA("sp", dv[:, r0, c0:c1], sview, ["stg%d" % i], [key])

        precast(win_d, win_b, D, DIN, "win_b")
        precast(wout_d, wout_b, 2 * D, D, "wout_b")
        precast(wgate_d, wgate_b, D, D, "wgate_b")
        precast(wple_d, wple_b, PLE, D, "wple_b")
        if do_s5:
            precast(wglu_d, wglu_b, D, D, "wglu_b")

        xt = sb("xt", [128, 8, D])
        em.region("xt", "XT", 0, 32768)
        xt_f = xt[:].rearrange("p t d -> p (t d)")
        xt_b = xt.bitcast(BF16)[:].rearrange("p t d -> p (t d)")

        def xview(key, off, shape, dt):
            n = 1
            for s_ in shape[1:]:
                n *= s_
            if dt == F32:
                ap = xt_f[:, off // 4: off // 4 + n]
                nb = n * 4
            else:
                ap = xt_b[:, off // 2: off // 2 + n]
                nb = n * 2
            assert off + nb <= 32768
            if len(shape) == 3:
                ap = ap.rearrange("p (a b) -> p a b", a=shape[1])
            elif len(shape) == 4:
                ap = ap.rearrange("p (a b c) -> p a b c", a=shape[1], b=shape[2])
            em.region(key, "XT", off, off + nb)
            return ap
        sq = sb("sq", [128, D], BF16)
        ss = sb("ss", [128, 8])
        rr = sb("rr", [128, 8])
        mix = sb("mix", [128, 16, 8, 128], BF16)
        mixA = mix[:, 0:8, :, :]
        hs = mix[:, 0:8, :, :].rearrange("p k t j -> p (k t j)").rearrange("p (t d) -> p t d", t=8)
        hn = sb("hn", [128, 8, BLK], BF16)
        NWB = cfg.get("nwb", 3)
        wbs = [sb("wb%d" % i, [128, 8, 512], BF16) for i in range(NWB)]
        wb_i = [0]
        wdt = sb("wdt", [128, 8, 16], BF16)
        DMAS("sp", wdt[:], win_b[:, 5120:5136].rearrange("(k p) c -> p k c", p=128), ["win_b"], ["wdt"])
        STf = sb("STf", [128, D])
        STb = sb("STb", [128, D], BF16)
        halo = sb("halo", [128, 16, 3], BF16)
        MSET("dve", STf[:], 0.0, ["STf"])
        MSET("dve", STb[:], 0.0, ["STb"])
        MSET("dve", halo[:], 0.0, ["halo"])
        dtr = sb("dtr", [128, 8, 16])
        dtv = sb("dtv", [128, 8, 16])
        av = sb("av", [128, 8, 16])
        acs = sb("acs", [128, 16])
        t16 = sb("t16", [128, 16])
        dec = sb("dec", [128, 16])
        el = sb("el", [128, 16])
        etot = sb("etot", [128, 16])
        ssg = sb("ssg", [128, 4])
        tri_b = sb("tri_b", [128, 128], BF16)
        CP("dve", tri_b[:], tri_f[:], ["tri_f"], ["tri_b"])
        ahi = sb("ahi", [128, 8, 16], BF16)
        alo = sb("alo", [128, 8, 16], BF16)
        alf = sb("alf", [128, 8, 16])
        rg = sb("rg", [128, 4])
        flg = sb("flg", [128, 1])
        DMA("sp", flg[:], flag_d, [], ["flg"])

        dq = [0]

        def load_w(src, col0, ncols, k0, nk, key):
            i = wb_i[0]
            wb_i[0] = (i + 1) % NWB
            view = wbs[i][:, 0:nk, 0:ncols]
            eng = "sp"
            DMA(eng, view, src[k0 * 128:(k0 + nk) * 128, col0:col0 + ncols].rearrange("(k p) c -> p k c", p=128),
                [key], ["wb%d" % i])
            return view, "wb%d" % i


        hn4 = hn[:].rearrange("p k (j t) -> p k t j", t=8)
        u8 = mix[:, 8:16, :, :].rearrange("p k t j -> p (k t j)").rearrange("p (t d) -> p t d", t=8)
        u8g = mix[:, 8:16, :, :].rearrange("p k t j -> p (k t j)").rearrange("p (g t h) -> p g t h", g=64, t=8)
        if do_s5:
            Q0r = sb("Q0r", [128, 32, 128], BF16)
            Q0i = sb("Q0i", [128, 32, 128], BF16)
            R0a = sb("R0a", [128, 64, 128], BF16)
            P0r = sb("P0r", [128, 32, 128], BF16)
            NP0i = sb("NP0i", [128, 32, 128], BF16)
            A1 = sb("A1", [128, 2, 32])
            A2 = sb("A2", [128, 2, 32])
            s5c = sb("s5c", [128, 2, 32])
            rt1 = sb("rt1", [128, 2, 32])
            rt2 = sb("rt2", [128, 2, 32])
            bglu_bf = sb("bglu_bf", [1, D], BF16)
            ones_bf = sb("ones_bf", [1, 128], BF16)
            Dcol = rview("Dcol", 22528, [128, 64], F32)
            mask8 = rview("mask8", 22528 + 256, [128, 128], F32)
            rowm = sb("rowm", [128, 2])
            DMA("sp", rowm[:], rowm_d, [], ["rowm"])
            sl = rview("sl", 24576, [128, 19, 32], F32)
            l2 = sb("l2", [32, 2])
            MSET("dve", s5c[:], 0.0, ["s5c"])
            MSET("dve", ones_bf[:], 1.0, ["ones_bf"])
            bglu_f = xview("bglu_f", 28672, [128, D], F32)
            DMA("sp", bglu_f[0:1, :], bglu_d, [], ["bglu_f"])
            CP("dve", bglu_bf[:], bglu_f[0:1, :], ["bglu_f"], ["bglu_bf"])
            DMA("sp", mask8, mask8_d, [], ["mask8"])
            for s_ in range(8):
                DMAS("sp", Dcol[s_ * 16:(s_ + 1) * 16, :], sD5_d.rearrange("(g h) -> h g", h=16), [], ["Dcol"])
            XA = rview("XA", 0, [128, 3, 128], F32)
            POWr = rview("POWr", 1536, [128, 32, 24], F32)
            POWi = rview("POWi", 1536 + 3072, [128, 32, 24], F32)
            Bslr = rview("Bslr", 8192, [128, 32, 16], F32)
            Bsli = rview("Bsli", 8192 + 2048, [128, 32, 16], F32)
            Bbr = rview("Bbr", 8192 + 4096, [128, 32, 16], F32)
            Bbi = rview("Bbi", 8192 + 6144, [128, 32, 16], F32)
            Cslr = rview("Cslr", 16384, [128, 32, 16], F32)
            Csli = rview("Csli", 16384 + 2048, [128, 32, 16], F32)
            hnf = hn.bitcast(F32)[:].rearrange("p k c -> p (k c)")
            xtf = xt[:].rearrange("p t d -> p (t d)")
            T1 = xtf[:, 0:4096].rearrange("p (g s h) -> p g s h", g=32, s=8)
            T2 = xtf[:, 4096:8192].rearrange("p (g s h) -> p g s h", g=32, s=8)
            tmpR = xtf[:, 0:512].rearrange("p (g l) -> p g l", g=4)
            mixf = mix[:].rearrange("p k t j -> p (k t j)")
            Qstr = mixf[:, 0:4096].rearrange("p (g c) -> p g c", g=32)
            Qsti = mixf[:, 4096:8192].rearrange("p (g c) -> p g c", g=32)
            P0mr = mixf[:, 8192:12288].rearrange("p (g c) -> p g c", g=32)
            NP0mi = mixf[:, 12288:16384].rearrange("p (g c) -> p g c", g=32)

            def SL(i):
                return sl[:, i, :]
            SLK = ["sl"]
            em.enabled = plvl >= 1
            DMA("sp", XA[0:32, 0, :], are_d.rearrange("(gp g2) n -> gp (g2 n)", g2=2), [], ["XA"])
            DMA("sp", XA[0:32, 1, :], aim_d.rearrange("(gp g2) n -> gp (g2 n)", g2=2), [], ["XA"])
            DMAS("sp", l2[:], ldt_d.rearrange("(gp g2) -> gp g2", g2=2), [], ["l2"])
            CP("dve", XA[0:32, 2, :].rearrange("p (a n) -> p a n", a=2), l2[:].unsqueeze(2).to_broadcast([32, 2, 64]),
               ["l2"], ["XA"])
            pA, pAk = PS()
            for i in range(3):
                em.op("pe", lambda e, i=i: e.transpose(pA[:, i * 32:(i + 1) * 32], XA[0:32, i, :], ident_f[0:32, 0:32]),
                      ["XA", "ident_f"], [pAk])
            iAr, iAi, iLd, iDt, iArd, iAid, iZr, iZi, iT, iU, iV, iCr, iCi, iLr1, iNr, iNi, iDen, iPr, iPi = range(19)
            for i in range(3):
                CP("dve", SL(i), pA[:, i * 32:(i + 1) * 32], [pAk], SLK)
            ACT(SL(iDt), SL(iLd), AF.Exp, SLK, SLK)
            TT("dve", SL(iArd), SL(iAr), SL(iDt), ALU.mult, SLK, SLK)
            TT("dve", SL(iAid), SL(iAi), SL(iDt), ALU.mult, SLK, SLK)
            ACT(SL(iT), SL(iArd), AF.Exp, SLK, SLK, scale=1.0 / 32)
            ACT(SL(iU), SL(iAid), AF.Sin, SLK, SLK, scale=1.0 / 32)
            ACT(SL(iV), SL(iAid), AF.Sin, SLK, SLK, scale=1.0 / 32, bias=math.pi / 2)
            TT("dve", SL(iZr), SL(iT), SL(iV), ALU.mult, SLK, SLK)
            TT("dve", SL(iZi), SL(iT), SL(iU), ALU.mult, SLK, SLK)
            for _ in range(5):
                TT("dve", SL(iT), SL(iZr), SL(iZr), ALU.mult, SLK, SLK)
                TT("dve", SL(iU), SL(iZi), SL(iZi), ALU.mult, SLK, SLK)
                TT("dve", SL(iV), SL(iZr), SL(iZi), ALU.mult, SLK, SLK)
                TT("dve", SL(iZr), SL(iT), SL(iU), ALU.subtract, SLK, SLK)
                TS("dve", SL(iZi), SL(iV), 2.0, None, ALU.mult, None, SLK, SLK)
            PK = ["POWr", "POWi"]
            CP("dve", SL(iPr), SL(iZr), SLK, SLK)
            CP("dve", SL(iPi), SL(iZi), SLK, SLK)
            MSET("dve", POWr[:, :, 15], 1.0, PK)
            MSET("dve", POWi[:, :, 15], 0.0, PK)
            MSET("dve", POWr[:, :, 23], 1.0, PK)
            MSET("dve", POWi[:, :, 23], 0.0, PK)
            for k in range(1, 9):
                if k > 1:
                    TT("dve", SL(iT), SL(iPr), SL(iZr), ALU.mult, SLK, SLK)
                    TT("dve", SL(iU), SL(iPi), SL(iZi), ALU.mult, SLK, SLK)
                    TT("dve", SL(iV), SL(iPr), SL(iZi), ALU.mult, SLK, SLK)
                    TT("dve", SL(iPr), SL(iT), SL(iU), ALU.subtract, SLK, SLK)
                    TT("dve", SL(iT), SL(iPi), SL(iZr), ALU.mult, SLK, SLK)
                    TT("dve", SL(iPi), SL(iT), SL(iV), ALU.add, SLK, SLK)
                CP("dve", POWr[:, :, k - 1], SL(iPr), SLK, PK)
                CP("dve", POWi[:, :, k - 1], SL(iPi), SLK, PK)
                if k <= 7:
                    CP("dve", POWr[:, :, 8 + 7 - k], SL(iPr), SLK, PK)
                    CP("dve", POWi[:, :, 8 + 7 - k], SL(iPi), SLK, PK)
                    TT("dve", SL(iT), SL(iPr), SL(iPr), ALU.mult, SLK, SLK)
                    TT("dve", SL(iU), SL(iPi), SL(iPi), ALU.mult, SLK, SLK)
                    TT("dve", SL(iT), SL(iT), SL(iU), ALU.add, SLK, SLK)
                    RECIP(SL(iT), SL(iT), SLK, SLK)
                    TT("dve", POWr[:, :, 16 + 7 - k], SL(iPr), SL(iT), ALU.mult, SLK, PK)
                    STT(POWi[:, :, 16 + 7 - k], SL(iPi), -1.0, SL(iT), ALU.mult, ALU.mult, SLK, PK)
            CP("dve", A1[:, 0, :], POWr[:, :, 7], PK, ["A1"])
            CP("dve", A1[:, 1, :], POWr[:, :, 7], PK, ["A1"])
            TS("dve", A2[:, 0, :], POWi[:, :, 7], -1.0, None, ALU.mult, None, PK, ["A2"])
            CP("dve", A2[:, 1, :], POWi[:, :, 7], PK, ["A2"])
            A1T = sb("A1T", [128, 8, 2, 32])
            A2T = sb("A2T", [128, 8, 2, 32])
            CP("dve", SL(iPr), POWr[:, :, 7], PK, SLK)
            CP("dve", SL(iPi), POWi[:, :, 7], PK, SLK)
            for l_ in range(8):
                if l_ > 0:
                    TT("dve", SL(iT), SL(iPr), SL(iPr), ALU.mult, SLK, SLK)
                    TT("dve", SL(iU), SL(iPi), SL(iPi), ALU.mult, SLK, SLK)
                    TT("dve", SL(iV), SL(iPr), SL(iPi), ALU.mult, SLK, SLK)
                    TT("dve", SL(iPr), SL(iT), SL(iU), ALU.subtract, SLK, SLK)
                    TS("dve", SL(iPi), SL(iV), 2.0, None, ALU.mult, None, SLK, SLK)
                CP("dve", A1T[:, l_, 0, :], SL(iPr), SLK, ["A1T"])
                CP("dve", A1T[:, l_, 1, :], SL(iPr), SLK, ["A1T"])
                TS("dve", A2T[:, l_, 0, :], SL(iPi), -1.0, None, ALU.mult, None, SLK, ["A2T"])
                CP("dve", A2T[:, l_, 1, :], SL(iPi), SLK, ["A2T"])
            TS("dve", SL(iLr1), SL(iZr), -1.0, None, ALU.add, None, SLK, SLK)
            TT("dve", SL(iT), SL(iLr1), SL(iAr), ALU.mult, SLK, SLK)
            TT("dve", SL(iU), SL(iZi), SL(iAi), ALU.mult, SLK, SLK)
            TT("dve", SL(iNr), SL(iT), SL(iU), ALU.add, SLK, SLK)
            TT("dve", SL(iT), SL(iZi), SL(iAr), ALU.mult, SLK, SLK)
            TT("dve", SL(iU), SL(iLr1), SL(iAi), ALU.mult, SLK, SLK)
            TT("dve", SL(iNi), SL(iT), SL(iU), ALU.subtract, SLK, SLK)
            TT("dve", SL(iT), SL(iAr), SL(iAr), ALU.mult, SLK, SLK)
            TT("dve", SL(iU), SL(iAi), SL(iAi), ALU.mult, SLK, SLK)
            TT("dve", SL(iDen), SL(iT), SL(iU), ALU.add, SLK, SLK)
            RECIP(SL(iDen), SL(iDen), SLK, SLK)
            TT("dve", SL(iCr), SL(iNr), SL(iDen), ALU.mult, SLK, SLK)
            TT("dve", SL(iCi), SL(iNi), SL(iDen), ALU.mult, SLK, SLK)

            def bc_pow(P, lo):
                return P[:, :, lo:lo + 8].unsqueeze(3).to_broadcast([128, 32, 8, 16])

            def bc_v(V):
                return V.unsqueeze(2).to_broadcast([128, 32, 8, 16])

            def cplx_table(lo, Vr, Vi, vk, outr, outi, okr, oki, neg_i):
                o4r = outr.rearrange("p g (s h) -> p g s h", s=8)
                o4i = outi.rearrange("p g (s h) -> p g s h", s=8)
                TT("dve", T1, bc_pow(POWr, lo), bc_v(Vr), ALU.mult, PK + vk, ["xt"])
                TT("pool", T2, bc_pow(POWi, lo), bc_v(Vi), ALU.mult, PK + vk, ["sq"])
                TT("dve", o4r, T1, T2, ALU.subtract, ["xt", "sq"], okr)
                TT("dve", T1, bc_pow(POWr, lo), bc_v(Vi), ALU.mult, PK + vk, ["xt"])
                TT("pool", T2, bc_pow(POWi, lo), bc_v(Vr), ALU.mult, PK + vk, ["sq"])
                if neg_i:
                    TT("dve", T1, T1, T2, ALU.add, ["xt", "sq"], ["xt"])
                    TS("dve", o4i, T1, -1.0, None, ALU.mult, None, ["xt"], oki)
                else:
                    TT("dve", o4i, T1, T2, ALU.add, ["xt", "sq"], oki)

            em.enabled = plvl >= 2
            for a_ in range(2):
                DMA("sp", hnf[0:32, 0:2048].rearrange("p (h a n) -> p h a n", h=16, a=2)[:, :, a_, :],
                    cre_d.rearrange("(gp g2) h n -> gp g2 h n", g2=2)[:, a_, :, :], [], ["hn"])
                DMA("act", hnf[0:32, 2048:4096].rearrange("p (h a n) -> p h a n", h=16, a=2)[:, :, a_, :],
                    cim_d.rearrange("(gp g2) h n -> gp g2 h n", g2=2)[:, a_, :, :], [], ["hn"])
            for comp, dst, dk in ((0, Cslr, "Cslr"), (1, Csli, "Csli")):
                pC, pCk = PS()
                src4 = hnf[0:32, comp * 2048:(comp + 1) * 2048].rearrange("p (h c) -> p h c", h=16)
                for h_ in range(16):
                    em.op("pe", lambda e, h_=h_, src4=src4, pC=pC: e.transpose(
                        pC[:, h_ * 32:(h_ + 1) * 32], src4[:, h_, :], ident_f[0:32, 0:32]),
                          ["hn", "ident_f"], [pCk])
                CP("dve", dst, pC[:].rearrange("p (h g) -> p g h", h=16), [pCk], [dk])
            cplx_table(0, Cslr, Csli, ["Cslr", "Csli"], P0r[:], NP0i[:], ["P0r"], ["NP0i"], True)
            cplx_table(16, Cslr, Csli, ["Cslr", "Csli"], P0mr, NP0mi, ["mixB"], ["mixB"], True)
            em.enabled = plvl >= 3
            DMA("sp", hnf[0:32, 0:2048], bre_d.rearrange("(gp g2) n h -> gp (g2 n h)", g2=2), [], ["hn"])
            DMA("act", hnf[0:32, 2048:4096], bim_d.rearrange("(gp g2) n h -> gp (g2 n h)", g2=2), [], ["hn"])
            for comp, dst, dk in ((0, Bslr, "Bslr"), (1, Bsli, "Bsli")):
                pB, pBk = PS()
                src3 = hnf[0:32, comp * 2048:(comp + 1) * 2048].rearrange("p (c h) -> p c h", h=16)
                xb2 = xtf[0:32, comp * 2048:(comp + 1) * 2048].rearrange("p (h c) -> p h c", h=16)
                CP("dve", xb2.rearrange("p h c -> p c h"), src3, ["hn"], ["xt"])
                for h_ in range(16):
                    em.op("pe", lambda e, h_=h_, xb2=xb2, pB=pB: e.transpose(pB[:, h_ * 32:(h_ + 1) * 32],
                                                                           xb2[:, h_, :], ident_f[0:32, 0:32]),
                          ["xt", "ident_f"], [pBk])
                CP("dve", dst, pB[:].rearrange("p (h g) -> p g h", h=16), [pBk], [dk])
            crb = SL(iCr).unsqueeze(2).to_broadcast([128, 32, 16])
            cib = SL(iCi).unsqueeze(2).to_broadcast([128, 32, 16])
            T1s = xtf[:, 0:512].rearrange("p (g h) -> p g h", g=32)
            T2s = xtf[:, 512:1024].rearrange("p (g h) -> p g h", g=32)
            TT("dve", T1s, Bslr, crb, ALU.mult, ["Bslr"] + SLK, ["xt"])
            TT("dve", T2s, Bsli, cib, ALU.mult, ["Bsli"] + SLK, ["xt"])
            TT("dve", Bbr, T1s, T2s, ALU.subtract, ["xt"], ["Bbr"])
            TT("dve", T1s, Bsli, crb, ALU.mult, ["Bsli"] + SLK, ["xt"])
            TT("dve", T2s, Bslr, cib, ALU.mult, ["Bslr"] + SLK, ["xt"])
            TT("dve", Bbi, T1s, T2s, ALU.add, ["xt"], ["Bbi"])
            cplx_table(8, Bbr, Bbi, ["Bbr", "Bbi"], Qstr, Qsti, ["mixA"], ["mixA"], False)
            em.enabled = plvl >= 4
            for comp, src, dst, dk in ((0, Qstr, Q0r, "Q0r"), (1, Qsti, Q0i, "Q0i")):
                for g8 in range(4):
                    pq, pqk = PS()
                    pqv = pq.bitcast(BF16)[:].rearrange("p (g c) -> p g c", g=8)
                    for gi in range(8):
                        TR(pqv[:, gi, :], src[:, g8 * 8 + gi, :], ["mixA"], [pqk])
                    CP("act", dst[:, g8 * 8:(g8 + 1) * 8, :], pqv, [pqk], [dk])
            em.enabled = plvl >= 5
            tmpP = [rview("tmpP0", 20480, [128, 4, 128], BF16), rview("tmpP1", 21504, [128, 4, 128], BF16)]
            for g4 in range(16):
                pR, pRk = PS()
                for gi in range(4):
                    g = g4 * 4 + gi
                    gp, g2 = divmod(g, 2)
                    tp = tmpP[gp % 2]
                    tpk = "tmpP%d" % (gp % 2)
                    if g2 == 0:
                        for a_ in range(2):
                            TS("dve", tp[:, a_, :], P0mr[:, gp, :], rowm[:, a_:a_ + 1], None, ALU.mult, None,
                               ["mixB", "rowm"], [tpk])
                            TS("dve", tp[:, 2 + a_, :], NP0mi[:, gp, :], rowm[:, a_:a_ + 1], None, ALU.mult, None,
                               ["mixB", "rowm"], [tpk])
                    MM(pR[:, gi * 128:(gi + 1) * 128], Qstr[:, gp, :], tp[:, g2, :], True, False,
                       ["mixA", tpk], [pRk])
                    MM(pR[:, gi * 128:(gi + 1) * 128], Qsti[:, gp, :], tp[:, 2 + g2, :], False, True,
                       ["mixA", tpk], [pRk])
                TT("dve", tmpR, pR[:].rearrange("p (g l) -> p g l", g=4),
                   mask8.unsqueeze(1).to_broadcast([128, 4, 128]), ALU.mult, [pRk, "mask8"], ["xt"])
                for gi in range(4):
                    g = g4 * 4 + gi
                    STT(R0a[:, g, :], ident_f[:], Dcol[:, g:g + 1], tmpR[:, gi, :], ALU.mult, ALU.add,
                        ["ident_f", "Dcol", "xt"], ["R0a"])

            em.enabled = True
            U8T = rview("U8T", 0, [128, 64, 128], BF16)
            y1fm = rview("y1fm", 0, [128, 8, 8, 128], BF16)
            Vx2 = xview("Vx2", 0, [128, 2, 32, 128], F32)
            Gs = [rview("Gs0", 16384, [128, 2, 4, 128], BF16),
                  rview("Gs1", 16384 + 2048, [128, 2, 4, 128], BF16)]
            sg5 = rview("sg5", 16384 + 4096, [128, 512], F32)
            trt = rview("trt", 0, [128, 2, 32, 64], F32)
            zs5 = rview("zs5", 16384 + 6144, [128, 512], F32)

        o = 0
        xc = xview("xc", 0, [128, 16, 512], BF16)
        pre = [rview("pre0", o, [128, 516], BF16), rview("pre1", o + 1032, [128, 516], BF16)]; o += 2064
        dg = [rview("dg0", o, [128, 4, 128], BF16), rview("dg1", o + 1024, [128, 4, 128], BF16)]; o += 2048
        zsb = xview("zsb", 16384, [128, 4, D], BF16)
        abc_l = [rview("abc0", o, [128, 2, 4, 128], BF16), xview("abc1", 24576, [128, 2, 4, 128], BF16)]; o += 2048
        dm_l = [rview("dm0", o, [128, 4, 128], F32), xview("dm1", 24576 + 2048, [128, 4, 128], F32)]; o += 2048
        Ee_l = [rview("Ee0", o, [128, 4, 128], F32), xview("Ee1", 24576 + 4096, [128, 4, 128], F32)]; o += 2048
        Mh_l = [rview("Mh0", o, [128, 4, 128], BF16), xview("Mh1", 24576 + 6144, [128, 4, 128], BF16)]; o += 1024
        GTm = rview("GTm", o, [128, 4, 128], F32); o += 2048
        xd = rview("xd", o, [128, 16, 64], BF16); o += 2048
        xdd = rview("xdd", o, [128, 16, 64], BF16); o += 2048
        xD = rview("xD", o, [128, 16, 64], BF16); o += 2048
        yv = rview("yv", o, [128, D], F32); o += 4096
        gn = rview("gn", o, [128, D], BF16); o += 2048
        Bc = rview("Bc", o, [128, 4, 128], BF16); o += 1024
        o = 0
        pt = rview("pt", o, [128, 8, PLE], F32); o += 8192
        pbf = rview("pbf", o, [128, 8, PLE], BF16); o += 4096
        p_fm = rview("p_fm", o, [128, 2, 8, 128], BF16); o += 4096
        gsig = rview("gsig", o, [128, 512], F32); o += 2048
        tmp2 = rview("tmp2", o, [128, 512], F32); o += 2048

        def rms_rr(src_key):
            for t in range(8):
                ACT(sq[:], xt[:, t, :], AF.Square, [src_key], ["sq", "ss"], accum=ss[:, t:t + 1])
            ACT(rr[:], ss[:], AF.Sqrt, ["ss"], ["rr"], scale=1.0 / D, bias=EPS)
            RECIP(rr[:], rr[:], ["rr"], ["rr"])

        out_toks = []
        for b in range(nblk):
            t0 = b * BLK
            full = b >= npre
            o0 = (b - npre) * BLK
            if b == npre and npre > 0:
                TS("dve", STf[:], STf[:], flg[:, 0:1], None, ALU.mult, None, ["STf", "flg"], ["STf"])
                TS("dve", STb[:], STb[:], flg[:, 0:1], None, ALU.mult, None, ["STb", "flg"], ["STb"])
                TS("dve", halo[:], halo[:], flg[:, 0:1], None, ALU.mult, None, ["halo", "flg"], ["halo"])
                if do_s5:
                    TS("dve", s5c[:], s5c[:], flg[:, 0:1], None, ALU.mult, None, ["s5c", "flg"], ["s5c"])
            DMA("sp", xt[:], x_d[t0:t0 + BLK, :].rearrange("(j t) d -> j t d", t=8), [], ["xt"])
            rms_rr("xt")
            for t in range(8):
                TS("dve", hs[:, t, :], xt[:, t, :], rr[:, t:t + 1], None, ALU.mult, None, ["xt", "rr"], ["mixA"])
            for kt in range(8):
                pt_, pk = PS()
                pv = pt_.bitcast(BF16)[:].rearrange("p (t j) -> p t j", t=8)
                for t in range(8):
                    TR(pv[:, t, :], hs[:, t, kt * 128:(kt + 1) * 128], ["mixA"], [pk])
                TS("dve", hn[:, kt, :].rearrange("p (j t) -> p t j", t=8), pv, nw[:, kt:kt + 1], None,
                   ALU.mult, None, [pk, "nw"], ["hn"])
            if not do_s5:
                MSET("pool", mixA, 0.0, ["mixA"])

            if do_s5 and lvl >= 1:
                for cb in range(2):
                    cs = slice(cb * 512, (cb + 1) * 512)
                    wv, wk = load_w(win_b, cb * 512, 512, 0, 8, "win_b")
                    for t in range(8):
                        pu, puk = PS()
                        for kt in range(8):
                            MM(pu[:], hn4[:, kt, t, :], wv[:, kt, :], kt == 0, kt == 7, ["hn", wk], [puk])
                        CP("act", u8g[:, cb * 32:(cb + 1) * 32, t, :], pu[:].rearrange("p (g h) -> p g h", h=16),
                           [puk], ["mixB"])
                for g8 in range(8):
                    pq, pqk = PS()
                    pqv = pq.bitcast(BF16)[:].rearrange("p (g c) -> p g c", g=8)
                    for gi in range(8):
                        g = g8 * 8 + gi
                        TR(pqv[:, gi, :], u8g[:, g, :, :].rearrange("p t h -> p (t h)"), ["mixB"], [pqk])
                    CP("act" if g8 % 2 else "dve", U8T[:, g8 * 8:(g8 + 1) * 8, :], pqv, [pqk], ["U8T"])
                if lvl >= 2:
                    for gp4 in range(8):
                        pvr, pvrk = PS()
                        pvi, pvik = PS()
                        for gq in range(4):
                            gp = gp4 * 4 + gq
                            for g2 in range(2):
                                g = 2 * gp + g2
                                rows = slice(g2 * 64, (g2 + 1) * 64)
                                MM(pvr[rows, gq * 128:(gq + 1) * 128], Q0r[:, gp, rows], U8T[:, g, :], True, True,
                                   ["Q0r", "U8T"], [pvrk])
                                MM(pvi[rows, gq * 128:(gq + 1) * 128], Q0i[:, gp, rows], U8T[:, g, :], True, True,
                                   ["Q0i", "U8T"], [pvik])
                        CP("act", Vx2[:, 0, gp4 * 4:(gp4 + 1) * 4, :], pvr[:].rearrange("p (g j) -> p g j", g=4),
                           [pvrk], ["Vx2"])
                        CP("dve", Vx2[:, 1, gp4 * 4:(gp4 + 1) * 4, :], pvi[:].rearrange("p (g j) -> p g j", g=4),
                           [pvik], ["Vx2"])
                if lvl >= 3 and not full:
                    for l_ in range(7):
                        s_ = 1 << l_
                        n_ = 64 >> l_
                        Xa = Vx2[:, :, :, s_ - 1::2 * s_]
                        Xb = Vx2[:, :, :, 2 * s_ - 1::2 * s_]
                        tv = trt[:, :, :, 0:n_]
                        TT("dve", tv, Xa, A1T[:, l_, :, :].unsqueeze(3).to_broadcast([128, 2, 32, n_]), ALU.mult,
                           ["Vx2", "A1T"], ["trt"])
                        TT("dve", Xb, Xb, tv, ALU.add, ["Vx2", "trt"], ["Vx2"])
                        TT("dve", tv[:, 0, :, :], Xa[:, 1, :, :],
                           A2T[:, l_, 0, :].unsqueeze(2).to_broadcast([128, 32, n_]), ALU.mult, ["Vx2", "A2T"], ["trt"])
                        TT("dve", tv[:, 1, :, :], Xa[:, 0, :, :],
                           A2T[:, l_, 1, :].unsqueeze(2).to_broadcast([128, 32, n_]), ALU.mult, ["Vx2", "A2T"], ["trt"])
                        TT("dve", Xb, Xb, tv, ALU.add, ["Vx2", "trt"], ["Vx2"])
                    TT("dve", rt1[:], s5c[:], A1T[:, 7, :, :], ALU.mult, ["s5c", "A1T"], ["rt1"])
                    TT("dve", rt2[:, 0, :], s5c[:, 1, :], A2T[:, 7, 0, :], ALU.mult, ["s5c", "A2T"], ["rt2"])
                    TT("dve", rt2[:, 1, :], s5c[:, 0, :], A2T[:, 7, 1, :], ALU.mult, ["s5c", "A2T"], ["rt2"])
                    TT("dve", rt1[:], rt1[:], rt2[:], ALU.add, ["rt1", "rt2"], ["rt1"])
                    TT("dve", s5c[:], Vx2[:, :, :, 127], rt1[:], ALU.add, ["Vx2", "rt1"], ["s5c"])
                if lvl >= 3 and full:
                    for j in range(128):
                        if j == 0:
                            Gj, Gjr, Gji, gk = s5c[:], s5c[:, 0, :], s5c[:, 1, :], "s5c"
                        else:
                            Gj, Gjr, Gji, gk = Vx2[:, :, :, j - 1], Vx2[:, 0, :, j - 1], Vx2[:, 1, :, j - 1], "Vx2"
                        TT("dve", rt1[:], Gj, A1[:], ALU.mult, [gk, "A1"], ["rt1"])
                        TT("dve", rt2[:, 0, :], Gji, A2[:, 0, :], ALU.mult, [gk, "A2"], ["rt2"])
                        TT("dve", rt2[:, 1, :], Gjr, A2[:, 1, :], ALU.mult, [gk, "A2"], ["rt2"])
                        TT("dve", Vx2[:, :, :, j], Vx2[:, :, :, j], rt1[:], ALU.add, ["Vx2", "rt1"], ["Vx2"])
                        TT("dve", Vx2[:, :, :, j], Vx2[:, :, :, j], rt2[:], ALU.add, ["Vx2", "rt2"], ["Vx2"])
                if lvl >= 4 and full:
                    for g4 in range(16):
                        gs = Gs[g4 % 2]
                        gsk = "Gs%d" % (g4 % 2)
                        if g4 < 2:
                            MSET("pool", gs, 0.0, [gsk])
                        for a_ in range(2):
                            rws = slice(a_ * 64, (a_ + 1) * 64)
                            gsv = gs.rearrange("p r (q a) j -> p r q a j", a=2)
                            CP("act" if a_ else "dve", gsv[rws, :, :, a_, 1:128], Vx2[rws, :, g4 * 2:(g4 + 1) * 2, 0:127],
                               ["Vx2"], [gsk])
                            CP("dve" if a_ else "act", gsv[rws, :, :, a_, 0], s5c[rws, :, g4 * 2:(g4 + 1) * 2], ["s5c"], [gsk])
                        py_, pyk = PS()
                        for gi in range(4):
                            g = g4 * 4 + gi
                            gp, g2 = divmod(g, 2)
                            rows = slice(g2 * 64, (g2 + 1) * 64)
                            osl = py_[:, gi * 128:(gi + 1) * 128]
                            MM(osl, U8T[:, g, :], R0a[:, g, :], True, False, ["U8T", "R0a"], [pyk])
                            MM(osl, gs[:, 0, gi, :], P0r[:, gp, :], False, False, [gsk, "P0r"], [pyk])
                            MM(osl, gs[:, 1, gi, :], NP0i[:, gp, :], False, True, [gsk, "NP0i"], [pyk])
                        ACT(u8[:, :, 64 * g4:64 * (g4 + 1)].rearrange("p t (g h) -> p g t h", g=4),
                            py_[:].rearrange("p (g t h) -> p g t h", g=4, t=8), AF.Gelu_apprx_tanh, [pyk], ["mixB"])
                    CP("dve", s5c[:], Vx2[:, :, :, 127], ["Vx2"], ["s5c"])
                if lvl >= 9 and full:
                    for kt in range(8):
                        pt_, pk = PS()
                        pv = pt_.bitcast(BF16)[:].rearrange("p (t j) -> p t j", t=8)
                        for t in range(8):
                            TR(pv[:, t, :], u8[:, t, kt * 128:(kt + 1) * 128], ["mixB"], [pk])
                        CP("act" if kt % 2 else "dve", y1fm[:, kt, :, :], pv, [pk], ["y1fm"])
                    for cb in range(2):
                        cs = slice(cb * 512, (cb + 1) * 512)
                        wv, wk = load_w(wglu_b, cb * 512, 512, 0, 8, "wglu_b")
                        wz, wzk = load_w(win_b, 1024 + cb * 512, 512, 0, 8, "win_b")
                        for t in range(8):
                            pg, pgk = PS()
                            for kt in range(8):
                                MM(pg[:], y1fm[:, kt, t, :], wv[:, kt, :], kt == 0, False, ["y1fm", wk], [pgk])
                            MM(pg[:], ones_bf[0:1, :], bglu_bf[0:1, cs], False, True, ["ones_bf", "bglu_bf"], [pgk])
                            pz, pzk = PS()
                            for kt in range(8):
                                MM(pz[:], hn4[:, kt, t, :], wz[:, kt, :], kt == 0, kt == 7, ["hn", wzk], [pzk])
                            ACT(sg5, pg[:], AF.Sigmoid, [pgk], ["sg5"])
                            ACT(zs5, pz[:], AF.Sigmoid, [pzk], ["zs5"])
                            TT("dve", zs5, zs5, pz[:], ALU.mult, ["zs5", pzk], ["zs5"])
                            TT("dve", sg5, sg5, zs5, ALU.mult, ["sg5", "zs5"], ["sg5"])
                            TT("dve", u8[:, t, cs], u8[:, t, cs], sg5, ALU.mult, ["mixB", "sg5"], ["mixB"])
                    for kt in range(8):
                        pt_, pk = PS()
                        pv = pt_.bitcast(BF16)[:].rearrange("p (t j) -> p t j", t=8)
                        for t in range(8):
                            TR(pv[:, t, :], u8[:, t, kt * 128:(kt + 1) * 128], ["mixB"], [pk])
                        CP("act" if kt % 2 else "dve", mix[:, kt, :, :], pv, [pk], ["mixA"])
            if do_s5 and lvl < 9:
                MSET("pool", mixA, 0.0, ["mixA"])
            pd_, pdk = PS()
            pdt = pd_[:, 0:128].rearrange("p (c h) -> p c h", h=16)
            for c in range(8):
                for kt in range(8):
                    MM(pdt[:, c, :], hn[:, kt, c * 128:(c + 1) * 128], wdt[:, kt, :], kt == 0, kt == 7,
                       ["hn", "wdt"], [pdk])
            TT("dve", dtr[:], pdt, dtb[:].unsqueeze(1).to_broadcast([128, 8, 16]), ALU.add, [pdk, "dtb"], ["dtr"])
            ACT(dtr[:], dtr[:], AF.Exp, ["dtr"], ["dtr"])
            ACT(dtv[:], dtr[:], AF.Ln, ["dtr"], ["dtv"], bias=1.0)
            TT("dve", av[:], dtv[:], Abc[:].unsqueeze(1).to_broadcast([128, 8, 16]), ALU.mult, ["dtv", "Abc"], ["av"])
            CP("dve", ahi[:], av[:], ["av"], ["ahi"])
            TT("dve", alf[:], av[:], ahi[:], ALU.subtract, ["av", "ahi"], ["alf"])
            CP("dve", alo[:], alf[:], ["alf"], ["alo"])

            for hf in range(2):
                hsl = slice(hf * 512, (hf + 1) * 512)
                for ctg in range(4 if (full or (b == npre - 1 and hf == 1)) else 3):
                    wv, wk = load_w(win_b, 3072 + ctg * 512, 512, 0, 8, "win_b")
                    for c4 in range(4):
                        ct = ctg * 4 + c4
                        pp, ppk = PS()
                        for kt in range(8):
                            MM(pp[:], wv[:, kt, c4 * 128:(c4 + 1) * 128], hn[:, kt, hsl], kt == 0, kt == 7,
                               ["hn", wk], [ppk])
                        pr = pre[ct % 2]
                        prk = "pre%d" % (ct % 2)
                        dgv = dg[ct % 2]
                        dgk = "dg%d" % (ct % 2)
                        CP("act", pr[:, 3:515], pp[:], [ppk], [prk])
                        CP("pool", pr[:, 0:3], halo[:, ct, :], ["halo"], [prk])
                        for k in range(4):
                            TS("dve", dgv[:, k, :], ident_f[:], cw[:, ct, k:k + 1], None, ALU.mult, None,
                               ["ident_f", "cw"], [dgk])
                        pc, pck = PS()
                        for k in range(4):
                            MM(pc[:], dgv[:, k, :], pr[:, k:k + 512], k == 0, k == 3, [dgk, prk], [pck])
                        ACT(xc[:, ct, :], pc[:], AF.Silu, [pck, "cbv"], ["xc"], bias=cbv[:, ct:ct + 1])
                        CP("pool", halo[:, ct, :], pr[:, 512:515], [prk], ["halo"])
                for cb in range(2 if full else 0):
                    wv, wk = load_w(win_b, 2048 + cb * 512, 512, 0, 8, "win_b")
                    for c in range(4):
                        pz, pzk = PS()
                        for kt in range(8):
                            MM(pz[:], hn[:, kt, hf * 512 + c * 128: hf * 512 + (c + 1) * 128], wv[:, kt, :],
                               kt == 0, kt == 7, ["hn", wk], [pzk])
                        ACT(zsb[:, c, cb * 512:(cb + 1) * 512], pz[:], AF.Silu, [pzk], ["zsb"])
                for c in range(4):
                    cg = hf * 4 + c
                    tok = slice(c * 128, (c + 1) * 128)
                    a_c = av[:, cg, :]
                    dt_c = dtv[:, cg, :]
                    pa, pak = PS()
                    MM(pa[:, 0:16], tri_f[:], a_c, True, True, ["tri_f", "av"], [pak])
                    MM(pa[:, 16:32], ones_f[:], a_c, True, True, ["ones_f", "av"], [pak])
                    CP("dve", acs[:], pa[:, 0:16], [pak], ["acs"])
                    TT("dve", t16[:], pa[:, 16:32], acs[:], ALU.subtract, [pak, "acs"], ["t16"])
                    ACT(dec[:], t16[:], AF.Exp, ["t16"], ["dec"])
                    ACT(el[:], acs[:], AF.Exp, ["acs"], ["el"])
                    ACT(etot[:], pa[:, 16:32], AF.Exp, [pak], ["etot"])
                    px, pxk = PS()
                    pxb = px.bitcast(BF16)
                    for ct in range(8):
                        TR(pxb[:, ct * 128:(ct + 1) * 128], xc[:, ct, tok], ["xc"], [pxk])
                    pxv = pxb[:].rearrange("p (h q) -> p h q", h=16)
                    TT("dve", xd, pxv, dt_c.unsqueeze(2).to_broadcast([128, 16, 64]), ALU.mult, [pxk, "dtv"], ["xd"])
                    TT("dve", xD, pxv, Dhb[:].unsqueeze(2).to_broadcast([128, 16, 64]), ALU.mult, [pxk, "Dhb"], ["xD"])
                    TT("pool", xdd, xd, dec[:].unsqueeze(2).to_broadcast([128, 16, 64]), ALU.mult, ["xd", "dec"], ["xdd"])
                    pb_, pbk = PS()
                    pbv = pb_.bitcast(BF16)[:, 0:512].rearrange("p (g n) -> p g n", g=4)
                    for g in range(4):
                        TR(pbv[:, g, :], xc[:, 8 + g, tok], ["xc"], [pbk])
                    CP("act", Bc, pbv, [pbk], ["Bc"])
                    if full:
                        pg_, pgk = PS()
                        pgv = pg_[:].rearrange("p (g l) -> p g l", g=4)
                        for g in range(4):
                            MM(pgv[:, g, :], xc[:, 8 + g, tok], xc[:, 12 + g, tok], True, True, ["xc"], [pgk])
                        TT("dve", GTm, pgv, tri_f[:].unsqueeze(1).to_broadcast([128, 4, 128]), ALU.mult,
                           [pgk, "tri_f"], ["GTm"])
                        py = [PS(), PS()]

                        def emit_abc(g):
                            abc = abc_l[g % 2]
                            kab = "abc%d" % (g % 2)
                            CP("act", abc[:, 0, :, :], ahi[:, cg, 4 * g:4 * g + 4].unsqueeze(2).to_broadcast([128, 4, 128]),
                               ["ahi"], [kab])
                            CP("act", abc[:, 1, :, :], alo[:, cg, 4 * g:4 * g + 4].unsqueeze(2).to_broadcast([128, 4, 128]),
                               ["alo"], [kab])

                        emit_abc(0)
                        for g in range(4):
                            abc, dm, Ee, Mh = abc_l[g % 2], dm_l[g % 2], Ee_l[g % 2], Mh_l[g % 2]
                            kab, kdm, kEe, kMh = "abc%d" % (g % 2), "dm%d" % (g % 2), "Ee%d" % (g % 2), "Mh%d" % (g % 2)
                            pdd, pddk = PS()
                            pdv = pdd[:].rearrange("p (h l) -> p h l", h=4)
                            for hh in range(4):
                                MM(pdv[:, hh, :], abc[:, 0, hh, :], tri_b[:], True, False, [kab, "tri_b"], [pddk])
                                MM(pdv[:, hh, :], abc[:, 1, hh, :], tri_b[:], False, True, [kab, "tri_b"], [pddk])
                            if g < 3:
                                emit_abc(g + 1)
                            for hh in range(4):
                                h = 4 * g + hh
                                TS("dve", dm[:, hh, :], pdv[:, hh, :], acs[:, h:h + 1], 0.0, ALU.subtract, ALU.min,
                                   [pddk, "acs"], [kdm])
                            ACT(Ee, dm, AF.Exp, [kdm], [kEe])
                            TT("pool", Mh, Ee, GTm[:, g, :].unsqueeze(1).to_broadcast([128, 4, 128]), ALU.mult,
                               [kEe, "GTm"], [kMh])
                            for hh in range(4):
                                h = 4 * g + hh
                                bank, bk = py[h // 8]
                                col = (h % 8) * 64
                                MM(bank[:, col:col + 64], Mh[:, hh, :], xd[:, h, :], True, False, [kMh, "xd"], [bk])
                                MM(bank[:, col:col + 64], ident[:], xD[:, h, :], False, True, ["ident", "xD"], [bk])
                        po = [PS(), PS()]
                        for g in range(4):
                            bank, bk = po[g // 2]
                            col = (g % 2) * 256
                            MM(bank[:, col:col + 256], xc[:, 12 + g, tok], STb[:, g * 256:(g + 1) * 256], True, True,
                               ["xc", "STb"], [bk])
                        for hb in range(2):
                            ysl = yv[:, hb * 512:(hb + 1) * 512]
                            y3 = ysl.rearrange("p (h q) -> p h q", h=8)
                            TT("dve", y3, po[hb][0][:].rearrange("p (h q) -> p h q", h=8),
                               el[:, hb * 8:(hb + 1) * 8].unsqueeze(2).to_broadcast([128, 8, 64]), ALU.mult,
                               [po[hb][1], "el"], ["yv"])
                            TT("dve", ysl, ysl, py[hb][0][:], ALU.add, ["yv", py[hb][1]], ["yv"])
                            TT("pool", ysl, ysl, zsb[:, c, hb * 512:(hb + 1) * 512], ALU.mult, ["yv", "zsb"], ["yv"])
                        for grp in range(4):
                            ACT(sq[:, 0:256], yv[:, grp * 256:(grp + 1) * 256], AF.Square, ["yv"], ["sq", "ssg"],
                                accum=ssg[:, grp:grp + 1])
                        ACT(rg[:], ssg[:], AF.Ln, ["ssg"], ["rg"], scale=1.0 / 256, bias=EPS)
                        ACT(rg[:], rg[:], AF.Exp, ["rg"], ["rg"], scale=-0.5)
                        TT("dve", gn.rearrange("p (g q) -> p g q", g=4), yv.rearrange("p (g q) -> p g q", g=4),
                           rg[:].unsqueeze(2).to_broadcast([128, 4, 256]), ALU.mult, ["yv", "rg"], ["gn"])
                        ptt, ptk = PS()
                        ptv = ptt.bitcast(BF16)[:].rearrange("p (k l) -> p k l", k=8)
                        for kt in range(8):
                            TR(ptv[:, kt, :], gn[:, kt * 128:(kt + 1) * 128], ["gn"], [ptk])
                        for kt in range(8):
                            TS("dve", mix[:, 8 + kt, :, 16 * cg:16 * cg + 16],
                               ptv[:, kt, :].rearrange("p (j t) -> p t j", t=8), snw[:, kt:kt + 1], None,
                               ALU.mult, None, [ptk, "snw"], ["mixB"])
                    pst = [PS(), PS()]
                    for g in range(4):
                        bank, bk = pst[g // 2]
                        col = (g % 2) * 256
                        MM(bank[:, col:col + 256], Bc[:, g, :],
                           xdd[:, 4 * g:4 * g + 4, :].rearrange("p h q -> p (h q)"), True, True, ["Bc", "xdd"], [bk])
                    for hb in range(2):
                        s3 = STf[:, hb * 512:(hb + 1) * 512].rearrange("p (h q) -> p h q", h=8)
                        TT("dve", s3, s3, etot[:, hb * 8:(hb + 1) * 8].unsqueeze(2).to_broadcast([128, 8, 64]),
                           ALU.mult, ["STf", "etot"], ["STf"])
                        TT("dve", STf[:, hb * 512:(hb + 1) * 512], STf[:, hb * 512:(hb + 1) * 512], pst[hb][0][:],
                           ALU.add, ["STf", pst[hb][1]], ["STf"])
                    CP("act", STb[:], STf[:], ["STf"], ["STb"])

            if full:
                pre_w = [load_w(wout_b, 0, 512, 0, 8, "wout_b"), load_w(wout_b, 0, 512, 8, 8, "wout_b")]
                DMA("sp", xt[:], x_d[t0:t0 + BLK, :].rearrange("(j t) d -> j t d", t=8), [], ["xt"])
                for cb in range(2):
                    cs = slice(cb * 512, (cb + 1) * 512)
                    if cb == 0:
                        (wa, wak), (wbb, wbk) = pre_w
                    else:
                        wa, wak = load_w(wout_b, cb * 512, 512, 0, 8, "wout_b")
                        wbb, wbk = load_w(wout_b, cb * 512, 512, 8, 8, "wout_b")
                    for t in range(8):
                        po_, pok = PS()
                        for kt in range(16):
                            wv, wk = (wa, wak) if kt < 8 else (wbb, wbk)
                            MM(po_[:], mix[:, kt, t, :], wv[:, kt % 8, :], kt == 0, kt == 15,
                               ["mixA" if kt < 8 else "mixB", wk], [pok])
                        TT("dve", xt[:, t, cs], xt[:, t, cs], po_[:], ALU.add, ["xt", pok], ["xt"])

                DMA("act", pt, p_d[o0:o0 + BLK, :].rearrange("(j t) d -> j t d", t=8), [], ["pt"])
                rms_rr("xt")
                for t in range(8):
                    TS("dve", hs[:, t, :], xt[:, t, :], rr[:, t:t + 1], None, ALU.mult, None, ["xt", "rr"], ["mixA"])
                hr4 = hn[:].rearrange("p k (t j) -> p k t j", t=8)
                for kt in range(8):
                    pt_, pk = PS()
                    pv = pt_.bitcast(BF16)[:].rearrange("p (t j) -> p t j", t=8)
                    for t in range(8):
                        TR(pv[:, t, :], hs[:, t, kt * 128:(kt + 1) * 128], ["mixA"], [pk])
                    TS("dve", hr4[:, kt, :, :], pv, plw[:, kt:kt + 1], None, ALU.mult, None, [pk, "plw"], ["hn"])
                CP("pool", pbf, pt, ["pt"], ["pbf"])
                for k2 in range(2):
                    pt_, pk = PS()
                    pv = pt_.bitcast(BF16)[:].rearrange("p (t j) -> p t j", t=8)
                    for t in range(8):
                        TR(pv[:, t, :], pbf[:, t, k2 * 128:(k2 + 1) * 128], ["pbf"], [pk])
                    CP("act", p_fm[:, k2, :, :], pv, [pk], ["p_fm"])
                for cb in range(2):
                    cs = slice(cb * 512, (cb + 1) * 512)
                    wgv, wgk = load_w(wgate_b, cb * 512, 512, 0, 8, "wgate_b")
                    wpv, wpk = load_w(wple_b, cb * 512, 512, 0, 2, "wple_b")
                    for t in range(8):
                        pg, pgk = PS()
                        for kt in range(8):
                            MM(pg[:], hr4[:, kt, t, :], wgv[:, kt, :], kt == 0, kt == 7, ["hn", wgk], [pgk])
                        ACT(gsig, pg[:], AF.Sigmoid, [pgk], ["gsig"])
                        pp, ppk = PS()
                        for k2 in range(2):
                            MM(pp[:], p_fm[:, k2, t, :], wpv[:, k2, :], k2 == 0, k2 == 1, ["p_fm", wpk], [ppk])
                        TT("dve", tmp2, pp[:], gsig, ALU.mult, [ppk, "gsig"], ["tmp2"])
                        TT("pool", xt[:, t, cs], xt[:, t, cs], tmp2, ALU.add, ["xt", "tmp2"], ["xt"])
                rms_rr("xt")
                mixo = mix.bitcast(F32)[:].rearrange("p k t j -> p (k t j)").rearrange("p (t d) -> p t d", t=8)
                for t in range(8):
                    STT(mixo[:, t, :], xt[:, t, :], rr[:, t:t + 1], fwb[:], ALU.mult, ALU.mult, ["xt", "rr", "fwb"],
                        ["mixA", "mixB"])
                tok_ = DMA("sp", out_d[o0:o0 + BLK, :].rearrange("(j t) d -> j t d", t=8), mixo, ["mixA", "mixB"], [])
                out_toks.append(tok_)
        em.final_wait("sp", out_toks)
        em.emit(nc, sems)
    return nc


_NC_CACHE = {}


def kernel(_cfg=None, **inputs):
    f = lambda a: np.ascontiguousarray(a, dtype=np.float32)
    key = repr(sorted((_cfg or {}).items()))
    if key not in _NC_CACHE:
        _NC_CACHE[key] = build_program(_cfg)
    nc = _NC_CACHE[key]
    tri = np.triu(np.ones((128, 128), dtype=np.float32))
    shared = {
        "c_ident": np.eye(128, dtype=np.float32),
        "c_tri": tri,
        "c_ones": np.ones((128, 128), dtype=np.float32),
        "final_norm_w": f(inputs["final_norm_w"]).reshape(1, D),
        "ple_norm_w": f(inputs["ple_norm_w"]).reshape(D),
        "norm_w": f(inputs["norm_w"]).reshape(D),
        "ssd_norm_w": f(inputs["ssd_norm_w"]).reshape(D),
        "w_in": f(inputs["w_in"]).reshape(D, DIN),
        "w_out": f(inputs["w_out"]).reshape(2 * D, D),
        "w_ple_gate": f(inputs["w_ple_gate"]).reshape(D, D),
        "w_ple_proj": f(inputs["w_ple_proj"]).reshape(PLE, D),
        "conv_w": f(inputs["conv_w"]).reshape(4, 2048),
        "conv_b": f(inputs["conv_b"]).reshape(2048),
        "dt_bias": f(inputs["dt_bias"]).reshape(1, 16),
        "A_log": f(inputs["A_log"]).reshape(1, 16),
        "ssd_D": f(inputs["ssd_D"]).reshape(1, 16),
    }
    if (_cfg or {}).get("s5", True):
        m8 = np.zeros((128, 128), dtype=np.float32)
        for s_lo in range(8):
            for t_lo in range(s_lo, 8):
                m8[s_lo * 16:(s_lo + 1) * 16, t_lo * 16:(t_lo + 1) * 16] = 1.0
        rowm = np.zeros((128, 2), dtype=np.float32)
        rowm[:64, 0] = 1.0
        rowm[64:, 1] = 1.0
        shared.update({
            "c_mask8": m8,
            "c_rowmask": rowm,
            "s5_A_re": f(inputs["s5_A_re"]).reshape(64, 64),
            "s5_A_im": f(inputs["s5_A_im"]).reshape(64, 64),
            "s5_log_dt": f(inputs["s5_log_dt"]).reshape(64),
            "s5_B_re": f(inputs["s5_B_re"]).reshape(64, 64, 16),
            "s5_B_im": f(inputs["s5_B_im"]).reshape(64, 64, 16),
            "s5_C_re": f(inputs["s5_C_re"]).reshape(64, 16, 64),
            "s5_C_im": f(inputs["s5_C_im"]).reshape(64, 16, 64),
            "s5_D": f(inputs["s5_D"]).reshape(D),
            "s5_w_glu": f(inputs["s5_w_glu"]).reshape(D, D),
            "s5_b_glu": f(inputs["s5_b_glu"]).reshape(1, D),
        })
    x = f(inputs["x"])
    p = f(inputs["p"])
    cfg_ = _cfg or {}
    nblk_ = cfg_.get("nblk", NBLK)
    npre_ = cfg_.get("npre", 4)
    nown_ = nblk_ - npre_
    HALF = L // 2
    in_maps = []
    for c in range(8):
        b, half = divmod(c, 2)
        if npre_ == 4 and nblk_ == 8:
            xin = np.concatenate([x[b, 0:HALF], x[b, half * HALF:(half + 1) * HALF]], axis=0)
            pin = p[0, b, half * HALF:(half + 1) * HALF]
            fl = float(half)
        else:
            xin = x[b]
            pin = p[0, b, npre_ * BLK:nblk_ * BLK]
            fl = 1.0
        m = {"x": np.ascontiguousarray(xin), "p": np.ascontiguousarray(pin),
             "flag": np.full((128, 1), fl, dtype=np.float32)}
        m.update(shared)
        in_maps.append(m)
    res = run_bass_kernel_spmd(nc, in_maps, core_ids=list(range(8)))
    if npre_ == 4 and nblk_ == 8:
        out = np.empty((NBATCH, L, D), dtype=np.float32)
        for c in range(8):
            b, half = divmod(c, 2)
            out[b, half * HALF:(half + 1) * HALF] = res.results[c]["out"]
        return out
    out = np.zeros((NBATCH, L, D), dtype=np.float32)
    for b in range(NBATCH):
        out[b, npre_ * BLK:nblk_ * BLK] = res.results[2 * b]["out"]
    return out
```
